# Optimizing a Trainium2 kernel written in Bass

```python
import jax, jax.numpy as jnp
from jax import lax
import numpy as np

D_MODEL = 1024
BATCH = 32
SEQ = 2048
DEPTH = 2
DEC_BATCH = 16
DEC_SEQ = 4096
PAST_LEN = 128

N_META = 16
N_MIXERS = 2
N_RWKV = (DEPTH + 1) // 2
N_FNET = DEPTH // 2
HEAD_SIZE = 64
N_HEADS = D_MODEL // HEAD_SIZE
DECAY_LORA = 64
AAA_LORA = 64
GATE_LORA = 160
N_MU = 6
FNET_GROUPS = 8
FNET_GROUP_DIM = D_MODEL // FNET_GROUPS
D_FF = 2816
CONV_WIDTH = 3
RMS_EPS = 1e-6
GN_EPS = 64e-5

kernel_name = 'bidir_rwkv7_fnet_convglu_trunk'


def rmsnorm(x, g):
    xf = x.astype(jnp.float32)
    y = xf * lax.rsqrt(jnp.mean(xf * xf, axis=-1, keepdims=True) + RMS_EPS)
    return (y * g.astype(jnp.float32)).astype(x.dtype)


def split_heads(z):
    return z.reshape(z.shape[:-1] + (N_HEADS, HEAD_SIZE))


def _wkv_step(S, inp):
    r, decay, k, v, avec, b = inp
    sa = jnp.einsum('bhij,bhj->bhi', S, avec)
    S = S * decay[..., None, :] + sa[..., :, None] * b[..., None, :] + v[..., :, None] * k[..., None, :]
    out = jnp.einsum('bhij,bhj->bhi', S, r)
    return S, out


def wkv_scan(r, decay, k, v, avec, b, reverse):
    B, T, H, N = r.shape
    xs = tuple(jnp.moveaxis(z, 1, 0) for z in (r, decay, k, v, avec, b))
    S0 = jnp.zeros((B, H, N, N), jnp.float32)
    _, out = lax.scan(_wkv_step, S0, xs, reverse=reverse)
    return jnp.moveaxis(out, 0, 1)


def rwkv7_time_mix(h, mu, w_rkv, w0, w1, w2, a0, a1, a2, g1, g2, k_k, k_a, r_k, gn_w, gn_b, w_o):
    B, T, D = h.shape
    f32 = jnp.float32
    hp = jnp.pad(h, ((0, 0), (1, 1), (0, 0)))
    xx = 0.5 * (hp[:, :-2] + hp[:, 2:]) - h
    xr, xw, xk, xv, xa, xg = [h + xx * mu[i] for i in range(N_MU)]
    r = xr @ w_rkv[0]
    k = xk @ w_rkv[1]
    v = xv @ w_rkv[2]
    g = jax.nn.sigmoid(xg @ g1) @ g2
    w_lin = w0[:, None, None, :] + jnp.einsum('zbtr,zrd->zbtd', jnp.tanh(jnp.einsum('btd,zdr->zbtr', xw, w1)), w2)
    w_log = -jax.nn.softplus(-w_lin.astype(f32)) - 0.5
    decay = jnp.exp(-jnp.exp(w_log))
    a = jax.nn.sigmoid((a0[:, None, None, :] + jnp.einsum('zbtr,zrd->zbtd', jnp.einsum('btd,zdr->zbtr', xa, a1), a2)).astype(f32))
    rf = split_heads(r.astype(f32))
    vf = split_heads(v.astype(f32))
    kf = k.astype(f32)
    kk = split_heads(kf * k_k.astype(f32))
    kk = kk / jnp.maximum(jnp.sqrt(jnp.sum(kk * kk, axis=-1, keepdims=True)), 1e-12)
    k_dir = split_heads(kf[None] * (1.0 + (a - 1.0) * k_a.astype(f32)))
    b_dir = kk[None] * split_heads(a)
    decay_h = split_heads(decay)
    o_fwd = wkv_scan(rf, decay_h[0], k_dir[0], vf, -kk, b_dir[0], reverse=False)
    o_bwd = wkv_scan(rf, decay_h[1], k_dir[1], vf, -kk, b_dir[1], reverse=True)
    o = o_fwd + o_bwd
    mean = jnp.mean(o, axis=-1, keepdims=True)
    var = jnp.mean(jnp.square(o - mean), axis=-1, keepdims=True)
    o = ((o - mean) * lax.rsqrt(var + GN_EPS)).reshape(B, T, D) * gn_w.astype(f32) + gn_b.astype(f32)
    bonus = jnp.sum(jnp.sum(rf[None] * k_dir * r_k.astype(f32), axis=-1, keepdims=True), axis=0)
    o = o + (bonus * vf).reshape(B, T, D)
    return (o.astype(h.dtype) * g) @ w_o


def fourier_mix(h, w_f):
    B, T, D = h.shape
    hg = h.astype(jnp.float32).reshape(B, T, FNET_GROUPS, FNET_GROUP_DIM)
    f = jnp.fft.fftn(hg, axes=(1, 3), norm='ortho').real
    return f.reshape(B, T, D).astype(h.dtype) @ w_f


def conv_glu_ffn(h, w_in, conv_w, conv_b, w_out):
    u = h @ w_in
    act_in, lin = u[..., :D_FF], u[..., D_FF:]
    p = jnp.pad(act_in, ((0, 0), (1, 1), (0, 0)))
    c = p[:, :-2] * conv_w[0] + p[:, 1:-1] * conv_w[1] + p[:, 2:] * conv_w[2] + conv_b
    return (jax.nn.silu(c) * lin) @ w_out


def trunk(x, meta_tokens, norm_mix, norm_ffn, norm_final,
          rwkv_mu, rwkv_w_rkv, rwkv_w0, rwkv_w1, rwkv_w2, rwkv_a0, rwkv_a1, rwkv_a2,
          rwkv_g1, rwkv_g2, rwkv_k_k, rwkv_k_a, rwkv_r_k, rwkv_gn_w, rwkv_gn_b, rwkv_w_o,
          fnet_w_o, ffn_w_in, ffn_conv_w, ffn_conv_b, ffn_w_out):
    B = x.shape[0]
    meta = jnp.broadcast_to(meta_tokens.astype(x.dtype)[None], (B, N_META, D_MODEL))
    h = jnp.concatenate([meta, x], axis=1)
    for i in range(DEPTH):
        hn = rmsnorm(h, norm_mix[i])
        j = i // N_MIXERS
        if i % N_MIXERS == 0:
            h = h + rwkv7_time_mix(hn, rwkv_mu[j], rwkv_w_rkv[j], rwkv_w0[j], rwkv_w1[j], rwkv_w2[j],
                                   rwkv_a0[j], rwkv_a1[j], rwkv_a2[j], rwkv_g1[j], rwkv_g2[j],
                                   rwkv_k_k[j], rwkv_k_a[j], rwkv_r_k[j], rwkv_gn_w[j], rwkv_gn_b[j], rwkv_w_o[j])
        else:
            h = h + fourier_mix(hn, fnet_w_o[j])
        h = h + conv_glu_ffn(rmsnorm(h, norm_ffn[i]), ffn_w_in[i], ffn_conv_w[i], ffn_conv_b[i], ffn_w_out[i])
    return rmsnorm(h, norm_final)[:, N_META:]


def setup_inputs(seed: int = 0) -> dict:
    key = jax.random.key(seed)
    ks = jax.random.split(key, 32)
    nrm = jax.random.normal
    D = D_MODEL
    return {
        'x_prompt': nrm(ks[0], (BATCH, SEQ, D), jnp.float32),
        'x_sample': nrm(ks[1], (DEC_BATCH, DEC_SEQ, D), jnp.float32),
        'meta_tokens': nrm(ks[2], (N_META, D), jnp.float32),
        'norm_mix': 1.0 + 0.02 * nrm(ks[3], (DEPTH, D), jnp.float32),
        'norm_ffn': 1.0 + 0.02 * nrm(ks[4], (DEPTH, D), jnp.float32),
        'norm_final': 1.0 + 0.02 * nrm(ks[5], (D,), jnp.float32),
        'rwkv_mu': jax.random.uniform(ks[6], (N_RWKV, N_MU, D), jnp.float32),
        'rwkv_w_rkv': nrm(ks[7], (N_RWKV, 3, D, D), jnp.float32) * D ** -0.5,
        'rwkv_w0': jax.random.uniform(ks[8], (N_RWKV, 2, D), jnp.float32, -5.0, 1.0),
        'rwkv_w1': nrm(ks[9], (N_RWKV, 2, D, DECAY_LORA), jnp.float32) * D ** -0.5,
        'rwkv_w2': nrm(ks[10], (N_RWKV, 2, DECAY_LORA, D), jnp.float32) * 0.1 * DECAY_LORA ** -0.5,
        'rwkv_a0': 0.5 * nrm(ks[11], (N_RWKV, 2, D), jnp.float32),
        'rwkv_a1': nrm(ks[12], (N_RWKV, 2, D, AAA_LORA), jnp.float32) * D ** -0.5,
        'rwkv_a2': nrm(ks[13], (N_RWKV, 2, AAA_LORA, D), jnp.float32) * 0.1 * AAA_LORA ** -0.5,
        'rwkv_g1': nrm(ks[14], (N_RWKV, D, GATE_LORA), jnp.float32) * D ** -0.5,
        'rwkv_g2': nrm(ks[15], (N_RWKV, GATE_LORA, D), jnp.float32) * GATE_LORA ** -0.5,
        'rwkv_k_k': 0.85 + 0.02 * nrm(ks[16], (N_RWKV, D), jnp.float32),
        'rwkv_k_a': 1.0 + 0.02 * nrm(ks[17], (N_RWKV, D), jnp.float32),
        'rwkv_r_k': 0.1 * nrm(ks[18], (N_RWKV, N_HEADS, HEAD_SIZE), jnp.float32),
        'rwkv_gn_w': 1.0 + 0.02 * nrm(ks[19], (N_RWKV, D), jnp.float32),
        'rwkv_gn_b': 0.02 * nrm(ks[20], (N_RWKV, D), jnp.float32),
        'rwkv_w_o': nrm(ks[21], (N_RWKV, D, D), jnp.float32) * D ** -0.5,
        'fnet_w_o': nrm(ks[22], (N_FNET, D, D), jnp.float32) * D ** -0.5,
        'ffn_w_in': nrm(ks[23], (DEPTH, D, 2 * D_FF), jnp.float32) * D ** -0.5,
        'ffn_conv_w': nrm(ks[24], (DEPTH, CONV_WIDTH, D_FF), jnp.float32) * CONV_WIDTH ** -0.5,
        'ffn_conv_b': 0.02 * nrm(ks[25], (DEPTH, D_FF), jnp.float32),
        'ffn_w_out': nrm(ks[26], (DEPTH, D_FF, D), jnp.float32) * D_FF ** -0.5,
    }


def reference(x_prompt, x_sample, meta_tokens, norm_mix, norm_ffn, norm_final,
              rwkv_mu, rwkv_w_rkv, rwkv_w0, rwkv_w1, rwkv_w2, rwkv_a0, rwkv_a1, rwkv_a2,
              rwkv_g1, rwkv_g2, rwkv_k_k, rwkv_k_a, rwkv_r_k, rwkv_gn_w, rwkv_gn_b, rwkv_w_o,
              fnet_w_o, ffn_w_in, ffn_conv_w, ffn_conv_b, ffn_w_out):
    y_prompt = trunk(x_prompt, meta_tokens, norm_mix, norm_ffn, norm_final,
                     rwkv_mu, rwkv_w_rkv, rwkv_w0, rwkv_w1, rwkv_w2, rwkv_a0, rwkv_a1, rwkv_a2,
                     rwkv_g1, rwkv_g2, rwkv_k_k, rwkv_k_a, rwkv_r_k, rwkv_gn_w, rwkv_gn_b, rwkv_w_o,
                     fnet_w_o, ffn_w_in, ffn_conv_w, ffn_conv_b, ffn_w_out)
    y_sample = trunk(x_sample, meta_tokens, norm_mix, norm_ffn, norm_final,
                     rwkv_mu, rwkv_w_rkv, rwkv_w0, rwkv_w1, rwkv_w2, rwkv_a0, rwkv_a1, rwkv_a2,
                     rwkv_g1, rwkv_g2, rwkv_k_k, rwkv_k_a, rwkv_r_k, rwkv_gn_w, rwkv_gn_b, rwkv_w_o,
                     fnet_w_o, ffn_w_in, ffn_conv_w, ffn_conv_b, ffn_w_out)
    return (y_prompt, y_sample)
```

```python
import contextlib
import math
import numpy as np
import ml_dtypes
import concourse.bass as bass
import concourse.mybir as mybir
from concourse.bass_utils import run_bass_kernel_spmd

F32 = mybir.dt.float32
BF16 = mybir.dt.bfloat16
AF = mybir.ActivationFunctionType
ALU = mybir.AluOpType
AX = mybir.AxisListType

D = 1024
NH = 16
HS = 64
DFF = 2816
NMETA = 16
CH = 64
LWC = math.exp(-0.5)
NDS = 40


class Buf:
    __slots__ = ("t", "w", "r")

    def __init__(self, t):
        self.t = t
        self.w = None
        self.r = []

    def __getitem__(self, k):
        return self.t[k]


class KB:
    def __init__(self, nc, stack):
        self.nc = nc
        self.names = ["pe", "act", "dve", "pool", "sp"]
        self.q = {e: [] for e in self.names}
        self.count = {e: 0 for e in self.names}
        self.mile = {e: [] for e in self.names}
        self.mileset = {e: set() for e in self.names}
        self.semval = {e: 0 for e in self.names}
        self.milemap = {e: {} for e in self.names}
        self.waited = {e: {} for e in self.names}
        self.flushed = {e: 0 for e in self.names}
        self.milekeys = {e: [] for e in self.names}
        self.esem = {e: stack.enter_context(nc.semaphore("s_" + e)) for e in self.names}
        self.dsem = [stack.enter_context(nc.semaphore("d%d" % i)) for i in range(NDS)]
        self.dval = [0] * NDS
        self.dnext = 0
        self.rr = 0

    def _wait(self, eng, deps):
        wd = self.waited[eng]
        for dep in deps:
            if dep[0] == "eng":
                _, f, idx = dep
                if f == eng and eng == "pe":
                    continue
                if idx <= self.flushed[f] and idx not in self.milemap[f]:
                    idx = min(k for k in self.milekeys[f] if k >= idx)
                if wd.get(("eng", f), 0) >= idx:
                    continue
                wd[("eng", f)] = idx
                if idx not in self.mileset[f] and idx not in self.milemap[f]:
                    self.mileset[f].add(idx)
                self.q[eng].append(("weng", f, idx))
            else:
                _, si, val = dep
                if wd.get(("dma", si), 0) >= val:
                    continue
                wd[("dma", si)] = val
                self.q[eng].append(("wdma", si, val))

    def _deps(self, reads, writes):
        deps = []
        for b in reads:
            if b is not None and b.w is not None:
                deps.append(b.w)
        for b in writes:
            if b is not None:
                if b.w is not None:
                    deps.append(b.w)
                deps.extend(b.r)
        return deps

    def _mark(self, tok, reads, writes):
        for b in reads:
            if b is None:
                continue
            if tok[0] == "eng":
                b.r = [t for t in b.r if not (t[0] == "eng" and t[1] == tok[1])]
            b.r.append(tok)
        for b in writes:
            if b is None:
                continue
            b.w = tok
            b.r = []

    def op(self, eng, fn, reads=(), writes=()):
        self._wait(eng, self._deps(reads, writes))
        self.count[eng] += 1
        idx = self.count[eng]
        tok = ("eng", eng, idx)
        self.q[eng].append(("op", fn, idx))
        self._mark(tok, reads, writes)
        return tok

    def dma(self, eng, out, in_, reads=(), writes=()):
        si = self.dnext
        self.dnext = (self.dnext + 1) % NDS
        deps = self._deps(reads, writes)
        if self.dval[si] > 0:
            deps.append(("dma", si, self.dval[si]))
        self._wait(eng, deps)
        self.dval[si] += 16
        tok = ("dma", si, self.dval[si])
        self.q[eng].append(("dma", out, in_, si))
        self._mark(tok, reads, writes)
        return tok

    def dma_rr(self, out, in_, reads=(), writes=(), cast=False):
        if cast:
            return self.dma("pool", out, in_, reads, writes)
        eng = ("sp", "act")[self.rr % 2]
        self.rr += 1
        return self.dma("sp", out, in_, reads, writes)

    def barrier(self):
        toks = [("eng", e, self.count[e]) for e in self.names if self.count[e] > 0]
        toks += [("dma", si, self.dval[si]) for si in range(NDS) if self.dval[si] > 0]
        for e in self.names:
            self._wait(e, toks)

    def flush(self):
        nc = self.nc
        for e in self.names:
            v = self.semval[e]
            if self.count[e] > self.flushed[e]:
                self.mileset[e].add(self.count[e])
            self.milekeys[e] = self.milekeys[e][-1:] + sorted(self.mileset[e])
            self.flushed[e] = self.count[e]
            for idx in sorted(self.mileset[e]):
                v += 1
                self.milemap[e][idx] = v
            self.semval[e] = v
        q = self.q
        kb = self

        def replay(name, eng):
            for ent in q[name]:
                k = ent[0]
                if k == "weng":
                    eng.wait_ge(kb.esem[ent[1]], kb.milemap[ent[1]][ent[2]])
                elif k == "wdma":
                    eng.wait_ge(kb.dsem[ent[1]], ent[2])
                elif k == "op":
                    ins = ent[1](eng)
                    if ent[2] in kb.mileset[name]:
                        ins.then_inc(kb.esem[name], 1)
                else:
                    eng.dma_start(out=ent[1], in_=ent[2]).then_inc(kb.dsem[ent[3]], 16)

        with nc.allow_non_contiguous_dma(reason="small per-feature vectors"), nc.Block() as block:
            @block.tensor
            def _(eng):
                replay("pe", eng)

            @block.scalar
            def _(eng):
                replay("act", eng)

            @block.vector
            def _(eng):
                replay("dve", eng)

            @block.gpsimd
            def _(eng):
                replay("pool", eng)

            @block.sync
            def _(eng):
                replay("sp", eng)

        self.q = {e: [] for e in self.names}
        self.mileset = {e: set() for e in self.names}


def blocks_of(T, maxn=510):
    nb = -(-T // maxn)
    base, rem = divmod(T, nb)
    out = []
    t = 0
    for i in range(nb):
        n = base + (1 if i < rem else 0)
        out.append((t, n))
        t += n
    return out


def tiles_of(n, p=128):
    return [(i, min(p, n - i)) for i in range(0, n, p)]


class Prog:
    def __init__(self, seqs_p, T_p, seqs_s, T_s, debug=False, phases=("p1", "scan", "p3", "p4")):
        self.np_, self.Tp, self.ns_, self.Ts = seqs_p, T_p, seqs_s, T_s
        self.debug = debug
        self.phases = phases
        self._dbg = set()
        self.seqs = [("p", i, T_p) for i in range(seqs_p)] + [("s", i, T_s) for i in range(seqs_s)]

    def build(self):
        nc = bass.Bass("TRN2", target_bir_lowering=False)
        self.nc = nc
        self.stack = contextlib.ExitStack()
        st = self.stack
        kb = KB(nc, st)
        self.kb = kb
        I = {}

        def din(name, shape, dt=F32):
            I[name] = nc.dram_tensor(name, list(shape), dt, kind="ExternalInput").ap()
            return I[name]

        self.I = I
        din("x_p", [self.np_, self.Tp - NMETA, D])
        din("x_s", [self.ns_, self.Ts - NMETA, D])
        din("meta", [NMETA, D])
        din("norm_mix", [2, D]); din("norm_ffn", [2, D]); din("norm_final", [1, D])
        din("mu", [6, D]); din("w_rkv", [3, D, D])
        din("w0", [2, D]); din("w1", [2, D, 64]); din("w2", [2, 64, D])
        din("a0", [2, D]); din("a1", [2, D, 64]); din("a2", [2, 64, D])
        din("g1", [D, 160]); din("g2", [160, D])
        din("k_k", [1, D]); din("k_a", [1, D]); din("r_k", [1, D])
        din("gn_w", [1, D]); din("gn_b", [1, D]); din("w_o", [D, D])
        din("w_f", [D, D]); din("ffn_w_in", [2, D, 2 * DFF]); din("conv_w", [2, 3, DFF])
        din("conv_b", [2, DFF]); din("ffn_w_out", [2, DFF, D])
        din("c_ident", [128, 128]); din("c_blk", [128, 128]); din("c_hsel", [128, 2])
        din("c_mask", [2, 128, 512]); din("c_maskT", [2, 64, 1024]); din("c_cm", [128, 512]); din("c_eye16", [64, 1024]); din("c_eye2", [128, 128])
        din("c_dftc", [128, 256])
        if "p4" in self.phases:
            din("c_dft_p", [2, self.Tp, self.Tp], BF16)
            din("c_dft_s", [2, self.Ts, self.Ts], BF16)
        okind = "ExternalOutput"
        self.y_p = nc.dram_tensor("y_p", [self.np_, self.Tp - NMETA, D], F32, kind=okind).ap()
        self.y_s = nc.dram_tensor("y_s", [self.ns_, self.Ts - NMETA, D], F32, kind=okind).ap()
        skind = "ExternalOutput" if self.debug else "Internal"
        self.S = []
        for si, (kind, bi, T) in enumerate(self.seqs):
            d = {}
            d["fm"] = nc.dram_tensor("fm%d" % si, [8, D, T], F32, kind=skind).ap()
            d["v"] = nc.dram_tensor("sv%d" % si, [T, D], BF16, kind=skind).ap()
            d["g"] = nc.dram_tensor("sg%d" % si, [T, D], BF16, kind=skind).ap()
            d["bonus"] = nc.dram_tensor("bonus%d" % si, [T, NH], F32, kind=skind).ap()
            d["o"] = nc.dram_tensor("so%d" % si, [2, T, D], F32, kind=skind).ap()
            d["h2"] = nc.dram_tensor("h2_%d" % si, [T, D], F32, kind=skind).ap()
            d["h1"] = nc.dram_tensor("h1_%d" % si, [T, D], F32, kind=skind).ap()
            d["h3"] = nc.dram_tensor("h3_%d" % si, [T, D], F32, kind=skind).ap()
            d["yc"] = nc.dram_tensor("yc%d" % si, [2, T, D], BF16, kind=skind).ap()
            self.S.append(d)

        self.consts()
        if "p1" in self.phases:
            self.phase1()
        if "scan" in self.phases:
            self.phase_scan()
        if "p3" in self.phases:
            self.phase3()
        if "p4" in self.phases:
            self.phase4()
        kb.barrier()
        kb.flush()
        st.close()
        return nc

    def sb(self, stack, name, shape, dt):
        return Buf(stack.enter_context(self.nc.sbuf_tensor("sb_" + name, list(shape), dt)))

    def ps(self, stack, name, shape, dt):
        return Buf(stack.enter_context(self.nc.psum_tensor("ps_" + name, list(shape), dt)))

    def xrows(self, si, t0, n):
        kind, bi, T = self.seqs[si]
        x = self.I["x_p"] if kind == "p" else self.I["x_s"]
        out = []
        if t0 < NMETA:
            m = min(NMETA, t0 + n) - t0
            out.append((self.I["meta"][t0:t0 + m, :], 0, m))
            if n > m:
                out.append((x[bi, 0:n - m, :], m, n - m))
        else:
            out.append((x[bi, t0 - NMETA:t0 - NMETA + n, :], 0, n))
        return out

    def consts(self):
        kb, nc, st = self.kb, self.nc, self.stack
        I = self.I
        C = {}
        self.C = C

        def ld(name, shape, dt, src, cast=False):
            b = self.sb(st, "k_" + name, shape, dt)
            kb.dma_rr(b[:], src, writes=[b], cast=cast)
            C[name] = b
            return b

        ld("ident", [128, 128], BF16, I["c_ident"], cast=True)
        ld("blk", [128, 128], BF16, I["c_blk"], cast=True)
        ld("hsel", [128, 2], BF16, I["c_hsel"], cast=True)
        ld("eye2", [128, 128], BF16, I["c_eye2"], cast=True)
        def colvec(name, src, n):
            b = self.sb(st, "k_" + name, [128, n * 8], F32)
            with nc.allow_non_contiguous_dma(reason="tiny per-feature vectors"):
                for i in range(n):
                    kb.dma("sp", b[:, i * 8:(i + 1) * 8], src[i].rearrange("(c p) -> p c", p=128), writes=[b])
            C[name] = b
        colvec("muT", I["mu"], 6)
        colvec("w0T", I["w0"], 2)
        colvec("a0T", I["a0"], 2)
        colvec("kkT", I["k_k"], 1)
        colvec("kaT", I["k_a"], 1)
        colvec("rkT", I["r_k"], 1)
        b = self.sb(st, "k_omka", [128, 8], F32)
        kb.op("dve", lambda e, b=b: e.tensor_scalar(out=b[:], in0=C["kaT"][:], scalar1=-1.0, scalar2=1.0,
                                                    op0=ALU.mult, op1=ALU.add), reads=[C["kaT"]], writes=[b])
        C["omka"] = b
        self.psb = [self.ps(st, "psb%d" % i, [128, 512], F32) for i in range(8)]
        self.psi = 0

    def dbg(self, name, buf, ap, shape, dt):
        if not self.debug or name in self._dbg:
            return
        self._dbg.add(name)
        d = self.nc.dram_tensor("dbg_" + name, list(shape), dt, kind="ExternalOutput").ap()
        self.kb.dma("sp", d, ap, reads=[buf])

    def rowvec(self, st, name, src):
        b = self.sb(st, "rv_" + name, [128, D], F32)
        self.kb.dma("sp", b[:], src.partition_broadcast(128), writes=[b])
        self.C[name] = b
        return b

    def psum(self):
        b = self.psb[self.psi % 8]
        self.psi += 1
        return b

    def norm_transpose(self, ws, xt, n, grow, hnT, col0, junk, ssb, hnb):
        kb = self.kb
        C = self.C
        kb.op("dve", lambda e: e.scalar_tensor_tensor(out=junk[:n, :], in0=xt[:n, :], scalar=1.0, in1=xt[:n, :],
                                                      op0=ALU.mult, op1=ALU.mult, accum_out=ssb[:n, 0:1]),
              reads=[xt], writes=[junk, ssb])
        kb.op("dve", lambda e: e.tensor_scalar(out=ssb[:n, 1:2], in0=ssb[:n, 0:1], scalar1=1.0 / D, scalar2=1e-6,
                                               op0=ALU.mult, op1=ALU.add), reads=[ssb], writes=[ssb])
        kb.op("act", lambda e: e.activation(out=ssb[:n, 2:3], in_=ssb[:n, 1:2], func=AF.Sqrt), reads=[ssb], writes=[ssb])
        kb.op("dve", lambda e: e.reciprocal(out=ssb[:n, 3:4], in_=ssb[:n, 2:3]), reads=[ssb], writes=[ssb])
        kb.op("dve", lambda e: e.scalar_tensor_tensor(out=hnb[:n, :], in0=xt[:n, :], scalar=ssb[:n, 3:4], in1=grow[:n, :],
                                                      op0=ALU.mult, op1=ALU.mult), reads=[xt, ssb, grow], writes=[hnb])
        self.transpose_into(hnb, n, hnT, col0)

    def transpose_into(self, src, n, dstT, col0, nchunk=8, eng="act"):
        kb = self.kb
        C = self.C
        pt = self.psum()
        ptb = pt.t[:].bitcast(BF16)
        for c in range(nchunk):
            kb.op("pe", lambda e, c=c: e.transpose(out=ptb[:, c * 128:c * 128 + n], in_=src[:n, c * 128:(c + 1) * 128],
                                                   identity=C["ident"][:n, :n]), reads=[src, C["ident"]], writes=[pt])
        src3 = ptb.rearrange("p (c t) -> p c t", t=128)
        if eng == "act":
            kb.op("act", lambda e: e.copy(out=dstT[:, 0:nchunk, col0:col0 + n], in_=src3[:, 0:nchunk, 0:n]),
                  reads=[pt], writes=[dstT])
        else:
            kb.op("dve", lambda e: e.tensor_copy(out=dstT[:, 0:nchunk, col0:col0 + n], in_=src3[:, 0:nchunk, 0:n]),
                  reads=[pt], writes=[dstT])

    def phase1(self):
        kb, nc, I, C = self.kb, self.nc, self.I, self.C
        with contextlib.ExitStack() as st:
            sb = lambda name, shape, dt: self.sb(st, name, shape, dt)
            self.rowvec(st, "nm0", I["norm_mix"][0:1, :])
            wr = [sb("wrkv%d" % i, [128, 8, D], BF16) for i in range(3)]
            for i in range(3):
                for c in range(8):
                    kb.dma("pool", wr[i][:, c, :], I["w_rkv"][i, c * 128:(c + 1) * 128, :], writes=[wr[i]])
            w1c = sb("w1c", [128, 8, 128], BF16)
            a1c = sb("a1c", [128, 8, 128], BF16)
            for z in range(2):
                kb.dma("pool", w1c[:, :, z * 64:(z + 1) * 64], I["w1"][z].rearrange("(c p) r -> p c r", p=128), writes=[w1c])
                kb.dma("pool", a1c[:, :, z * 64:(z + 1) * 64], I["a1"][z].rearrange("(c p) r -> p c r", p=128), writes=[a1c])
            w2c = sb("w2c", [128, D], BF16)
            a2c = sb("a2c", [128, D], BF16)
            for z in range(2):
                kb.dma("pool", w2c[z * 64:(z + 1) * 64, :], I["w2"][z], writes=[w2c])
                kb.dma("pool", a2c[z * 64:(z + 1) * 64, :], I["a2"][z], writes=[a2c])
            g1c = sb("g1c", [128, 8, 160], BF16)
            kb.dma("pool", g1c[:], I["g1"].rearrange("(c p) r -> p c r", p=128), writes=[g1c])
            g2a = sb("g2a", [128, D], BF16)
            g2b = sb("g2b", [32, D], BF16)
            kb.dma("pool", g2a[:], I["g2"][0:128, :], writes=[g2a])
            kb.dma("pool", g2b[:], I["g2"][128:160, :], writes=[g2b])

            xt = [sb("xt%d" % i, [128, D], F32) for i in range(2)]
            junk = sb("junk", [128, D], BF16)
            ssb = sb("ssb", [128, 4], F32)
            hnb = sb("hnb", [128, D], BF16)
            hnT = sb("hnT", [128, 8, 512], BF16)
            tmp = sb("tmp", [128, 8, 512], BF16)
            xx = tmp
            xm = [sb("xm%d" % i, [128, 8, 512], BF16) for i in range(5)]
            tw = sb("tw", [128, 512], BF16)
            ta = sb("ta", [128, 512], BF16)
            sg1 = sb("sg1", [128, 512], BF16)
            sg2 = sb("sg2", [32, 512], BF16)
            soL = [[sb("so%d_%d" % (i, j), [128, 512], F32) for i in range(8)] for j in range(2)]
            kunL = [sb("kun%d" % j, [128, 512], F32) for j in range(2)]
            ksbL = [sb("ksb%d" % j, [128, 512], F32) for j in range(2)]
            sqb = sb("sqb", [128, 512], BF16)
            rnL = [sb("rn0", [128, 512], F32)] * 2
            t1L = [sb("t1_0", [128, 512], F32)] * 2
            av = [sb("av%d" % i, [128, 512], F32) for i in range(2)]
            rkp = sb("rkp", [128, 8, 512], BF16)
            vst = sb("vst", [128, 2, D], BF16)
            gst = sb("gst", [128, 2, D], BF16)
            bst = sb("bst", [128, 4, NH], F32)
            for b_ in (xt[0], xt[1], hnb):
                kb.op("pool", lambda e, b_=b_: e.memset(b_[:], 0.0), writes=[b_])

            for si, (kind, bi, T) in enumerate(self.seqs):
                S = self.S[si]
                for (t0, nb) in blocks_of(T):
                    ws, we = t0 - 1, t0 + nb + 1
                    W = we - ws
                    n = nb
                    for ti, (o, cnt) in enumerate(tiles_of(W)):
                        a, b = ws + o, ws + o + cnt
                        a2_, b2_ = max(a, 0), min(b, T)
                        x_ = xt[ti % 2]
                        for (src, ro, nr) in self.xrows(si, a2_, b2_ - a2_):
                            kb.dma("sp", x_[a2_ - a + ro:a2_ - a + ro + nr, :], src, writes=[x_])
                        self.norm_transpose(ws, x_, cnt, C["nm0"], hnT, o, junk, ssb, hnb)
                    if ws < 0:
                        kb.op("pool", lambda e: e.memset(hnT[:, :, 0:1], 0.0), writes=[hnT])
                    if we > T:
                        kb.op("pool", lambda e, W=W: e.memset(hnT[:, :, W - 1:W], 0.0), writes=[hnT])
                    kb.op("dve", lambda e, n=n: e.tensor_tensor(out=tmp[:, :, 0:n], in0=hnT[:, :, 0:n], in1=hnT[:, :, 2:n + 2],
                                                                op=ALU.add), reads=[hnT], writes=[tmp])
                    kb.op("dve", lambda e, n=n: e.scalar_tensor_tensor(out=tmp[:, :, 0:n], in0=tmp[:, :, 0:n], scalar=0.5,
                                                                        in1=hnT[:, :, 1:n + 1], op0=ALU.mult, op1=ALU.subtract),
                          reads=[hnT], writes=[tmp])
                    def mix(m, dst):
                        for c in range(8):
                            kb.op("dve", lambda e, m=m, c=c, n=n, dst=dst: e.scalar_tensor_tensor(
                                out=dst[:, c, 0:n], in0=xx[:, c, 0:n], scalar=C["muT"][:, m * 8 + c:m * 8 + c + 1],
                                in1=hnT[:, c, 1:n + 1], op0=ALU.mult, op1=ALU.add), reads=[xx, hnT, C["muT"]], writes=[dst])
                    xw, xa, xg, xr, xk = xm[0], xm[1], xm[2], xm[3], xm[4]
                    xv = xm[0]
                    mix(1, xw); mix(4, xa); mix(5, xg); mix(0, xr); mix(2, xk)
                    p = self.psum()
                    for c in range(8):
                        kb.op("pe", lambda e, c=c, p=p, n=n: e.matmul(p[:, 0:n], lhsT=w1c[:, c, :], rhs=xw[:, c, 0:n],
                                                                      start=(c == 0), stop=(c == 7)), reads=[w1c, xw], writes=[p])
                    kb.op("act", lambda e, p=p, n=n: e.activation(out=tw[:, 0:n], in_=p[:, 0:n], func=AF.Tanh), reads=[p], writes=[tw])
                    p = self.psum()
                    for c in range(8):
                        kb.op("pe", lambda e, c=c, p=p, n=n: e.matmul(p[:, 0:n], lhsT=a1c[:, c, :], rhs=xa[:, c, 0:n],
                                                                      start=(c == 0), stop=(c == 7)), reads=[a1c, xa], writes=[p])
                    kb.op("act", lambda e, p=p, n=n: e.copy(out=ta[:, 0:n], in_=p[:, 0:n]), reads=[p], writes=[ta])
                    p = self.psum()
                    for c in range(8):
                        kb.op("pe", lambda e, c=c, p=p, n=n: e.matmul(p[:, 0:n], lhsT=g1c[:, c, 0:128], rhs=xg[:, c, 0:n],
                                                                      start=(c == 0), stop=(c == 7)), reads=[g1c, xg], writes=[p])
                    kb.op("act", lambda e, p=p, n=n: e.activation(out=sg1[:, 0:n], in_=p[:, 0:n], func=AF.Sigmoid), reads=[p], writes=[sg1])
                    p = self.psum()
                    for c in range(8):
                        kb.op("pe", lambda e, c=c, p=p, n=n: e.matmul(p[0:32, 0:n], lhsT=g1c[:, c, 128:160], rhs=xg[:, c, 0:n],
                                                                      start=(c == 0), stop=(c == 7)), reads=[g1c, xg], writes=[p])
                    kb.op("act", lambda e, p=p, n=n: e.activation(out=sg2[:, 0:n], in_=p[0:32, 0:n], func=AF.Sigmoid), reads=[p], writes=[sg2])
                    mix(3, xv)
                    for s in range(8):
                        so, kun, ksb, rn, t1 = soL[s % 2], kunL[s % 2], ksbL[s % 2], rnL[s % 2], t1L[s % 2]
                        sl = slice(s * 128, (s + 1) * 128)
                        p = self.psum()
                        for c in range(8):
                            kb.op("pe", lambda e, so=so, kun=kun, ksb=ksb, rn=rn, t1=t1, c=c, p=p, n=n, sl=sl: e.matmul(p[:, 0:n], lhsT=wr[0][:, c, sl], rhs=xr[:, c, 0:n],
                                                                                 start=(c == 0), stop=(c == 7)), reads=[wr[0], xr], writes=[p])
                        kb.op("act", lambda e, so=so, kun=kun, ksb=ksb, rn=rn, t1=t1, p=p, n=n, s=s: e.copy(out=so[0][:, 0:n], in_=p[:, 0:n]), reads=[p], writes=[so[0]])
                        p = self.psum()
                        for c in range(8):
                            kb.op("pe", lambda e, so=so, kun=kun, ksb=ksb, rn=rn, t1=t1, c=c, p=p, n=n, sl=sl: e.matmul(p[:, 0:n], lhsT=wr[1][:, c, sl], rhs=xk[:, c, 0:n],
                                                                                 start=(c == 0), stop=(c == 7)), reads=[wr[1], xk], writes=[p])
                        kb.op("act", lambda e, so=so, kun=kun, ksb=ksb, rn=rn, t1=t1, p=p, n=n: e.copy(out=ksb[:, 0:n], in_=p[:, 0:n]), reads=[p], writes=[ksb])
                        kb.op("dve", lambda e, so=so, kun=kun, ksb=ksb, rn=rn, t1=t1, n=n, s=s: e.tensor_scalar(out=kun[:, 0:n], in0=ksb[:, 0:n], scalar1=C["kkT"][:, s:s + 1],
                                                                         scalar2=None, op0=ALU.mult), reads=[ksb, C["kkT"]], writes=[kun])
                        kb.op("pool", lambda e, so=so, kun=kun, ksb=ksb, rn=rn, t1=t1, n=n: e.tensor_tensor(out=sqb[:, 0:n], in0=kun[:, 0:n], in1=kun[:, 0:n], op=ALU.mult),
                              reads=[kun], writes=[sqb])
                        p2 = self.psum()
                        kb.op("pe", lambda e, so=so, kun=kun, ksb=ksb, rn=rn, t1=t1, p2=p2, n=n: e.matmul(p2[:, 0:n], lhsT=C["blk"][:, :], rhs=sqb[:, 0:n], start=True, stop=True),
                              reads=[C["blk"], sqb], writes=[p2])
                        kb.op("act", lambda e, so=so, kun=kun, ksb=ksb, rn=rn, t1=t1, p2=p2, n=n: e.activation(out=rn[:, 0:n], in_=p2[:, 0:n], func=AF.Sqrt), reads=[p2], writes=[rn])
                        kb.op("dve", lambda e, so=so, kun=kun, ksb=ksb, rn=rn, t1=t1, n=n: e.tensor_scalar(out=rn[:, 0:n], in0=rn[:, 0:n], scalar1=1e-12, scalar2=None, op0=ALU.max),
                              reads=[rn], writes=[rn])
                        kb.op("dve", lambda e, so=so, kun=kun, ksb=ksb, rn=rn, t1=t1, n=n: e.reciprocal(out=rn[:, 0:n], in_=rn[:, 0:n]), reads=[rn], writes=[rn])
                        kb.op("dve", lambda e, so=so, kun=kun, ksb=ksb, rn=rn, t1=t1, n=n, s=s: e.tensor_tensor(out=so[1][:, 0:n], in0=kun[:, 0:n], in1=rn[:, 0:n], op=ALU.mult),
                              reads=[kun, rn], writes=[so[1]])
                        for z in range(2):
                            zs = slice(z * 64, (z + 1) * 64)
                            p = self.psum()
                            kb.op("pe", lambda e, so=so, kun=kun, ksb=ksb, rn=rn, t1=t1, p=p, n=n, sl=sl, zs=zs: e.matmul(p[:, 0:n], lhsT=w2c[zs, sl], rhs=tw[zs, 0:n], start=True, stop=True),
                                  reads=[w2c, tw], writes=[p])
                            kb.op("act", lambda e, so=so, kun=kun, ksb=ksb, rn=rn, t1=t1, p=p, n=n, s=s, z=z: e.activation(out=so[6 + z][:, 0:n], in_=p[:, 0:n], func=AF.Sigmoid,
                                                                                    bias=C["w0T"][:, z * 8 + s:z * 8 + s + 1], scale=1.0),
                                  reads=[p, C["w0T"]], writes=[so[6 + z]])
                            p = self.psum()
                            kb.op("pe", lambda e, so=so, kun=kun, ksb=ksb, rn=rn, t1=t1, p=p, n=n, sl=sl, zs=zs: e.matmul(p[:, 0:n], lhsT=a2c[zs, sl], rhs=ta[zs, 0:n], start=True, stop=True),
                                  reads=[a2c, ta], writes=[p])
                            kb.op("act", lambda e, so=so, kun=kun, ksb=ksb, rn=rn, t1=t1, p=p, n=n, s=s, z=z: e.activation(out=av[z][:, 0:n], in_=p[:, 0:n], func=AF.Sigmoid,
                                                                                    bias=C["a0T"][:, z * 8 + s:z * 8 + s + 1], scale=1.0),
                                  reads=[p, C["a0T"]], writes=[av[z]])
                            kb.op("dve", lambda e, so=so, kun=kun, ksb=ksb, rn=rn, t1=t1, n=n, s=s, z=z: e.tensor_scalar(out=t1[:, 0:n], in0=av[z][:, 0:n], scalar1=C["kaT"][:, s:s + 1],
                                                                                  scalar2=C["omka"][:, s:s + 1], op0=ALU.mult, op1=ALU.add),
                                  reads=[av[z], C["kaT"], C["omka"]], writes=[t1])
                            kb.op("pool", lambda e, so=so, kun=kun, ksb=ksb, rn=rn, t1=t1, n=n, s=s, z=z: e.tensor_tensor(out=so[2 + z][:, 0:n], in0=t1[:, 0:n], in1=ksb[:, 0:n], op=ALU.mult),
                                  reads=[t1, ksb], writes=[so[2 + z]])
                            kb.op("pool", lambda e, so=so, kun=kun, ksb=ksb, rn=rn, t1=t1, n=n, s=s, z=z: e.tensor_tensor(out=so[4 + z][:, 0:n], in0=so[1][:, 0:n], in1=av[z][:, 0:n], op=ALU.mult),
                                  reads=[so[1], av[z]], writes=[so[4 + z]])
                        kb.op("dve", lambda e, so=so, kun=kun, ksb=ksb, rn=rn, t1=t1, n=n, s=s: e.tensor_tensor(out=t1[:, 0:n], in0=so[2][:, 0:n], in1=so[3][:, 0:n], op=ALU.add),
                              reads=[so[2], so[3]], writes=[t1])
                        kb.op("dve", lambda e, so=so, kun=kun, ksb=ksb, rn=rn, t1=t1, n=n, s=s: e.scalar_tensor_tensor(out=rkp[:, s, 0:n], in0=t1[:, 0:n], scalar=C["rkT"][:, s:s + 1],
                                                                                in1=so[0][:, 0:n], op0=ALU.mult, op1=ALU.mult),
                              reads=[t1, so[0], C["rkT"]], writes=[rkp])
                        for q in range(8):
                            kb.dma("sp" if q % 2 == 0 else "act", S["fm"][q, s * 128:(s + 1) * 128, t0:t0 + n],
                                   so[q][:, 0:n], reads=[so[q]])
                    tl = tiles_of(n)
                    for ti, (o, cnt) in enumerate(tl):
                        for hf in range(2):
                            hs_ = slice(hf * 512, (hf + 1) * 512)
                            p = self.psum()
                            for c in range(8):
                                kb.op("pe", lambda e, c=c, p=p, o=o, cnt=cnt, hs_=hs_: e.matmul(p[:cnt, :], lhsT=xv[:, c, o:o + cnt], rhs=wr[2][:, c, hs_],
                                                                                                start=(c == 0), stop=(c == 7)), reads=[xv, wr[2]], writes=[p])
                            kb.op("act", lambda e, p=p, cnt=cnt, ti=ti, hs_=hs_: e.copy(out=vst[:cnt, ti % 2, hs_], in_=p[:cnt, :]), reads=[p], writes=[vst])
                            p = self.psum()
                            kb.op("pe", lambda e, p=p, o=o, cnt=cnt, hs_=hs_: e.matmul(p[:cnt, :], lhsT=sg1[:, o:o + cnt], rhs=g2a[:, hs_], start=True, stop=False),
                                  reads=[sg1, g2a], writes=[p])
                            kb.op("pe", lambda e, p=p, o=o, cnt=cnt, hs_=hs_: e.matmul(p[:cnt, :], lhsT=sg2[:, o:o + cnt], rhs=g2b[:, hs_], start=False, stop=True),
                                  reads=[sg2, g2b], writes=[p])
                            kb.op("dve", lambda e, p=p, cnt=cnt, ti=ti, hs_=hs_: e.tensor_copy(out=gst[:cnt, ti % 2, hs_], in_=p[:cnt, :]), reads=[p], writes=[gst])
                        p = self.psum()
                        for s in range(8):
                            kb.op("pe", lambda e, p=p, s=s, o=o, cnt=cnt: e.matmul(p[:cnt, 2 * s:2 * s + 2], lhsT=rkp[:, s, o:o + cnt], rhs=C["hsel"][:, :],
                                                                                   start=True, stop=True), reads=[rkp, C["hsel"]], writes=[p])
                        kb.op("dve", lambda e, p=p, cnt=cnt, ti=ti: e.tensor_copy(out=bst[:cnt, ti, :], in_=p[:cnt, 0:NH]), reads=[p], writes=[bst])
                        kb.dma("sp", S["v"][t0 + o:t0 + o + cnt, :], vst[:cnt, ti % 2, :], reads=[vst])
                        kb.dma("act", S["g"][t0 + o:t0 + o + cnt, :], gst[:cnt, ti % 2, :], reads=[gst])
                        kb.dma("sp", S["bonus"][t0 + o:t0 + o + cnt, :], bst[:cnt, ti, :], reads=[bst])
            kb.barrier()

    def phase_scan(self):
        kb, nc, I, C = self.kb, self.nc, self.I, self.C
        with contextlib.ExitStack() as st:
            sb = lambda name, shape, dt: self.sb(st, name, shape, dt)

            def ld(name, shape, dt, src, cast=False):
                b = sb("k_" + name, shape, dt)
                kb.dma("pool" if cast else "sp", b[:], src, writes=[b])
                C[name] = b
            ld("mask0", [128, 512], F32, I["c_mask"][0])
            ld("mask1", [128, 512], F32, I["c_mask"][1])
            ld("maskT0", [64, 1024], F32, I["c_maskT"][0])
            ld("maskT1", [64, 1024], F32, I["c_maskT"][1])
            ld("eye16", [64, 1024], BF16, I["c_eye16"], cast=True)
            ld("cm", [128, 512], F32, I["c_cm"])

            def mkstream(tag):
                B = {}
                B["BK"] = sb(tag + "BK", [128, 8, 128], BF16)
                B["AR"] = sb(tag + "AR", [128, 8, 128], BF16)
                B["BKp"] = [sb(tag + "BKp%d" % i, [128, 8, 128], BF16) for i in range(2)]
                B["ARp"] = [sb(tag + "ARp%d" % i, [128, 8, 128], BF16) for i in range(2)]
                B["vz"] = sb(tag + "vz", [128, 1024], BF16)
                for t_ in B["BKp"] + B["ARp"] + [B["vz"]]:
                    kb.op("pool", lambda e, t_=t_: e.memset(t_[:], 0.0), writes=[t_])
                B["BKT"] = sb(tag + "BKT", [128, 8, 128], BF16)
                B["M"] = sb(tag + "M", [128, 16, 128], BF16)
                B["A"] = sb(tag + "A", [64, 16, 64], BF16)
                B["AT"] = sb(tag + "AT", [64, 16, 64], BF16)
                B["X"] = sb(tag + "X", [64, 16, 64], BF16)
                B["Z"] = sb(tag + "Z", [64, 1024], BF16)
                B["vu"] = sb(tag + "vu", [128, 1024], BF16)
                B["o"] = sb(tag + "o", [64, 1024], F32)
                B["ST"] = sb(tag + "ST", [128, 8, 64], F32)
                B["STb"] = sb(tag + "STb", [128, 8, 64], BF16)
                B["PC"] = sb(tag + "PC", [128, 8], F32)
                return B

            def evac(i, fn_act, fn_dve, reads, writes):
                if i % 2 == 0:
                    kb.op("act", fn_act, reads=reads, writes=writes)
                else:
                    kb.op("dve", fn_dve, reads=reads, writes=writes)

            import os
            STOP = int(os.environ.get("SCAN_STOP", "9"))
            SUB = int(os.environ.get("SCAN_SUB", "9"))
            HACK = int(os.environ.get("HACK", "0"))

            free_banks = list(self.psb)

            def galloc(n):
                while len(free_banks) < n:
                    yield
                return [free_banks.pop(0) for _ in range(n)]

            def release(bs):
                free_banks.extend(bs)

            def stream(B, si, z):
                S = self.S[si]
                T = self.seqs[si][2]
                nch = -(-T // CH)
                mask = C["mask%d" % z]
                maskT = C["maskT%d" % z]
                kb.op("pool", lambda e: e.memset(B["ST"][:], 0.0), writes=[B["ST"]])
                kb.op("pool", lambda e: e.memset(B["STb"][:], 0.0), writes=[B["STb"]])
                order = range(nch) if z == 0 else range(nch - 1, -1, -1)
                for c in order:
                    c0 = c * CH
                    nt = min(CH, T - c0)
                    while not free_sets:
                        yield
                    Tt = free_sets.pop(0)
                    srcs = (("r", 0), ("kk", 1), ("k", 2 + z), ("b", 4 + z), ("sg", 6 + z))
                    for qi, (nm, q) in enumerate(srcs):
                        if nt < CH:
                            kb.op("pool", lambda e, nm=nm, Tt=Tt: e.memset(Tt[nm][:], 0.0), writes=[Tt[nm]])
                        kb.dma("sp" if qi % 2 == 0 else "act", Tt[nm][:, :, 0:nt],
                               S["fm"][q].rearrange("(s p) t -> p s t", p=128)[:, :, c0:c0 + nt], writes=[Tt[nm]])
                    if nt < CH:
                        kb.op("pool", lambda e: e.memset(B["vu"][64:128, :], 0.0), writes=[B["vu"]])
                    kb.dma("sp", B["vu"][64:64 + nt, :], S["v"][c0:c0 + nt, :], writes=[B["vu"]])
                    if nt < CH:
                        kb.op("pool", lambda e: e.memset(B["vz"][64:128, :], 0.0), writes=[B["vz"]])
                    kb.dma("act", B["vz"][64:64 + nt, :], S["v"][c0:c0 + nt, :], writes=[B["vz"]])
                    yield
                    cum2 = Tt["cum"][:].rearrange("p s t -> p (s t)")
                    sg2 = Tt["sg"][:].rearrange("p s t -> p (s t)")
                    kb.op("dve", lambda e, cum2=cum2, sg2=sg2: e.tensor_tensor_scan(out=cum2, data0=C["cm"][:, :], data1=sg2, initial=0.0,
                                                                op0=ALU.mult, op1=ALU.add), reads=[Tt["sg"], C["cm"]], writes=[Tt["cum"]])
                    if z == 0:
                        E1 = Tt["cum"]
                        last = CH - 1
                    else:
                        E1 = Tt["e1"]
                        last = 0
                        kb.op("pool", lambda e, Tt=Tt: e.tensor_tensor(out=Tt["e1"][:], in0=Tt["sg"][:], in1=Tt["cum"][:], op=ALU.subtract),
                              reads=[Tt["sg"], Tt["cum"]], writes=[Tt["e1"]])
                        for s_ in range(8):
                            kb.op("pool", lambda e, s_=s_, Tt=Tt: e.tensor_scalar(out=Tt["e1"][:, s_, :], in0=Tt["e1"][:, s_, :],
                                                                         scalar1=Tt["cum"][:, s_, CH - 1:CH], scalar2=None, op0=ALU.add),
                                  reads=[Tt["e1"], Tt["cum"]], writes=[Tt["e1"]])
                    kb.op("act", lambda e, Tt=Tt, E1=E1: e.activation(out=Tt["eP"][:], in_=E1[:], func=AF.Exp, scale=-LWC), reads=[E1], writes=[Tt["eP"]])
                    kb.op("act", lambda e, Tt=Tt, E1=E1: e.activation(out=Tt["eN"][:], in_=E1[:], func=AF.Exp, scale=LWC), reads=[E1], writes=[Tt["eN"]])
                    kb.op("pool", lambda e, Tt=Tt, E1=E1: e.tensor_tensor(out=Tt["eA"][:], in0=E1[:], in1=Tt["sg"][:], op=ALU.subtract),
                          reads=[E1, Tt["sg"]], writes=[Tt["eA"]])
                    kb.op("act", lambda e, Tt=Tt: e.activation(out=Tt["eA"][:], in_=Tt["eA"][:], func=AF.Exp, scale=-LWC), reads=[Tt["eA"]], writes=[Tt["eA"]])
                    kb.op("dve", lambda e, Tt=Tt: e.scalar_tensor_tensor(out=B["AR"][:, :, 0:64], in0=Tt["kk"][:], scalar=-1.0, in1=Tt["eA"][:],
                                                                  op0=ALU.mult, op1=ALU.mult), reads=[Tt["kk"], Tt["eA"]], writes=[B["AR"]])
                    kb.op("pool", lambda e, Tt=Tt: e.tensor_tensor(out=B["AR"][:, :, 64:128], in0=Tt["r"][:], in1=Tt["eP"][:], op=ALU.mult),
                          reads=[Tt["r"], Tt["eP"]], writes=[B["AR"]])
                    kb.op("dve", lambda e, Tt=Tt: e.tensor_tensor(out=B["BK"][:, :, 0:64], in0=Tt["b"][:], in1=Tt["eN"][:], op=ALU.mult),
                          reads=[Tt["b"], Tt["eN"]], writes=[B["BK"]])
                    kb.op("pool", lambda e, Tt=Tt: e.tensor_tensor(out=B["BK"][:, :, 64:128], in0=Tt["k"][:], in1=Tt["eN"][:], op=ALU.mult),
                          reads=[Tt["k"], Tt["eN"]], writes=[B["BK"]])
                    kb.op("dve", lambda e, Tt=Tt, last=last: e.tensor_copy(out=B["PC"][:, :], in_=Tt["eP"][:, :, last]), reads=[Tt["eP"]], writes=[B["PC"]])
                    for hp_ in range(2):
                        pq = slice(hp_ * 64, hp_ * 64 + 64)
                        kb.op("pool", lambda e, hp_=hp_, pq=pq: e.tensor_copy(out=B["BKp"][hp_][pq, :, :], in_=B["BK"][pq, :, :]), reads=[B["BK"]], writes=[B["BKp"][hp_]])
                        kb.op("act", lambda e, hp_=hp_, pq=pq: e.copy(out=B["ARp"][hp_][pq, :, :], in_=B["AR"][pq, :, :]), reads=[B["AR"]], writes=[B["ARp"][hp_]])
                    free_sets.append(Tt)
                    yield
                    if STOP <= 1:
                        continue
                    pall = yield from galloc(6)
                    pms = pall[0:4]
                    for h in range(16):
                        s_, hp = h // 2, h % 2
                        pm = pms[h // 4]
                        kb.op("pe", lambda e, pm=pm, h=h, s_=s_, hp=hp: e.matmul(pm[:, (h % 4) * 128:(h % 4 + 1) * 128], lhsT=B["BKp"][hp][:, s_, :],
                                                                               rhs=B["AR"][:, s_, :], start=True, stop=True),
                              reads=[B["BKp"][hp], B["AR"]], writes=[pm])
                    pts = pall[4:6]
                    for h in range(16):
                        s_, hp = h // 2, h % 2
                        pm = pts[h // 8]
                        kb.op("pe", lambda e, pm=pm, h=h, s_=s_, hp=hp: e.matmul(pm[0:64, (h % 8) * 64:(h % 8 + 1) * 64], lhsT=B["ARp"][hp][:, s_, 0:64],
                                                                               rhs=B["BK"][:, s_, 0:64], start=True, stop=True),
                              reads=[B["BK"], B["ARp"][hp]], writes=[pm])
                    yield
                    for g in range(4):
                        kb.op("dve", lambda e, g=g, pms=pms: e.tensor_tensor(out=B["M"][:, g * 4:(g + 1) * 4, :].rearrange("p h t -> p (h t)"),
                                                                         in0=pms[g][:, :], in1=mask[:, :], op=ALU.mult), reads=[pms[g], mask], writes=[B["M"]])
                    for g in range(2):
                        kb.op("dve", lambda e, g=g, pts=pts: e.tensor_tensor(out=B["AT"][:, g * 8:(g + 1) * 8, :].rearrange("p h t -> p (h t)"),
                                                                           in0=pts[g][0:64, :], in1=maskT[:, g * 512:(g + 1) * 512], op=ALU.mult),
                              reads=[pts[g], maskT], writes=[B["AT"]])
                    release(pall)
                    ptt = yield from galloc(2)
                    for s_ in range(8):
                        kb.op("pe", lambda e, s_=s_, ptt=ptt: e.matmul(ptt[s_ // 4][:, (s_ % 4) * 128:(s_ % 4 + 1) * 128], lhsT=B["BK"][:, s_, :], rhs=C["ident"][:, :],
                                                                       start=True, stop=True), reads=[B["BK"], C["ident"]], writes=[ptt[s_ // 4]])
                    yield
                    kb.op("act", lambda e: e.copy(out=B["A"][:], in_=B["M"][0:64, :, 0:64]), reads=[B["M"]], writes=[B["A"]])
                    for g in range(2):
                        kb.op("act", lambda e, g=g, ptt=ptt: e.copy(out=B["BKT"][:, g * 4:(g + 1) * 4, :].rearrange("p s t -> p (s t)"), in_=ptt[g][:, :]),
                              reads=[ptt[g]], writes=[B["BKT"]])
                    release(ptt)
                    yield
                    kb.op("pool", lambda e: e.tensor_tensor(out=B["X"][:].rearrange("p h t -> p (h t)"), in0=B["A"][:].rearrange("p h t -> p (h t)"),
                                                            in1=C["eye16"][:, :], op=ALU.add), reads=[B["A"], C["eye16"]], writes=[B["X"]])
                    for lvl in range(5):
                        lastl = (lvl == 4)
                        pab = yield from galloc(2 if lastl else 4)
                        pa = pab[0:2]
                        for h in range(16):
                            kb.op("pe", lambda e, h=h, pa=pa: e.matmul(pa[h // 8][0:64, (h % 8) * 64:(h % 8 + 1) * 64], lhsT=B["A"][:, h, :], rhs=B["AT"][:, h, :],
                                                                       start=True, stop=True), reads=[B["A"], B["AT"]], writes=[pa[h // 8]])
                        if not lastl:
                            pb = pab[2:4]
                            for h in range(16):
                                kb.op("pe", lambda e, h=h, pb=pb: e.matmul(pb[h // 8][0:64, (h % 8) * 64:(h % 8 + 1) * 64], lhsT=B["AT"][:, h, :], rhs=B["A"][:, h, :],
                                                                           start=True, stop=True), reads=[B["A"], B["AT"]], writes=[pb[h // 8]])
                        yield
                        for g in range(2):
                            kb.op("act", lambda e, g=g, pa=pa: e.copy(out=B["AT"][:, g * 8:(g + 1) * 8, :].rearrange("p h t -> p (h t)"), in_=pa[g][0:64, :]),
                                  reads=[pa[g]], writes=[B["AT"]])
                        if not lastl:
                            for g in range(2):
                                kb.op("act", lambda e, g=g, pb=pb: e.copy(out=B["A"][:, g * 8:(g + 1) * 8, :].rearrange("p h t -> p (h t)"), in_=pb[g][0:64, :]),
                                      reads=[pb[g]], writes=[B["A"]])
                        release(pab)
                        yield
                        px = yield from galloc(2)
                        for h in range(16):
                            kb.op("pe", lambda e, h=h, px=px: e.matmul(px[h // 8][0:64, (h % 8) * 64:(h % 8 + 1) * 64], lhsT=B["AT"][:, h, :], rhs=B["X"][:, h, :],
                                                                       start=True, stop=True), reads=[B["AT"], B["X"]], writes=[px[h // 8]])
                        yield
                        for g in range(2):
                            kb.op("dve", lambda e, g=g, px=px: e.tensor_tensor(out=B["X"][:, g * 8:(g + 1) * 8, :].rearrange("p h t -> p (h t)"),
                                                                               in0=px[g][0:64, :], in1=B["X"][:, g * 8:(g + 1) * 8, :].rearrange("p h t -> p (h t)"),
                                                                               op=ALU.add), reads=[px[g], B["X"]], writes=[B["X"]])
                        release(px)
                        yield
                    pz = yield from galloc(2)
                    for h in range(16):
                        s_, hp = h // 2, h % 2
                        oz = pz[h // 8][0:64, (h % 8) * 64:(h % 8 + 1) * 64]
                        kb.op("pe", lambda e, oz=oz, s_=s_, hp=hp: e.matmul(oz, lhsT=B["ARp"][hp][:, s_, 0:64], rhs=B["STb"][:, s_, :], start=True, stop=False),
                              reads=[B["ARp"][hp], B["STb"]], writes=[pz[h // 8]])
                        kb.op("pe", lambda e, oz=oz, h=h: e.matmul(oz, lhsT=B["M"][:, h, 0:64], rhs=B["vz"][:, h * 64:(h + 1) * 64], start=False, stop=True),
                              reads=[B["M"], B["vz"]], writes=[pz[h // 8]])
                    yield
                    for g in range(2):
                        evac(g, lambda e, g=g, pz=pz: e.copy(out=B["Z"][:, g * 512:(g + 1) * 512], in_=pz[g][0:64, :]),
                             lambda e, g=g, pz=pz: e.tensor_copy(out=B["Z"][:, g * 512:(g + 1) * 512], in_=pz[g][0:64, :]), [pz[g]], [B["Z"]])
                    release(pz)
                    yield
                    pu = yield from galloc(2)
                    for h in range(16):
                        kb.op("pe", lambda e, h=h, pu=pu: e.matmul(pu[h // 8][0:64, (h % 8) * 64:(h % 8 + 1) * 64], lhsT=B["X"][:, h, :], rhs=B["Z"][:, h * 64:(h + 1) * 64],
                                                            start=True, stop=True), reads=[B["X"], B["Z"]], writes=[pu[h // 8]])
                    yield
                    for g in range(2):
                        evac(g, lambda e, g=g, pu=pu: e.copy(out=B["vu"][0:64, g * 512:(g + 1) * 512], in_=pu[g][0:64, :]),
                             lambda e, g=g, pu=pu: e.tensor_copy(out=B["vu"][0:64, g * 512:(g + 1) * 512], in_=pu[g][0:64, :]), [pu[g]], [B["vu"]])
                    release(pu)
                    yield
                    pod = yield from galloc(4)
                    po = pod[0:2]
                    for h in range(16):
                        s_, hp = h // 2, h % 2
                        oo = po[h // 8][0:64, (h % 8) * 64:(h % 8 + 1) * 64]
                        kb.op("pe", lambda e, oo=oo, s_=s_, hp=hp: e.matmul(oo, lhsT=B["ARp"][hp][:, s_, 64:128], rhs=B["STb"][:, s_, :], start=True, stop=False),
                              reads=[B["ARp"][hp], B["STb"]], writes=[po[h // 8]])
                        kb.op("pe", lambda e, oo=oo, h=h: e.matmul(oo, lhsT=B["M"][:, h, 64:128], rhs=B["vu"][:, h * 64:(h + 1) * 64], start=False, stop=True),
                              reads=[B["M"], B["vu"]], writes=[po[h // 8]])
                    pd = pod[2:4]
                    for s_ in range(8):
                        kb.op("pe", lambda e, s_=s_, pd=pd: e.matmul(pd[s_ // 4][:, (s_ % 4) * 128:(s_ % 4 + 1) * 128], lhsT=B["BKT"][:, s_, :], rhs=B["vu"][:, s_ * 128:(s_ + 1) * 128],
                                                              start=True, stop=True), reads=[B["BKT"], B["vu"]], writes=[pd[s_ // 4]])
                    yield
                    for g in range(2):
                        kb.op("act", lambda e, g=g, po=po: e.copy(out=B["o"][:, g * 512:(g + 1) * 512], in_=po[g][0:64, :]), reads=[po[g]], writes=[B["o"]])
                    kb.dma("sp", S["o"][z, c0:c0 + nt, :], B["o"][0:nt, :], reads=[B["o"]])
                    for g in range(2):
                        pv = pd[g][:, :].rearrange("p (s t) -> p s t", t=128)
                        for hp in range(2):
                            ps_ = slice(hp * 64, hp * 64 + 64)
                            kb.op("dve", lambda e, g=g, pv=pv, hp=hp, ps_=ps_: e.tensor_tensor(out=B["ST"][ps_, g * 4:(g + 1) * 4, :], in0=pv[ps_, :, hp * 64:(hp + 1) * 64],
                                                                                             in1=B["ST"][ps_, g * 4:(g + 1) * 4, :], op=ALU.add),
                                  reads=[pd[g], B["ST"]], writes=[B["ST"]])
                    release(pod)
                    yield
                    for s_ in range(8):
                        kb.op("pool" if s_ % 2 else "dve", lambda e, s_=s_: e.tensor_scalar(out=B["ST"][:, s_, :], in0=B["ST"][:, s_, :], scalar1=B["PC"][:, s_:s_ + 1], scalar2=None, op0=ALU.mult),
                              reads=[B["ST"], B["PC"]], writes=[B["ST"]])
                    yield
                    kb.op("act", lambda e: e.copy(out=B["STb"][:], in_=B["ST"][:]), reads=[B["ST"]], writes=[B["STb"]])
                    yield

            NS = 4
            Bs = [mkstream("s%d_" % i) for i in range(NS)]
            free_sets = []
            for i in range(2):
                free_sets.append({nm: sb("t%d_%s" % (i, nm), [128, 8, 64], F32)
                                  for nm in ("r", "kk", "k", "b", "sg", "cum", "e1", "eP", "eN", "eA")})
            todo = [(si, z) for si in range(len(self.seqs)) for z in range(2)]
            active = [None] * NS
            while todo or any(a is not None for a in active):
                for i in range(NS):
                    if active[i] is None and todo:
                        si, z = todo.pop(0)
                        active[i] = stream(Bs[i], si, z)
                    if active[i] is not None:
                        try:
                            next(active[i])
                        except StopIteration:
                            active[i] = None
            kb.barrier()

    def phase3(self):
        self.phase3a()
        self.ffn_pass(0)

    def phase4(self):
        self.phase4a()
        self.ffn_pass(1)

    def phase3a(self):
        kb, nc, I, C = self.kb, self.nc, self.I, self.C
        with contextlib.ExitStack() as st:
            sb = lambda name, shape, dt: self.sb(st, "p3_" + name, shape, dt)
            gnw = self.rowvec(st, "gnw", I["gn_w"][0:1, :])
            gnb = self.rowvec(st, "gnb", I["gn_b"][0:1, :])
            wo = sb("wo", [128, 8, D], BF16)
            for c in range(8):
                kb.dma("pool", wo[:, c, :], I["w_o"][c * 128:(c + 1) * 128, :], writes=[wo])
            NB = 4
            sets = []
            for i in range(NB):
                sets.append(dict(a=sb("o0_%d" % i, [128, D], F32), b=sb("o1_%d" % i, [128, D], F32), v=sb("v_%d" % i, [128, D], BF16),
                                 g=sb("g_%d" % i, [128, D], BF16), bn=sb("b_%d" % i, [128, NH], F32), x=sb("x_%d" % i, [128, D], F32),
                                 s=sb("st_%d" % i, [128, 6, NH], F32), h=sb("h1_%d" % i, [128, D], F32),
                                 ogb=sb("ogb%d" % i, [128, D], BF16), ogT=sb("ogT%d" % i, [128, 8, 128], BF16)))
                kb.op("pool", lambda e, t_=sets[i]["ogb"]: e.memset(t_[:], 0.0), writes=[sets[i]["ogb"]])
            free_banks = list(self.psb)

            def galloc(nb_):
                while len(free_banks) < nb_:
                    yield
                return [free_banks.pop(0) for _ in range(nb_)]

            def tile_gen(Q, si, t0, n):
                S = self.S[si]
                a, b_, v_, g_, bn, x_, s_, h_, ogb, ogT = Q["a"], Q["b"], Q["v"], Q["g"], Q["bn"], Q["x"], Q["s"], Q["h"], Q["ogb"], Q["ogT"]
                kb.dma("sp", a[:n, :], S["o"][0, t0:t0 + n, :], writes=[a])
                kb.dma("act", b_[:n, :], S["o"][1, t0:t0 + n, :], writes=[b_])
                kb.dma("sp", v_[:n, :], S["v"][t0:t0 + n, :], writes=[v_])
                kb.dma("act", g_[:n, :], S["g"][t0:t0 + n, :], writes=[g_])
                kb.dma("sp", bn[:n, :], S["bonus"][t0:t0 + n, :], writes=[bn])
                for (src, ro, nr) in self.xrows(si, t0, n):
                    kb.dma("act", x_[ro:ro + nr, :], src, writes=[x_])
                yield
                kb.op("pool", lambda e: e.tensor_tensor(out=a[:n, :], in0=a[:n, :], in1=b_[:n, :], op=ALU.add), reads=[b_], writes=[a])
                kb.op("pool", lambda e: e.tensor_tensor(out=b_[:n, :], in0=a[:n, :], in1=a[:n, :], op=ALU.mult), reads=[a], writes=[b_])
                yield
                a3 = a[:n, :].rearrange("p (h j) -> p h j", j=HS)
                b3 = b_[:n, :].rearrange("p (h j) -> p h j", j=HS)
                kb.op("dve", lambda e: e.reduce_sum(out=s_[:n, 0, :], in_=a3, axis=AX.X), reads=[a], writes=[s_])
                kb.op("dve", lambda e: e.reduce_sum(out=s_[:n, 1, :], in_=b3, axis=AX.X), reads=[b_], writes=[s_])
                kb.op("dve", lambda e: e.tensor_scalar(out=s_[:n, 2, :], in0=s_[:n, 0, :], scalar1=1.0 / HS, scalar2=None, op0=ALU.mult), reads=[s_], writes=[s_])
                kb.op("dve", lambda e: e.tensor_tensor(out=s_[:n, 3, :], in0=s_[:n, 2, :], in1=s_[:n, 2, :], op=ALU.mult), reads=[s_], writes=[s_])
                kb.op("dve", lambda e: e.scalar_tensor_tensor(out=s_[:n, 4, :], in0=s_[:n, 1, :], scalar=1.0 / HS, in1=s_[:n, 3, :], op0=ALU.mult, op1=ALU.subtract), reads=[s_], writes=[s_])
                kb.op("dve", lambda e: e.tensor_scalar(out=s_[:n, 4, :], in0=s_[:n, 4, :], scalar1=64e-5, scalar2=None, op0=ALU.add), reads=[s_], writes=[s_])
                yield
                kb.op("act", lambda e: e.activation(out=s_[:n, 5, :], in_=s_[:n, 4, :], func=AF.Sqrt), reads=[s_], writes=[s_])
                yield
                kb.op("dve", lambda e: e.reciprocal(out=s_[:n, 5, :], in_=s_[:n, 5, :]), reads=[s_], writes=[s_])
                for h in range(NH):
                    hs_ = slice(h * HS, (h + 1) * HS)
                    kb.op("dve" if h % 2 == 0 else "pool", lambda e, h=h, hs_=hs_: e.tensor_scalar(
                        out=a[:n, hs_], in0=a[:n, hs_], scalar1=s_[:n, 2, h:h + 1], scalar2=s_[:n, 5, h:h + 1],
                        op0=ALU.subtract, op1=ALU.mult), reads=[a, s_], writes=[a])
                yield
                kb.op("pool", lambda e: e.tensor_tensor(out=a[:n, :], in0=a[:n, :], in1=gnw[:n, :], op=ALU.mult), reads=[gnw], writes=[a])
                kb.op("pool", lambda e: e.tensor_tensor(out=a[:n, :], in0=a[:n, :], in1=gnb[:n, :], op=ALU.add), reads=[gnb], writes=[a])
                yield
                for h in range(NH):
                    hs_ = slice(h * HS, (h + 1) * HS)
                    kb.op("dve", lambda e, h=h, hs_=hs_: e.scalar_tensor_tensor(
                        out=a[:n, hs_], in0=v_[:n, hs_], scalar=bn[:n, h:h + 1], in1=a[:n, hs_], op0=ALU.mult, op1=ALU.add),
                        reads=[v_, bn], writes=[a])
                yield
                kb.op("pool", lambda e: e.tensor_tensor(out=ogb[:n, :], in0=a[:n, :], in1=g_[:n, :], op=ALU.mult), reads=[a, g_], writes=[ogb])
                yield
                (pt,) = yield from galloc(1)
                ptb = pt.t[:].bitcast(BF16)
                for c in range(8):
                    kb.op("pe", lambda e, c=c: e.transpose(out=ptb[:, c * 128:c * 128 + n], in_=ogb[:n, c * 128:(c + 1) * 128],
                                                           identity=C["ident"][:n, :n]), reads=[ogb, C["ident"]], writes=[pt])
                yield
                src3 = ptb.rearrange("p (c t) -> p c t", t=128)
                kb.op("act", lambda e: e.copy(out=ogT[:, 0:8, 0:n], in_=src3[:, 0:8, 0:n]), reads=[pt], writes=[ogT])
                free_banks.append(pt)
                yield
                pp = yield from galloc(2)
                for hf in range(2):
                    hs_ = slice(hf * 512, (hf + 1) * 512)
                    for c in range(8):
                        kb.op("pe", lambda e, hf=hf, c=c, hs_=hs_: e.matmul(pp[hf][:n, :], lhsT=ogT[:, c, 0:n], rhs=wo[:, c, hs_], start=(c == 0), stop=(c == 7)),
                              reads=[ogT, wo], writes=[pp[hf]])
                yield
                for hf in range(2):
                    hs_ = slice(hf * 512, (hf + 1) * 512)
                    kb.op("dve", lambda e, hf=hf, hs_=hs_: e.tensor_tensor(out=h_[:n, hs_], in0=pp[hf][:n, :], in1=x_[:n, hs_], op=ALU.add),
                          reads=[pp[hf], x_], writes=[h_])
                free_banks.extend(pp)
                kb.dma("sp", S["h1"][t0:t0 + n, :], h_[:n, :], reads=[h_])
                yield

            todo = [(si, t0, n) for si, (kind, bi, T) in enumerate(self.seqs) for (t0, n) in tiles_of(T)]
            active = [None] * NB
            while todo or any(x is not None for x in active):
                for i in range(NB):
                    if active[i] is None and todo:
                        si, t0, n = todo.pop(0)
                        active[i] = tile_gen(sets[i], si, t0, n)
                    if active[i] is not None:
                        try:
                            next(active[i])
                        except StopIteration:
                            active[i] = None
            kb.barrier()

    def ffn_pass(self, l):
        kb, nc, I, C = self.kb, self.nc, self.I, self.C
        NF = DFF // 128
        with contextlib.ExitStack() as st:
            sb = lambda name, shape, dt: self.sb(st, "f%d_" % l + name, shape, dt)
            nf = self.rowvec(st, "nf%d" % l, I["norm_ffn"][l:l + 1, :])
            if l == 0:
                n2 = self.rowvec(st, "nm1", I["norm_mix"][1:2, :])
                dftc = sb("dftc", [128, 256], BF16)
                kb.dma("pool", dftc[:], I["c_dftc"], writes=[dftc])
            else:
                n2 = self.rowvec(st, "nfin", I["norm_final"][0:1, :])
            win = sb("win", [128, 8, 2 * DFF], BF16)
            for c in range(8):
                kb.dma("pool", win[:, c, :], I["ffn_w_in"][l, c * 128:(c + 1) * 128, :], writes=[win])
            wout = sb("wout", [128, NF, D], BF16)
            for f in range(NF):
                kb.dma("pool", wout[:, f, :], I["ffn_w_out"][l, f * 128:(f + 1) * 128, :], writes=[wout])
            cw = sb("cw", [128, 3 * NF], F32)
            cb = sb("cb", [128, NF], F32)
            for k in range(3):
                kb.dma("sp", cw[:, k * NF:(k + 1) * NF], I["conv_w"][l, k].rearrange("(f p) -> p f", p=128), writes=[cw])
            kb.dma("sp", cb[:, :], I["conv_b"][l].rearrange("(f p) -> p f", p=128), writes=[cb])
            ht = [sb("ht%d" % i, [128, D], F32) for i in range(4)]
            ssb = sb("ssb", [128, 4], F32)
            hnb = sb("hnb", [128, D], BF16)
            junk = hnb
            hnT = sb("hnT", [128, 8, 512], BF16)
            uaL = [sb("ua%d" % i, [128, 512], F32) for i in range(2)]
            tmpL = [sb("tmp%d" % i, [128, 512], F32) for i in range(2)]
            slL = [sb("sl%d" % i, [128, 512], F32) for i in range(2)]
            zT = sb("zT", [128, NF, 512], BF16)
            ot = sb("ot", [128, 2, D], BF16) if l == 0 else sb("ot", [128, 1, D], F32)
            kb.op("pool", lambda e: e.memset(zT[:], 0.0), writes=[zT])
            kb.op("pool", lambda e: e.memset(hnb[:], 0.0), writes=[hnb])
            for t_ in ht:
                kb.op("pool", lambda e, t_=t_: e.memset(t_[:], 0.0), writes=[t_])
            for si, (kind, bi, T) in enumerate(self.seqs):
                S = self.S[si]
                Hs = S["h1"] if l == 0 else S["h3"]
                yout = self.y_p if kind == "p" else self.y_s
                for (t0, nb) in blocks_of(T):
                    ws, we = t0 - 1, t0 + nb + 1
                    W = we - ws
                    n = nb
                    wt = tiles_of(W)
                    for ti, (o, cnt) in enumerate(wt):
                        a, b = ws + o, ws + o + cnt
                        a2_, b2_ = max(a, 0), min(b, T)
                        kb.dma("sp" if ti % 2 == 0 else "act", ht[ti][a2_ - a:b2_ - a, :], Hs[a2_:b2_, :], writes=[ht[ti]])
                        self.norm_transpose(ws, ht[ti], cnt, nf, hnT, o, junk, ssb, hnb)
                    if ws < 0:
                        kb.op("pool", lambda e: e.memset(hnT[:, :, 0:1], 0.0), writes=[hnT])
                    if we > T:
                        kb.op("pool", lambda e, W=W: e.memset(hnT[:, :, W - 1:W], 0.0), writes=[hnT])
                    pls = {}
                    for i in range(NF + 3):
                        if i < NF:
                            f = i
                            fa = slice(f * 128, (f + 1) * 128)
                            fl = slice(DFF + f * 128, DFF + (f + 1) * 128)
                            pa = self.psum()
                            for c in range(8):
                                kb.op("pe", lambda e, pa=pa, c=c, fa=fa, W=W: e.matmul(pa[:, 0:W], lhsT=win[:, c, fa], rhs=hnT[:, c, 0:W], start=(c == 0), stop=(c == 7)),
                                      reads=[win, hnT], writes=[pa])
                            pl = self.psum()
                            pls[f] = pl
                            for c in range(8):
                                kb.op("pe", lambda e, pl=pl, c=c, fl=fl, W=W: e.matmul(pl[:, 0:W], lhsT=win[:, c, fl], rhs=hnT[:, c, 0:W], start=(c == 0), stop=(c == 7)),
                                      reads=[win, hnT], writes=[pl])
                        if 2 <= i <= NF + 1:
                            f = i - 2
                            tmp, sl = tmpL[f % 2], slL[f % 2]
                            kb.op("act", lambda e, f=f, n=n, tmp=tmp, sl=sl: e.activation(out=sl[:, 0:n], in_=tmp[:, 0:n], func=AF.Silu, bias=cb[:, f:f + 1], scale=1.0),
                                  reads=[tmp, cb], writes=[sl])
                        if i < NF:
                            ua = uaL[i % 2]
                            kb.op("act", lambda e, pa=pa, W=W, ua=ua: e.copy(out=ua[:, 0:W], in_=pa[:, 0:W]), reads=[pa], writes=[ua])
                        if 3 <= i <= NF + 2:
                            f = i - 3
                            sl = slL[f % 2]
                            pl_ = pls.pop(f)
                            kb.op("dve", lambda e, f=f, n=n, pl_=pl_, sl=sl: e.tensor_tensor(out=zT[:, f, 1:n + 1], in0=pl_[:, 1:n + 1], in1=sl[:, 0:n], op=ALU.mult),
                                  reads=[pl_, sl], writes=[zT])
                        if 1 <= i <= NF:
                            f = i - 1
                            ua, tmp = uaL[f % 2], tmpL[f % 2]
                            kb.op("pool", lambda e, f=f, n=n, ua=ua, tmp=tmp: e.tensor_scalar(out=tmp[:, 0:n], in0=ua[:, 0:n], scalar1=cw[:, f:f + 1], scalar2=None, op0=ALU.mult),
                                  reads=[ua, cw], writes=[tmp])
                            kb.op("dve", lambda e, f=f, n=n, ua=ua, tmp=tmp: e.scalar_tensor_tensor(out=tmp[:, 0:n], in0=ua[:, 1:n + 1], scalar=cw[:, NF + f:NF + f + 1], in1=tmp[:, 0:n],
                                                                                    op0=ALU.mult, op1=ALU.add), reads=[ua, cw], writes=[tmp])
                            kb.op("dve", lambda e, f=f, n=n, ua=ua, tmp=tmp: e.scalar_tensor_tensor(out=tmp[:, 0:n], in0=ua[:, 2:n + 2], scalar=cw[:, 2 * NF + f:2 * NF + f + 1], in1=tmp[:, 0:n],
                                                                                    op0=ALU.mult, op1=ALU.add), reads=[ua, cw], writes=[tmp])
                    for ti, (o, cnt) in enumerate(wt):
                        h_ = ht[ti]
                        for hf in range(2):
                            hs_ = slice(hf * 512, (hf + 1) * 512)
                            py = self.psum()
                            for f in range(NF):
                                kb.op("pe", lambda e, py=py, f=f, o=o, cnt=cnt, hs_=hs_: e.matmul(py[:cnt, :], lhsT=zT[:, f, o:o + cnt], rhs=wout[:, f, hs_],
                                                                                                  start=(f == 0), stop=(f == NF - 1)), reads=[zT, wout], writes=[py])
                            kb.op("dve", lambda e, py=py, cnt=cnt, hs_=hs_, h_=h_: e.tensor_tensor(out=h_[:cnt, hs_], in0=py[:cnt, :], in1=h_[:cnt, hs_], op=ALU.add),
                                  reads=[py], writes=[h_])
                        lo = max(o, 1) - o
                        hi = min(o + cnt, W - 1) - o
                        tok0 = ws + o + lo
                        if l == 0:
                            if hi > lo:
                                kb.dma("sp", S["h2"][tok0:tok0 + hi - lo, :], h_[lo:hi, :], reads=[h_])
                            self.norm_transpose(ws, h_, cnt, n2, hnT, o, junk, ssb, hnb)
                            ob = ot
                            for gp in range(4):
                                pd = self.psum()
                                for gg in range(2):
                                    g = gp * 2 + gg
                                    kb.op("pe", lambda e, pd=pd, g=g, gg=gg, o=o, cnt=cnt: e.matmul(pd[:cnt, gg * 256:(gg + 1) * 256], lhsT=hnT[:, g, o:o + cnt], rhs=dftc[:, :],
                                                                                                    start=True, stop=True), reads=[hnT, dftc], writes=[pd])
                                pd4 = pd[:cnt, :].rearrange("p (g k c) -> p g k c", g=2, k=2)
                                for k in range(2):
                                    dst = ob[:cnt, k, gp * 256:(gp + 1) * 256].rearrange("p (g c) -> p g c", g=2)
                                    if k == 0:
                                        kb.op("act", lambda e, dst=dst, pd4=pd4, k=k: e.copy(out=dst, in_=pd4[:, :, k, :]), reads=[pd], writes=[ob])
                                    else:
                                        kb.op("dve", lambda e, dst=dst, pd4=pd4, k=k: e.tensor_copy(out=dst, in_=pd4[:, :, k, :]), reads=[pd], writes=[ob])
                            if hi > lo:
                                kb.dma("sp", S["yc"][0, tok0:tok0 + hi - lo, :], ob[lo:hi, 0, :], reads=[ob])
                                kb.dma("act", S["yc"][1, tok0:tok0 + hi - lo, :], ob[lo:hi, 1, :], reads=[ob])
                        else:
                            kb.op("dve", lambda e, h_=h_, cnt=cnt: e.scalar_tensor_tensor(out=junk[:cnt, :], in0=h_[:cnt, :], scalar=1.0, in1=h_[:cnt, :],
                                                                                         op0=ALU.mult, op1=ALU.mult, accum_out=ssb[:cnt, 0:1]), reads=[h_], writes=[junk, ssb])
                            kb.op("dve", lambda e, cnt=cnt: e.tensor_scalar(out=ssb[:cnt, 1:2], in0=ssb[:cnt, 0:1], scalar1=1.0 / D, scalar2=1e-6, op0=ALU.mult, op1=ALU.add),
                                  reads=[ssb], writes=[ssb])
                            kb.op("act", lambda e, cnt=cnt: e.activation(out=ssb[:cnt, 2:3], in_=ssb[:cnt, 1:2], func=AF.Sqrt), reads=[ssb], writes=[ssb])
                            kb.op("dve", lambda e, cnt=cnt: e.reciprocal(out=ssb[:cnt, 3:4], in_=ssb[:cnt, 2:3]), reads=[ssb], writes=[ssb])
                            kb.op("dve", lambda e, h_=h_, cnt=cnt: e.scalar_tensor_tensor(out=ot[:cnt, 0, :], in0=h_[:cnt, :], scalar=ssb[:cnt, 3:4], in1=n2[:cnt, :],
                                                                                         op0=ALU.mult, op1=ALU.mult), reads=[h_, ssb, n2], writes=[ot])
                            lo2 = max(lo, NMETA - (ws + o))
                            if hi > lo2:
                                tk = ws + o + lo2 - NMETA
                                kb.dma("sp", yout[bi, tk:tk + hi - lo2, :], ot[lo2:hi, 0, :], reads=[ot])
            kb.barrier()

    def phase4a(self):
        kb, nc, I, C = self.kb, self.nc, self.I, self.C
        NTmax = -(-max(T for _, _, T in self.seqs) // 128)
        with contextlib.ExitStack() as st:
            sb = lambda name, shape, dt: self.sb(st, "p4_" + name, shape, dt)
            wf = sb("wf", [128, 8, D], BF16)
            for c in range(8):
                kb.dma("pool", wf[:, c, :], I["w_f"][c * 128:(c + 1) * 128, :], writes=[wf])
            Y = [sb("Y%d" % k, [128, NTmax, D], BF16) for k in range(2)]
            M = [[sb("M%d_%d" % (k, j), [128, NTmax, 128], BF16) for k in range(2)] for j in range(2)]
            fb = sb("fb", [128, D], BF16)
            fT = sb("fT", [128, 8, 128], BF16)
            h2t = [sb("h2t%d" % i, [128, D], F32) for i in range(2)]
            h3t = [sb("h3t%d" % i, [128, D], F32) for i in range(2)]
            kb.op("pool", lambda e: e.memset(fb[:], 0.0), writes=[fb])
            it = 0
            for si, (kind, bi, T) in enumerate(self.seqs):
                S = self.S[si]
                dft = I["c_dft_p"] if kind == "p" else I["c_dft_s"]
                NT = -(-T // 128)
                nfull = T // 128
                rem = T - nfull * 128
                for k in range(2):
                    if nfull:
                        kb.dma("sp" if k == 0 else "act", Y[k][:, 0:nfull, :], S["yc"][k, 0:nfull * 128, :].rearrange("(c p) d -> p c d", p=128), writes=[Y[k]])
                    if rem:
                        kb.dma("sp" if k == 0 else "act", Y[k][0:rem, nfull, :], S["yc"][k, nfull * 128:T, :], writes=[Y[k]])
                for (k0, kn) in tiles_of(T):
                    j = it % 2
                    it += 1
                    for k in range(2):
                        if nfull:
                            kb.dma("sp" if k == 0 else "act", M[j][k][:, 0:nfull, 0:kn], dft[k, 0:nfull * 128, k0:k0 + kn].rearrange("(c p) q -> p c q", p=128), writes=[M[j][k]])
                        if rem:
                            kb.dma("sp" if k == 0 else "act", M[j][k][0:rem, nfull, 0:kn], dft[k, nfull * 128:T, k0:k0 + kn], writes=[M[j][k]])
                    kb.dma("sp", h2t[j][:kn, :], S["h2"][k0:k0 + kn, :], writes=[h2t[j]])
                    for hf in range(2):
                        hs_ = slice(hf * 512, (hf + 1) * 512)
                        pf = self.psum()
                        for ch in range(NT):
                            cc = 128 if ch < nfull else rem
                            for k in range(2):
                                kb.op("pe", lambda e, pf=pf, ch=ch, cc=cc, k=k, j=j, kn=kn, hs_=hs_: e.matmul(pf[:kn, :], lhsT=M[j][k][0:cc, ch, 0:kn], rhs=Y[k][0:cc, ch, hs_],
                                                                                                           start=(ch == 0 and k == 0), stop=(ch == NT - 1 and k == 1)),
                                      reads=[M[j][k], Y[k]], writes=[pf])
                        if hf == 0:
                            kb.op("act", lambda e, pf=pf, kn=kn, hs_=hs_: e.copy(out=fb[:kn, hs_], in_=pf[:kn, :]), reads=[pf], writes=[fb])
                        else:
                            kb.op("dve", lambda e, pf=pf, kn=kn, hs_=hs_: e.tensor_copy(out=fb[:kn, hs_], in_=pf[:kn, :]), reads=[pf], writes=[fb])
                    self.transpose_into(fb, kn, fT, 0)
                    for hf in range(2):
                        hs_ = slice(hf * 512, (hf + 1) * 512)
                        p = self.psum()
                        for c in range(8):
                            kb.op("pe", lambda e, p=p, c=c, kn=kn, hs_=hs_: e.matmul(p[:kn, :], lhsT=fT[:, c, 0:kn], rhs=wf[:, c, hs_], start=(c == 0), stop=(c == 7)),
                                  reads=[fT, wf], writes=[p])
                        kb.op("dve", lambda e, p=p, kn=kn, hs_=hs_, j=j: e.tensor_tensor(out=h3t[j][:kn, hs_], in0=p[:kn, :], in1=h2t[j][:kn, hs_], op=ALU.add),
                              reads=[p, h2t[j]], writes=[h3t[j]])
                    kb.dma("act", S["h3"][k0:k0 + kn, :], h3t[j][:kn, :], reads=[h3t[j]])
            kb.barrier()


def host_consts(Tp, Ts, dft=False):
    c = {}
    c["c_ident"] = np.eye(128, dtype=np.float32)
    blk = np.zeros((128, 128), np.float32)
    blk[:64, :64] = 1; blk[64:, 64:] = 1
    c["c_blk"] = blk
    hs = np.zeros((128, 2), np.float32); hs[:64, 0] = 1; hs[64:, 1] = 1
    c["c_hsel"] = hs
    s_ = np.arange(64)[:, None]; t_ = np.arange(64)[None, :]
    m = np.zeros((2, 128, 128), np.float32)
    for r0 in (0, 64):
        m[0, r0:r0 + 64, 0:64] = (s_ < t_); m[0, r0:r0 + 64, 64:128] = (s_ <= t_)
        m[1, r0:r0 + 64, 0:64] = (s_ > t_); m[1, r0:r0 + 64, 64:128] = (s_ >= t_)
    c["c_mask"] = np.tile(m, (1, 1, 4))
    mt = np.zeros((2, 64, 64), np.float32)
    mt[0] = (s_ > t_); mt[1] = (s_ < t_)
    c["c_maskT"] = np.tile(mt, (1, 1, 16))
    cm = np.ones((128, 512), np.float32); cm[:, ::64] = 0
    c["c_cm"] = cm
    c["c_eye2"] = np.eye(128, dtype=np.float32)
    c["c_eye16"] = np.tile(np.eye(64, dtype=np.float32), (1, 16))
    cc = np.arange(128)
    ang = 2 * np.pi * np.outer(cc, cc) / 128
    c["c_dftc"] = np.concatenate([np.cos(ang), np.sin(ang)], 1).astype(np.float32) / np.sqrt(128)
    for nm, T in ((("c_dft_p", Tp), ("c_dft_s", Ts)) if dft else ()):
        t = np.arange(T, dtype=np.int64)
        ph = (np.outer(t, t) % T).astype(np.float64) * (2 * np.pi / T)
        c[nm] = np.stack([np.cos(ph), -np.sin(ph)]).astype(np.float32) / np.float32(np.sqrt(T))
        c[nm] = c[nm].astype(ml_dtypes.bfloat16)
    return c


PHASES = ("p1", "scan", "p3", "p4")
NCORES = 8


def kernel(x_prompt, x_sample, meta_tokens, norm_mix, norm_ffn, norm_final,
           rwkv_mu, rwkv_w_rkv, rwkv_w0, rwkv_w1, rwkv_w2, rwkv_a0, rwkv_a1, rwkv_a2,
           rwkv_g1, rwkv_g2, rwkv_k_k, rwkv_k_a, rwkv_r_k, rwkv_gn_w, rwkv_gn_b, rwkv_w_o,
           fnet_w_o, ffn_w_in, ffn_conv_w, ffn_conv_b, ffn_w_out):
    f = lambda a: np.ascontiguousarray(np.asarray(a, dtype=np.float32))
    x_prompt, x_sample = f(x_prompt), f(x_sample)
    Bp, Sp, _ = x_prompt.shape
    Bs, Ss, _ = x_sample.shape
    npc, nsc = Bp // NCORES, Bs // NCORES
    Tp, Ts = Sp + NMETA, Ss + NMETA
    prog = Prog(npc, Tp, nsc, Ts, debug=False, phases=PHASES)
    nc = prog.build()
    shared = {
        "meta": f(meta_tokens), "norm_mix": f(norm_mix), "norm_ffn": f(norm_ffn),
        "norm_final": f(norm_final).reshape(1, D), "mu": f(rwkv_mu)[0], "w_rkv": f(rwkv_w_rkv)[0],
        "w0": f(rwkv_w0)[0], "w1": f(rwkv_w1)[0], "w2": f(rwkv_w2)[0],
        "a0": f(rwkv_a0)[0], "a1": f(rwkv_a1)[0], "a2": f(rwkv_a2)[0],
        "g1": f(rwkv_g1)[0], "g2": f(rwkv_g2)[0], "k_k": f(rwkv_k_k), "k_a": f(rwkv_k_a),
        "r_k": f(rwkv_r_k).reshape(1, D), "gn_w": f(rwkv_gn_w), "gn_b": f(rwkv_gn_b), "w_o": f(rwkv_w_o)[0],
        "w_f": f(fnet_w_o)[0], "ffn_w_in": f(ffn_w_in), "conv_w": f(ffn_conv_w), "conv_b": f(ffn_conv_b),
        "ffn_w_out": f(ffn_w_out),
    }
    shared.update(host_consts(Tp, Ts, dft=("p4" in PHASES)))
    in_maps = []
    for c in range(NCORES):
        m = dict(shared)
        m["x_p"] = x_prompt[c * npc:(c + 1) * npc]
        m["x_s"] = x_sample[c * nsc:(c + 1) * nsc]
        in_maps.append(m)
    res = run_bass_kernel_spmd(nc, in_maps, core_ids=list(range(NCORES)))
    y_p = np.concatenate([np.asarray(r["y_p"], dtype=np.float32) for r in res.results], axis=0)
    y_s = np.concatenate([np.asarray(r["y_s"], dtype=np.float32) for r in res.results], axis=0)
    return (y_p, y_s)
```

```python
import contextlib
import math
import numpy as np
import ml_dtypes
import concourse.bass as bass
import concourse.mybir as mybir
from concourse.bass_utils import run_bass_kernel_spmd

F32 = mybir.dt.float32
BF16 = mybir.dt.bfloat16
AF = mybir.ActivationFunctionType
ALU = mybir.AluOpType
AX = mybir.AxisListType

D = 1024
NH = 16
HS = 64
DFF = 2816
NMETA = 16
CH = 64
LWC = math.exp(-0.5)
NDS = 40


class Buf:
    __slots__ = ("t", "w", "r")

    def __init__(self, t):
        self.t = t
        self.w = None
        self.r = []

    def __getitem__(self, k):
        return self.t[k]


class KB:
    def __init__(self, nc, stack):
        self.nc = nc
        self.names = ["pe", "act", "dve", "pool", "sp"]
        self.q = {e: [] for e in self.names}
        self.count = {e: 0 for e in self.names}
        self.mile = {e: [] for e in self.names}
        self.mileset = {e: set() for e in self.names}
        self.semval = {e: 0 for e in self.names}
        self.milemap = {e: {} for e in self.names}
        self.waited = {e: {} for e in self.names}
        self.flushed = {e: 0 for e in self.names}
        self.milekeys = {e: [] for e in self.names}
        self.esem = {e: stack.enter_context(nc.semaphore("s_" + e)) for e in self.names}
        self.dsem = [stack.enter_context(nc.semaphore("d%d" % i)) for i in range(NDS)]
        self.dval = [0] * NDS
        self.dnext = 0
        self.rr = 0

    def _wait(self, eng, deps):
        wd = self.waited[eng]
        for dep in deps:
            if dep[0] == "eng":
                _, f, idx = dep
                if f == eng and eng == "pe":
                    continue
                if idx <= self.flushed[f] and idx not in self.milemap[f]:
                    idx = min(k for k in self.milekeys[f] if k >= idx)
                if wd.get(("eng", f), 0) >= idx:
                    continue
                wd[("eng", f)] = idx
                if idx not in self.mileset[f] and idx not in self.milemap[f]:
                    self.mileset[f].add(idx)
                self.q[eng].append(("weng", f, idx))
            else:
                _, si, val = dep
                if wd.get(("dma", si), 0) >= val:
                    continue
                wd[("dma", si)] = val
                self.q[eng].append(("wdma", si, val))

    def _deps(self, reads, writes):
        deps = []
        for b in reads:
            if b is not None and b.w is not None:
                deps.append(b.w)
        for b in writes:
            if b is not None:
                if b.w is not None:
                    deps.append(b.w)
                deps.extend(b.r)
        return deps

    def _mark(self, tok, reads, writes):
        for b in reads:
            if b is None:
                continue
            if tok[0] == "eng":
                b.r = [t for t in b.r if not (t[0] == "eng" and t[1] == tok[1])]
            b.r.append(tok)
        for b in writes:
            if b is None:
                continue
            b.w = tok
            b.r = []

    def op(self, eng, fn, reads=(), writes=()):
        self._wait(eng, self._deps(reads, writes))
        self.count[eng] += 1
        idx = self.count[eng]
        tok = ("eng", eng, idx)
        self.q[eng].append(("op", fn, idx))
        self._mark(tok, reads, writes)
        return tok

    def dma(self, eng, out, in_, reads=(), writes=()):
        si = self.dnext
        self.dnext = (self.dnext + 1) % NDS
        deps = self._deps(reads, writes)
        if self.dval[si] > 0:
            deps.append(("dma", si, self.dval[si]))
        self._wait(eng, deps)
        self.dval[si] += 16
        tok = ("dma", si, self.dval[si])
        self.q[eng].append(("dma", out, in_, si))
        self._mark(tok, reads, writes)
        return tok

    def dma_rr(self, out, in_, reads=(), writes=(), cast=False):
        if cast:
            return self.dma("pool", out, in_, reads, writes)
        eng = ("sp", "act")[self.rr % 2]
        self.rr += 1
        return self.dma("sp", out, in_, reads, writes)

    def barrier(self):
        toks = [("eng", e, self.count[e]) for e in self.names if self.count[e] > 0]
        toks += [("dma", si, self.dval[si]) for si in range(NDS) if self.dval[si] > 0]
        for e in self.names:
            self._wait(e, toks)

    def flush(self):
        nc = self.nc
        for e in self.names:
            v = self.semval[e]
            if self.count[e] > self.flushed[e]:
                self.mileset[e].add(self.count[e])
            self.milekeys[e] = self.milekeys[e][-1:] + sorted(self.mileset[e])
            self.flushed[e] = self.count[e]
            for idx in sorted(self.mileset[e]):
                v += 1
                self.milemap[e][idx] = v
            self.semval[e] = v
        q = self.q
        kb = self

        def replay(name, eng):
            for ent in q[name]:
                k = ent[0]
                if k == "weng":
                    eng.wait_ge(kb.esem[ent[1]], kb.milemap[ent[1]][ent[2]])
                elif k == "wdma":
                    eng.wait_ge(kb.dsem[ent[1]], ent[2])
                elif k == "op":
                    ins = ent[1](eng)
                    if ent[2] in kb.mileset[name]:
                        ins.then_inc(kb.esem[name], 1)
                else:
                    eng.dma_start(out=ent[1], in_=ent[2]).then_inc(kb.dsem[ent[3]], 16)

        with nc.allow_non_contiguous_dma(reason="small per-feature vectors"), nc.Block() as block:
            @block.tensor
            def _(eng):
                replay("pe", eng)

            @block.scalar
            def _(eng):
                replay("act", eng)

            @block.vector
            def _(eng):
                replay("dve", eng)

            @block.gpsimd
            def _(eng):
                replay("pool", eng)

            @block.sync
            def _(eng):
                replay("sp", eng)

        self.q = {e: [] for e in self.names}
        self.mileset = {e: set() for e in self.names}


def blocks_of(T, maxn=510):
    nb = -(-T // maxn)
    base, rem = divmod(T, nb)
    out = []
    t = 0
    for i in range(nb):
        n = base + (1 if i < rem else 0)
        out.append((t, n))
        t += n
    return out


def tiles_of(n, p=128):
    return [(i, min(p, n - i)) for i in range(0, n, p)]


class Prog:
    def __init__(self, seqs_p, T_p, seqs_s, T_s, debug=False, phases=("p1", "scan", "p3", "p4")):
        self.np_, self.Tp, self.ns_, self.Ts = seqs_p, T_p, seqs_s, T_s
        self.debug = debug
        self.phases = phases
        self._dbg = set()
        self.seqs = [("p", i, T_p) for i in range(seqs_p)] + [("s", i, T_s) for i in range(seqs_s)]

    def build(self):
        nc = bass.Bass("TRN2", target_bir_lowering=False)
        self.nc = nc
        self.stack = contextlib.ExitStack()
        st = self.stack
        kb = KB(nc, st)
        self.kb = kb
        I = {}

        def din(name, shape, dt=F32):
            I[name] = nc.dram_tensor(name, list(shape), dt, kind="ExternalInput").ap()
            return I[name]

        self.I = I
        din("x_p", [self.np_, self.Tp - NMETA, D])
        din("x_s", [self.ns_, self.Ts - NMETA, D])
        din("meta", [NMETA, D])
        din("norm_mix", [2, D]); din("norm_ffn", [2, D]); din("norm_final", [1, D])
        din("mu", [6, D]); din("w_rkv", [3, D, D])
        din("w0", [2, D]); din("w1", [2, D, 64]); din("w2", [2, 64, D])
        din("a0", [2, D]); din("a1", [2, D, 64]); din("a2", [2, 64, D])
        din("g1", [D, 160]); din("g2", [160, D])
        din("k_k", [1, D]); din("k_a", [1, D]); din("r_k", [1, D])
        din("gn_w", [1, D]); din("gn_b", [1, D]); din("w_o", [D, D])
        din("w_f", [D, D]); din("ffn_w_in", [2, D, 2 * DFF]); din("conv_w", [2, 3, DFF])
        din("conv_b", [2, DFF]); din("ffn_w_out", [2, DFF, D])
        din("c_ident", [128, 128]); din("c_blk", [128, 128]); din("c_hsel", [128, 2])
        din("c_mask", [2, 128, 512]); din("c_maskT", [2, 64, 1024]); din("c_cm", [128, 512]); din("c_eye16", [64, 1024]); din("c_eye2", [128, 128])
        din("c_dftc", [128, 256])
        if "p4" in self.phases:
            din("c_dft_p", [2, self.Tp, self.Tp], BF16)
            din("c_dft_s", [2, self.Ts, self.Ts], BF16)
        okind = "ExternalOutput"
        self.y_p = nc.dram_tensor("y_p", [self.np_, self.Tp - NMETA, D], F32, kind=okind).ap()
        self.y_s = nc.dram_tensor("y_s", [self.ns_, self.Ts - NMETA, D], F32, kind=okind).ap()
        skind = "ExternalOutput" if self.debug else "Internal"
        self.S = []
        for si, (kind, bi, T) in enumerate(self.seqs):
            d = {}
            d["fm"] = nc.dram_tensor("fm%d" % si, [8, D, T], F32, kind=skind).ap()
            d["v"] = nc.dram_tensor("sv%d" % si, [T, D], BF16, kind=skind).ap()
            d["g"] = nc.dram_tensor("sg%d" % si, [T, D], BF16, kind=skind).ap()
            d["bonus"] = nc.dram_tensor("bonus%d" % si, [T, NH], F32, kind=skind).ap()
            d["o"] = nc.dram_tensor("so%d" % si, [2, T, D], F32, kind=skind).ap()
            d["h2"] = nc.dram_tensor("h2_%d" % si, [T, D], F32, kind=skind).ap()
            d["h1"] = nc.dram_tensor("h1_%d" % si, [T, D], F32, kind=skind).ap()
            d["h3"] = nc.dram_tensor("h3_%d" % si, [T, D], F32, kind=skind).ap()
            d["yc"] = nc.dram_tensor("yc%d" % si, [2, T, D], BF16, kind=skind).ap()
            self.S.append(d)

        self.consts()
        if "p1" in self.phases:
            self.phase1()
        if "scan" in self.phases:
            self.phase_scan()
        if "p3" in self.phases:
            self.phase3()
        if "p4" in self.phases:
            self.phase4()
        kb.barrier()
        kb.flush()
        st.close()
        return nc

    def sb(self, stack, name, shape, dt):
        return Buf(stack.enter_context(self.nc.sbuf_tensor("sb_" + name, list(shape), dt)))

    def ps(self, stack, name, shape, dt):
        return Buf(stack.enter_context(self.nc.psum_tensor("ps_" + name, list(shape), dt)))

    def xrows(self, si, t0, n):
        kind, bi, T = self.seqs[si]
        x = self.I["x_p"] if kind == "p" else self.I["x_s"]
        out = []
        if t0 < NMETA:
            m = min(NMETA, t0 + n) - t0
            out.append((self.I["meta"][t0:t0 + m, :], 0, m))
            if n > m:
                out.append((x[bi, 0:n - m, :], m, n - m))
        else:
            out.append((x[bi, t0 - NMETA:t0 - NMETA + n, :], 0, n))
        return out

    def consts(self):
        kb, nc, st = self.kb, self.nc, self.stack
        I = self.I
        C = {}
        self.C = C

        def ld(name, shape, dt, src, cast=False):
            b = self.sb(st, "k_" + name, shape, dt)
            kb.dma_rr(b[:], src, writes=[b], cast=cast)
            C[name] = b
            return b

        ld("ident", [128, 128], BF16, I["c_ident"], cast=True)
        ld("blk", [128, 128], BF16, I["c_blk"], cast=True)
        ld("hsel", [128, 2], BF16, I["c_hsel"], cast=True)
        ld("eye2", [128, 128], BF16, I["c_eye2"], cast=True)
        def colvec(name, src, n):
            b = self.sb(st, "k_" + name, [128, n * 8], F32)
            with nc.allow_non_contiguous_dma(reason="tiny per-feature vectors"):
                for i in range(n):
                    kb.dma("sp", b[:, i * 8:(i + 1) * 8], src[i].rearrange("(c p) -> p c", p=128), writes=[b])
            C[name] = b
        colvec("muT", I["mu"], 6)
        colvec("w0T", I["w0"], 2)
        colvec("a0T", I["a0"], 2)
        colvec("kkT", I["k_k"], 1)
        colvec("kaT", I["k_a"], 1)
        colvec("rkT", I["r_k"], 1)
        b = self.sb(st, "k_omka", [128, 8], F32)
        kb.op("dve", lambda e, b=b: e.tensor_scalar(out=b[:], in0=C["kaT"][:], scalar1=-1.0, scalar2=1.0,
                                                    op0=ALU.mult, op1=ALU.add), reads=[C["kaT"]], writes=[b])
        C["omka"] = b
        self.psb = [self.ps(st, "psb%d" % i, [128, 512], F32) for i in range(8)]
        self.psi = 0

    def dbg(self, name, buf, ap, shape, dt):
        if not self.debug or name in self._dbg:
            return
        self._dbg.add(name)
        d = self.nc.dram_tensor("dbg_" + name, list(shape), dt, kind="ExternalOutput").ap()
        self.kb.dma("sp", d, ap, reads=[buf])

    def rowvec(self, st, name, src):
        b = self.sb(st, "rv_" + name, [128, D], F32)
        self.kb.dma("sp", b[:], src.partition_broadcast(128), writes=[b])
        self.C[name] = b
        return b

    def psum(self):
        b = self.psb[self.psi % 8]
        self.psi += 1
        return b

    def norm_transpose(self, ws, xt, n, grow, hnT, col0, junk, ssb, hnb):
        kb = self.kb
        C = self.C
        kb.op("dve", lambda e: e.scalar_tensor_tensor(out=junk[:n, :], in0=xt[:n, :], scalar=1.0, in1=xt[:n, :],
                                                      op0=ALU.mult, op1=ALU.mult, accum_out=ssb[:n, 0:1]),
              reads=[xt], writes=[junk, ssb])
        kb.op("dve", lambda e: e.tensor_scalar(out=ssb[:n, 1:2], in0=ssb[:n, 0:1], scalar1=1.0 / D, scalar2=1e-6,
                                               op0=ALU.mult, op1=ALU.add), reads=[ssb], writes=[ssb])
        kb.op("act", lambda e: e.activation(out=ssb[:n, 2:3], in_=ssb[:n, 1:2], func=AF.Sqrt), reads=[ssb], writes=[ssb])
        kb.op("dve", lambda e: e.reciprocal(out=ssb[:n, 3:4], in_=ssb[:n, 2:3]), reads=[ssb], writes=[ssb])
        kb.op("dve", lambda e: e.scalar_tensor_tensor(out=hnb[:n, :], in0=xt[:n, :], scalar=ssb[:n, 3:4], in1=grow[:n, :],
                                                      op0=ALU.mult, op1=ALU.mult), reads=[xt, ssb, grow], writes=[hnb])
        self.transpose_into(hnb, n, hnT, col0)

    def norm_transpose_multi(self, items, grow, hnT, ssbs, hnbs):
        kb, C = self.kb, self.C
        for i, (xt, n, c0) in enumerate(items):
            kb.op("dve", lambda e, xt=xt, n=n, i=i: e.scalar_tensor_tensor(out=hnbs[i][:n, :], in0=xt[:n, :], scalar=1.0, in1=xt[:n, :],
                                                                         op0=ALU.mult, op1=ALU.mult, accum_out=ssbs[i][:n, 0:1]),
                  reads=[xt], writes=[hnbs[i], ssbs[i]])
        for i, (xt, n, c0) in enumerate(items):
            kb.op("dve", lambda e, n=n, i=i: e.tensor_scalar(out=ssbs[i][:n, 1:2], in0=ssbs[i][:n, 0:1], scalar1=1.0 / D, scalar2=1e-6,
                                                           op0=ALU.mult, op1=ALU.add), reads=[ssbs[i]], writes=[ssbs[i]])
        for i, (xt, n, c0) in enumerate(items):
            kb.op("act", lambda e, n=n, i=i: e.activation(out=ssbs[i][:n, 2:3], in_=ssbs[i][:n, 1:2], func=AF.Sqrt), reads=[ssbs[i]], writes=[ssbs[i]])
        for i, (xt, n, c0) in enumerate(items):
            kb.op("dve", lambda e, n=n, i=i: e.reciprocal(out=ssbs[i][:n, 3:4], in_=ssbs[i][:n, 2:3]), reads=[ssbs[i]], writes=[ssbs[i]])
        for i, (xt, n, c0) in enumerate(items):
            kb.op("dve", lambda e, xt=xt, n=n, i=i: e.scalar_tensor_tensor(out=hnbs[i][:n, :], in0=xt[:n, :], scalar=ssbs[i][:n, 3:4], in1=grow[:n, :],
                                                                         op0=ALU.mult, op1=ALU.mult), reads=[xt, ssbs[i], grow], writes=[hnbs[i]])
        pts = []
        for i, (xt, n, c0) in enumerate(items):
            pt = self.psum()
            pts.append(pt)
            ptb = pt.t[:].bitcast(BF16)
            for c in range(8):
                kb.op("pe", lambda e, c=c, n=n, i=i, ptb=ptb: e.transpose(out=ptb[:, c * 128:c * 128 + n], in_=hnbs[i][:n, c * 128:(c + 1) * 128],
                                                                        identity=C["ident"][:n, :n]), reads=[hnbs[i], C["ident"]], writes=[pt])
        for i, (xt, n, c0) in enumerate(items):
            src3 = pts[i].t[:].bitcast(BF16).rearrange("p (c t) -> p c t", t=128)
            kb.op("act", lambda e, n=n, c0=c0, src3=src3: e.copy(out=hnT[:, 0:8, c0:c0 + n], in_=src3[:, 0:8, 0:n]), reads=[pts[i]], writes=[hnT])

    def transpose_into(self, src, n, dstT, col0, nchunk=8, eng="act"):
        kb = self.kb
        C = self.C
        pt = self.psum()
        ptb = pt.t[:].bitcast(BF16)
        for c in range(nchunk):
            kb.op("pe", lambda e, c=c: e.transpose(out=ptb[:, c * 128:c * 128 + n], in_=src[:n, c * 128:(c + 1) * 128],
                                                   identity=C["ident"][:n, :n]), reads=[src, C["ident"]], writes=[pt])
        src3 = ptb.rearrange("p (c t) -> p c t", t=128)
        if eng == "act":
            kb.op("act", lambda e: e.copy(out=dstT[:, 0:nchunk, col0:col0 + n], in_=src3[:, 0:nchunk, 0:n]),
                  reads=[pt], writes=[dstT])
        else:
            kb.op("dve", lambda e: e.tensor_copy(out=dstT[:, 0:nchunk, col0:col0 + n], in_=src3[:, 0:nchunk, 0:n]),
                  reads=[pt], writes=[dstT])

    def phase1(self):
        kb, nc, I, C = self.kb, self.nc, self.I, self.C
        with contextlib.ExitStack() as st:
            sb = lambda name, shape, dt: self.sb(st, name, shape, dt)
            self.rowvec(st, "nm0", I["norm_mix"][0:1, :])
            wr = [sb("wrkv%d" % i, [128, 8, D], BF16) for i in range(3)]
            for i in range(3):
                for c in range(8):
                    kb.dma("pool", wr[i][:, c, :], I["w_rkv"][i, c * 128:(c + 1) * 128, :], writes=[wr[i]])
            w1c = sb("w1c", [128, 8, 128], BF16)
            a1c = sb("a1c", [128, 8, 128], BF16)
            for z in range(2):
                kb.dma("pool", w1c[:, :, z * 64:(z + 1) * 64], I["w1"][z].rearrange("(c p) r -> p c r", p=128), writes=[w1c])
                kb.dma("pool", a1c[:, :, z * 64:(z + 1) * 64], I["a1"][z].rearrange("(c p) r -> p c r", p=128), writes=[a1c])
            w2c = sb("w2c", [128, D], BF16)
            a2c = sb("a2c", [128, D], BF16)
            for z in range(2):
                kb.dma("pool", w2c[z * 64:(z + 1) * 64, :], I["w2"][z], writes=[w2c])
                kb.dma("pool", a2c[z * 64:(z + 1) * 64, :], I["a2"][z], writes=[a2c])
            g1c = sb("g1c", [128, 8, 160], BF16)
            kb.dma("pool", g1c[:], I["g1"].rearrange("(c p) r -> p c r", p=128), writes=[g1c])
            g2a = sb("g2a", [128, D], BF16)
            g2b = sb("g2b", [32, D], BF16)
            kb.dma("pool", g2a[:], I["g2"][0:128, :], writes=[g2a])
            kb.dma("pool", g2b[:], I["g2"][128:160, :], writes=[g2b])

            xt = [sb("xt%d" % i, [128, D], F32) for i in range(2)]
            hnbs = [sb("hnb%d" % i, [128, D], BF16) for i in range(2)]
            ssbs = [sb("ssb%d" % i, [128, 4], F32) for i in range(2)]
            hnb = hnbs[0]
            hnT = sb("hnT", [128, 8, 512], BF16)
            tmp = sb("tmp", [128, 8, 512], BF16)
            xx = tmp
            xm = [sb("xm%d" % i, [128, 8, 512], BF16) for i in range(5)]
            tw = sb("tw", [128, 512], BF16)
            ta = sb("ta", [128, 512], BF16)
            sg1 = sb("sg1", [128, 512], BF16)
            sg2 = sb("sg2", [32, 512], BF16)
            soL = [[sb("so%d_%d" % (i, j), [128, 512], F32) for i in range(8)] for j in range(2)]
            kunL = [sb("kun%d" % j, [128, 512], F32) for j in range(2)]
            ksbL = [sb("ksb%d" % j, [128, 512], F32) for j in range(2)]
            sqb = sb("sqb", [128, 512], BF16)
            rnL = [sb("rn0", [128, 512], F32)] * 2
            t1L = [sb("t1_0", [128, 512], F32)] * 2
            av = [sb("av%d" % i, [128, 512], F32) for i in range(2)]
            rkp = sb("rkp", [128, 8, 512], BF16)
            vst = sb("vst", [128, 2, D], BF16)
            gst = sb("gst", [128, 2, D], BF16)
            bst = sb("bst", [128, 4, NH], F32)
            for b_ in (xt[0], xt[1], hnbs[0], hnbs[1]):
                kb.op("pool", lambda e, b_=b_: e.memset(b_[:], 0.0), writes=[b_])

            for si, (kind, bi, T) in enumerate(self.seqs):
                S = self.S[si]
                for (t0, nb) in blocks_of(T):
                    ws, we = t0 - 1, t0 + nb + 1
                    W = we - ws
                    n = nb
                    wtl = tiles_of(W)
                    for p0 in range(0, len(wtl), 2):
                        items = []
                        for ti in range(p0, min(p0 + 2, len(wtl))):
                            o, cnt = wtl[ti]
                            a, b = ws + o, ws + o + cnt
                            a2_, b2_ = max(a, 0), min(b, T)
                            x_ = xt[ti % 2]
                            for (src, ro, nr) in self.xrows(si, a2_, b2_ - a2_):
                                kb.dma("sp" if ti % 2 == 0 else "act", x_[a2_ - a + ro:a2_ - a + ro + nr, :], src, writes=[x_])
                            items.append((x_, cnt, o))
                        self.norm_transpose_multi(items, C["nm0"], hnT, ssbs, hnbs)
                    if ws < 0:
                        kb.op("pool", lambda e: e.memset(hnT[:, :, 0:1], 0.0), writes=[hnT])
                    if we > T:
                        kb.op("pool", lambda e, W=W: e.memset(hnT[:, :, W - 1:W], 0.0), writes=[hnT])
                    kb.op("dve", lambda e, n=n: e.tensor_tensor(out=tmp[:, :, 0:n], in0=hnT[:, :, 0:n], in1=hnT[:, :, 2:n + 2],
                                                                op=ALU.add), reads=[hnT], writes=[tmp])
                    kb.op("dve", lambda e, n=n: e.scalar_tensor_tensor(out=tmp[:, :, 0:n], in0=tmp[:, :, 0:n], scalar=0.5,
                                                                        in1=hnT[:, :, 1:n + 1], op0=ALU.mult, op1=ALU.subtract),
                          reads=[hnT], writes=[tmp])
                    def mix(m, dst):
                        for c in range(8):
                            kb.op("dve", lambda e, m=m, c=c, n=n, dst=dst: e.scalar_tensor_tensor(
                                out=dst[:, c, 0:n], in0=xx[:, c, 0:n], scalar=C["muT"][:, m * 8 + c:m * 8 + c + 1],
                                in1=hnT[:, c, 1:n + 1], op0=ALU.mult, op1=ALU.add), reads=[xx, hnT, C["muT"]], writes=[dst])
                    xw, xa, xg, xr, xk = xm[0], xm[1], xm[2], xm[3], xm[4]
                    xv = xm[0]
                    mix(1, xw); mix(4, xa); mix(5, xg); mix(0, xr); mix(2, xk)
                    p = self.psum()
                    for c in range(8):
                        kb.op("pe", lambda e, c=c, p=p, n=n: e.matmul(p[:, 0:n], lhsT=w1c[:, c, :], rhs=xw[:, c, 0:n],
                                                                      start=(c == 0), stop=(c == 7)), reads=[w1c, xw], writes=[p])
                    kb.op("act", lambda e, p=p, n=n: e.activation(out=tw[:, 0:n], in_=p[:, 0:n], func=AF.Tanh), reads=[p], writes=[tw])
                    p = self.psum()
                    for c in range(8):
                        kb.op("pe", lambda e, c=c, p=p, n=n: e.matmul(p[:, 0:n], lhsT=a1c[:, c, :], rhs=xa[:, c, 0:n],
                                                                      start=(c == 0), stop=(c == 7)), reads=[a1c, xa], writes=[p])
                    kb.op("act", lambda e, p=p, n=n: e.copy(out=ta[:, 0:n], in_=p[:, 0:n]), reads=[p], writes=[ta])
                    p = self.psum()
                    for c in range(8):
                        kb.op("pe", lambda e, c=c, p=p, n=n: e.matmul(p[:, 0:n], lhsT=g1c[:, c, 0:128], rhs=xg[:, c, 0:n],
                                                                      start=(c == 0), stop=(c == 7)), reads=[g1c, xg], writes=[p])
                    kb.op("act", lambda e, p=p, n=n: e.activation(out=sg1[:, 0:n], in_=p[:, 0:n], func=AF.Sigmoid), reads=[p], writes=[sg1])
                    p = self.psum()
                    for c in range(8):
                        kb.op("pe", lambda e, c=c, p=p, n=n: e.matmul(p[0:32, 0:n], lhsT=g1c[:, c, 128:160], rhs=xg[:, c, 0:n],
                                                                      start=(c == 0), stop=(c == 7)), reads=[g1c, xg], writes=[p])
                    kb.op("act", lambda e, p=p, n=n: e.activation(out=sg2[:, 0:n], in_=p[0:32, 0:n], func=AF.Sigmoid), reads=[p], writes=[sg2])
                    mix(3, xv)
                    for s in range(8):
                        so, kun, ksb, rn, t1 = soL[s % 2], kunL[s % 2], ksbL[s % 2], rnL[s % 2], t1L[s % 2]
                        sl = slice(s * 128, (s + 1) * 128)
                        p = self.psum()
                        for c in range(8):
                            kb.op("pe", lambda e, so=so, kun=kun, ksb=ksb, rn=rn, t1=t1, c=c, p=p, n=n, sl=sl: e.matmul(p[:, 0:n], lhsT=wr[0][:, c, sl], rhs=xr[:, c, 0:n],
                                                                                 start=(c == 0), stop=(c == 7)), reads=[wr[0], xr], writes=[p])
                        kb.op("act", lambda e, so=so, kun=kun, ksb=ksb, rn=rn, t1=t1, p=p, n=n, s=s: e.copy(out=so[0][:, 0:n], in_=p[:, 0:n]), reads=[p], writes=[so[0]])
                        p = self.psum()
                        for c in range(8):
                            kb.op("pe", lambda e, so=so, kun=kun, ksb=ksb, rn=rn, t1=t1, c=c, p=p, n=n, sl=sl: e.matmul(p[:, 0:n], lhsT=wr[1][:, c, sl], rhs=xk[:, c, 0:n],
                                                                                 start=(c == 0), stop=(c == 7)), reads=[wr[1], xk], writes=[p])
                        kb.op("act", lambda e, so=so, kun=kun, ksb=ksb, rn=rn, t1=t1, p=p, n=n: e.copy(out=ksb[:, 0:n], in_=p[:, 0:n]), reads=[p], writes=[ksb])
                        kb.op("dve", lambda e, so=so, kun=kun, ksb=ksb, rn=rn, t1=t1, n=n, s=s: e.tensor_scalar(out=kun[:, 0:n], in0=ksb[:, 0:n], scalar1=C["kkT"][:, s:s + 1],
                                                                         scalar2=None, op0=ALU.mult), reads=[ksb, C["kkT"]], writes=[kun])
                        kb.op("pool", lambda e, so=so, kun=kun, ksb=ksb, rn=rn, t1=t1, n=n: e.tensor_tensor(out=sqb[:, 0:n], in0=kun[:, 0:n], in1=kun[:, 0:n], op=ALU.mult),
                              reads=[kun], writes=[sqb])
                        p2 = self.psum()
                        kb.op("pe", lambda e, so=so, kun=kun, ksb=ksb, rn=rn, t1=t1, p2=p2, n=n: e.matmul(p2[:, 0:n], lhsT=C["blk"][:, :], rhs=sqb[:, 0:n], start=True, stop=True),
                              reads=[C["blk"], sqb], writes=[p2])
                        kb.op("act", lambda e, so=so, kun=kun, ksb=ksb, rn=rn, t1=t1, p2=p2, n=n: e.activation(out=rn[:, 0:n], in_=p2[:, 0:n], func=AF.Sqrt), reads=[p2], writes=[rn])
                        kb.op("dve", lambda e, so=so, kun=kun, ksb=ksb, rn=rn, t1=t1, n=n: e.tensor_scalar(out=rn[:, 0:n], in0=rn[:, 0:n], scalar1=1e-12, scalar2=None, op0=ALU.max),
                              reads=[rn], writes=[rn])
                        kb.op("dve", lambda e, so=so, kun=kun, ksb=ksb, rn=rn, t1=t1, n=n: e.reciprocal(out=rn[:, 0:n], in_=rn[:, 0:n]), reads=[rn], writes=[rn])
                        kb.op("dve", lambda e, so=so, kun=kun, ksb=ksb, rn=rn, t1=t1, n=n, s=s: e.tensor_tensor(out=so[1][:, 0:n], in0=kun[:, 0:n], in1=rn[:, 0:n], op=ALU.mult),
                              reads=[kun, rn], writes=[so[1]])
                        for z in range(2):
                            zs = slice(z * 64, (z + 1) * 64)
                            p = self.psum()
                            kb.op("pe", lambda e, so=so, kun=kun, ksb=ksb, rn=rn, t1=t1, p=p, n=n, sl=sl, zs=zs: e.matmul(p[:, 0:n], lhsT=w2c[zs, sl], rhs=tw[zs, 0:n], start=True, stop=True),
                                  reads=[w2c, tw], writes=[p])
                            kb.op("act", lambda e, so=so, kun=kun, ksb=ksb, rn=rn, t1=t1, p=p, n=n, s=s, z=z: e.activation(out=so[6 + z][:, 0:n], in_=p[:, 0:n], func=AF.Sigmoid,
                                                                                    bias=C["w0T"][:, z * 8 + s:z * 8 + s + 1], scale=1.0),
                                  reads=[p, C["w0T"]], writes=[so[6 + z]])
                            p = self.psum()
                            kb.op("pe", lambda e, so=so, kun=kun, ksb=ksb, rn=rn, t1=t1, p=p, n=n, sl=sl, zs=zs: e.matmul(p[:, 0:n], lhsT=a2c[zs, sl], rhs=ta[zs, 0:n], start=True, stop=True),
                                  reads=[a2c, ta], writes=[p])
                            kb.op("act", lambda e, so=so, kun=kun, ksb=ksb, rn=rn, t1=t1, p=p, n=n, s=s, z=z: e.activation(out=av[z][:, 0:n], in_=p[:, 0:n], func=AF.Sigmoid,
                                                                                    bias=C["a0T"][:, z * 8 + s:z * 8 + s + 1], scale=1.0),
                                  reads=[p, C["a0T"]], writes=[av[z]])
                            kb.op("dve", lambda e, so=so, kun=kun, ksb=ksb, rn=rn, t1=t1, n=n, s=s, z=z: e.tensor_scalar(out=t1[:, 0:n], in0=av[z][:, 0:n], scalar1=C["kaT"][:, s:s + 1],
                                                                                  scalar2=C["omka"][:, s:s + 1], op0=ALU.mult, op1=ALU.add),
                                  reads=[av[z], C["kaT"], C["omka"]], writes=[t1])
                            kb.op("pool", lambda e, so=so, kun=kun, ksb=ksb, rn=rn, t1=t1, n=n, s=s, z=z: e.tensor_tensor(out=so[2 + z][:, 0:n], in0=t1[:, 0:n], in1=ksb[:, 0:n], op=ALU.mult),
                                  reads=[t1, ksb], writes=[so[2 + z]])
                            kb.op("pool", lambda e, so=so, kun=kun, ksb=ksb, rn=rn, t1=t1, n=n, s=s, z=z: e.tensor_tensor(out=so[4 + z][:, 0:n], in0=so[1][:, 0:n], in1=av[z][:, 0:n], op=ALU.mult),
                                  reads=[so[1], av[z]], writes=[so[4 + z]])
                        kb.op("dve", lambda e, so=so, kun=kun, ksb=ksb, rn=rn, t1=t1, n=n, s=s: e.tensor_tensor(out=t1[:, 0:n], in0=so[2][:, 0:n], in1=so[3][:, 0:n], op=ALU.add),
                              reads=[so[2], so[3]], writes=[t1])
                        kb.op("dve", lambda e, so=so, kun=kun, ksb=ksb, rn=rn, t1=t1, n=n, s=s: e.scalar_tensor_tensor(out=rkp[:, s, 0:n], in0=t1[:, 0:n], scalar=C["rkT"][:, s:s + 1],
                                                                                in1=so[0][:, 0:n], op0=ALU.mult, op1=ALU.mult),
                              reads=[t1, so[0], C["rkT"]], writes=[rkp])
                        for q in range(8):
                            kb.dma("sp" if q % 2 == 0 else "act", S["fm"][q, s * 128:(s + 1) * 128, t0:t0 + n],
                                   so[q][:, 0:n], reads=[so[q]])
                    tl = tiles_of(n)
                    for ti, (o, cnt) in enumerate(tl):
                        for hf in range(2):
                            hs_ = slice(hf * 512, (hf + 1) * 512)
                            p = self.psum()
                            for c in range(8):
                                kb.op("pe", lambda e, c=c, p=p, o=o, cnt=cnt, hs_=hs_: e.matmul(p[:cnt, :], lhsT=xv[:, c, o:o + cnt], rhs=wr[2][:, c, hs_],
                                                                                                start=(c == 0), stop=(c == 7)), reads=[xv, wr[2]], writes=[p])
                            kb.op("act", lambda e, p=p, cnt=cnt, ti=ti, hs_=hs_: e.copy(out=vst[:cnt, ti % 2, hs_], in_=p[:cnt, :]), reads=[p], writes=[vst])
                            p = self.psum()
                            kb.op("pe", lambda e, p=p, o=o, cnt=cnt, hs_=hs_: e.matmul(p[:cnt, :], lhsT=sg1[:, o:o + cnt], rhs=g2a[:, hs_], start=True, stop=False),
                                  reads=[sg1, g2a], writes=[p])
                            kb.op("pe", lambda e, p=p, o=o, cnt=cnt, hs_=hs_: e.matmul(p[:cnt, :], lhsT=sg2[:, o:o + cnt], rhs=g2b[:, hs_], start=False, stop=True),
                                  reads=[sg2, g2b], writes=[p])
                            kb.op("dve", lambda e, p=p, cnt=cnt, ti=ti, hs_=hs_: e.tensor_copy(out=gst[:cnt, ti % 2, hs_], in_=p[:cnt, :]), reads=[p], writes=[gst])
                        p = self.psum()
                        for s in range(8):
                            kb.op("pe", lambda e, p=p, s=s, o=o, cnt=cnt: e.matmul(p[:cnt, 2 * s:2 * s + 2], lhsT=rkp[:, s, o:o + cnt], rhs=C["hsel"][:, :],
                                                                                   start=True, stop=True), reads=[rkp, C["hsel"]], writes=[p])
                        kb.op("dve", lambda e, p=p, cnt=cnt, ti=ti: e.tensor_copy(out=bst[:cnt, ti, :], in_=p[:cnt, 0:NH]), reads=[p], writes=[bst])
                        kb.dma("sp", S["v"][t0 + o:t0 + o + cnt, :], vst[:cnt, ti % 2, :], reads=[vst])
                        kb.dma("act", S["g"][t0 + o:t0 + o + cnt, :], gst[:cnt, ti % 2, :], reads=[gst])
                        kb.dma("sp", S["bonus"][t0 + o:t0 + o + cnt, :], bst[:cnt, ti, :], reads=[bst])
            kb.barrier()

    def phase_scan(self):
        kb, nc, I, C = self.kb, self.nc, self.I, self.C
        with contextlib.ExitStack() as st:
            sb = lambda name, shape, dt: self.sb(st, name, shape, dt)

            def ld(name, shape, dt, src, cast=False):
                b = sb("k_" + name, shape, dt)
                kb.dma("pool" if cast else "sp", b[:], src, writes=[b])
                C[name] = b
            ld("mask0", [128, 512], F32, I["c_mask"][0])
            ld("mask1", [128, 512], F32, I["c_mask"][1])
            ld("maskT0", [64, 1024], F32, I["c_maskT"][0])
            ld("maskT1", [64, 1024], F32, I["c_maskT"][1])
            ld("eye16", [64, 1024], BF16, I["c_eye16"], cast=True)
            ld("cm", [128, 512], F32, I["c_cm"])

            def mkstream(tag):
                B = {}
                B["BK"] = sb(tag + "BK", [128, 8, 128], BF16)
                B["AR"] = sb(tag + "AR", [128, 8, 128], BF16)
                B["BKp"] = [sb(tag + "BKp%d" % i, [128, 8, 128], BF16) for i in range(2)]
                B["ARp"] = [sb(tag + "ARp%d" % i, [128, 8, 128], BF16) for i in range(2)]
                B["vz"] = sb(tag + "vz", [128, 1024], BF16)
                for t_ in B["BKp"] + B["ARp"] + [B["vz"]]:
                    kb.op("pool", lambda e, t_=t_: e.memset(t_[:], 0.0), writes=[t_])
                B["BKT"] = sb(tag + "BKT", [128, 8, 128], BF16)
                B["M"] = sb(tag + "M", [128, 16, 128], BF16)
                B["A"] = sb(tag + "A", [64, 16, 64], BF16)
                B["AT"] = sb(tag + "AT", [64, 16, 64], BF16)
                B["X"] = sb(tag + "X", [64, 16, 64], BF16)
                B["Z"] = sb(tag + "Z", [64, 1024], BF16)
                B["vu"] = sb(tag + "vu", [128, 1024], BF16)
                B["o"] = sb(tag + "o", [64, 1024], F32)
                B["ST"] = sb(tag + "ST", [128, 8, 64], F32)
                B["STb"] = sb(tag + "STb", [128, 8, 64], BF16)
                B["PC"] = sb(tag + "PC", [128, 8], F32)
                return B

            def evac(i, fn_act, fn_dve, reads, writes):
                if i % 2 == 0:
                    kb.op("act", fn_act, reads=reads, writes=writes)
                else:
                    kb.op("dve", fn_dve, reads=reads, writes=writes)

            import os
            STOP = int(os.environ.get("SCAN_STOP", "9"))
            SUB = int(os.environ.get("SCAN_SUB", "9"))
            HACK = int(os.environ.get("HACK", "0"))

            free_banks = list(self.psb)

            def galloc(n):
                while len(free_banks) < n:
                    yield
                return [free_banks.pop(0) for _ in range(n)]

            def release(bs):
                free_banks.extend(bs)

            def stream(B, si, z):
                S = self.S[si]
                T = self.seqs[si][2]
                nch = -(-T // CH)
                mask = C["mask%d" % z]
                maskT = C["maskT%d" % z]
                kb.op("pool", lambda e: e.memset(B["ST"][:], 0.0), writes=[B["ST"]])
                kb.op("pool", lambda e: e.memset(B["STb"][:], 0.0), writes=[B["STb"]])
                order = range(nch) if z == 0 else range(nch - 1, -1, -1)
                for c in order:
                    c0 = c * CH
                    nt = min(CH, T - c0)
                    while not free_sets:
                        yield
                    Tt = free_sets.pop(0)
                    srcs = (("r", 0), ("kk", 1), ("k", 2 + z), ("b", 4 + z), ("sg", 6 + z))
                    for qi, (nm, q) in enumerate(srcs):
                        if nt < CH:
                            kb.op("pool", lambda e, nm=nm, Tt=Tt: e.memset(Tt[nm][:], 0.0), writes=[Tt[nm]])
                        kb.dma("sp" if qi % 2 == 0 else "act", Tt[nm][:, :, 0:nt],
                               S["fm"][q].rearrange("(s p) t -> p s t", p=128)[:, :, c0:c0 + nt], writes=[Tt[nm]])
                    if nt < CH:
                        kb.op("pool", lambda e: e.memset(B["vu"][64:128, :], 0.0), writes=[B["vu"]])
                    kb.dma("sp", B["vu"][64:64 + nt, :], S["v"][c0:c0 + nt, :], writes=[B["vu"]])
                    if nt < CH:
                        kb.op("pool", lambda e: e.memset(B["vz"][64:128, :], 0.0), writes=[B["vz"]])
                    kb.dma("act", B["vz"][64:64 + nt, :], S["v"][c0:c0 + nt, :], writes=[B["vz"]])
                    yield
                    cum2 = Tt["cum"][:].rearrange("p s t -> p (s t)")
                    sg2 = Tt["sg"][:].rearrange("p s t -> p (s t)")
                    kb.op("dve", lambda e, cum2=cum2, sg2=sg2: e.tensor_tensor_scan(out=cum2, data0=C["cm"][:, :], data1=sg2, initial=0.0,
                                                                op0=ALU.mult, op1=ALU.add), reads=[Tt["sg"], C["cm"]], writes=[Tt["cum"]])
                    if z == 0:
                        E1 = Tt["cum"]
                        last = CH - 1
                    else:
                        E1 = Tt["e1"]
                        last = 0
                        kb.op("pool", lambda e, Tt=Tt: e.tensor_tensor(out=Tt["e1"][:], in0=Tt["sg"][:], in1=Tt["cum"][:], op=ALU.subtract),
                              reads=[Tt["sg"], Tt["cum"]], writes=[Tt["e1"]])
                        for s_ in range(8):
                            kb.op("dve", lambda e, s_=s_, Tt=Tt: e.tensor_scalar(out=Tt["e1"][:, s_, :], in0=Tt["e1"][:, s_, :],
                                                                         scalar1=Tt["cum"][:, s_, CH - 1:CH], scalar2=None, op0=ALU.add),
                                  reads=[Tt["e1"], Tt["cum"]], writes=[Tt["e1"]])
                    kb.op("act", lambda e, Tt=Tt, E1=E1: e.activation(out=Tt["eP"][:], in_=E1[:], func=AF.Exp, scale=-LWC), reads=[E1], writes=[Tt["eP"]])
                    kb.op("act", lambda e, Tt=Tt, E1=E1: e.activation(out=Tt["eN"][:], in_=E1[:], func=AF.Exp, scale=LWC), reads=[E1], writes=[Tt["eN"]])
                    kb.op("pool", lambda e, Tt=Tt, E1=E1: e.tensor_tensor(out=Tt["eA"][:], in0=E1[:], in1=Tt["sg"][:], op=ALU.subtract),
                          reads=[E1, Tt["sg"]], writes=[Tt["eA"]])
                    kb.op("act", lambda e, Tt=Tt: e.activation(out=Tt["eA"][:], in_=Tt["eA"][:], func=AF.Exp, scale=-LWC), reads=[Tt["eA"]], writes=[Tt["eA"]])
                    kb.op("dve", lambda e, Tt=Tt: e.scalar_tensor_tensor(out=B["AR"][:, :, 0:64], in0=Tt["kk"][:], scalar=-1.0, in1=Tt["eA"][:],
                                                                  op0=ALU.mult, op1=ALU.mult), reads=[Tt["kk"], Tt["eA"]], writes=[B["AR"]])
                    kb.op("pool", lambda e, Tt=Tt: e.tensor_tensor(out=B["AR"][:, :, 64:128], in0=Tt["r"][:], in1=Tt["eP"][:], op=ALU.mult),
                          reads=[Tt["r"], Tt["eP"]], writes=[B["AR"]])
                    kb.op("dve", lambda e, Tt=Tt: e.tensor_tensor(out=B["BK"][:, :, 0:64], in0=Tt["b"][:], in1=Tt["eN"][:], op=ALU.mult),
                          reads=[Tt["b"], Tt["eN"]], writes=[B["BK"]])
                    kb.op("pool", lambda e, Tt=Tt: e.tensor_tensor(out=B["BK"][:, :, 64:128], in0=Tt["k"][:], in1=Tt["eN"][:], op=ALU.mult),
                          reads=[Tt["k"], Tt["eN"]], writes=[B["BK"]])
                    kb.op("dve", lambda e, Tt=Tt, last=last: e.tensor_copy(out=B["PC"][:, :], in_=Tt["eP"][:, :, last]), reads=[Tt["eP"]], writes=[B["PC"]])
                    for hp_ in range(2):
                        pq = slice(hp_ * 64, hp_ * 64 + 64)
                        kb.op("pool", lambda e, hp_=hp_, pq=pq: e.tensor_copy(out=B["BKp"][hp_][pq, :, :], in_=B["BK"][pq, :, :]), reads=[B["BK"]], writes=[B["BKp"][hp_]])
                        kb.op("act", lambda e, hp_=hp_, pq=pq: e.copy(out=B["ARp"][hp_][pq, :, :], in_=B["AR"][pq, :, :]), reads=[B["AR"]], writes=[B["ARp"][hp_]])
                    free_sets.append(Tt)
                    yield
                    if STOP <= 1:
                        continue
                    pall = yield from galloc(6)
                    pms = pall[0:4]
                    for h in range(16):
                        s_, hp = h // 2, h % 2
                        pm = pms[h // 4]
                        kb.op("pe", lambda e, pm=pm, h=h, s_=s_, hp=hp: e.matmul(pm[:, (h % 4) * 128:(h % 4 + 1) * 128], lhsT=B["BKp"][hp][:, s_, :],
                                                                               rhs=B["AR"][:, s_, :], start=True, stop=True),
                              reads=[B["BKp"][hp], B["AR"]], writes=[pm])
                    pts = pall[4:6]
                    for h in range(16):
                        s_, hp = h // 2, h % 2
                        pm = pts[h // 8]
                        kb.op("pe", lambda e, pm=pm, h=h, s_=s_, hp=hp: e.matmul(pm[0:64, (h % 8) * 64:(h % 8 + 1) * 64], lhsT=B["ARp"][hp][:, s_, 0:64],
                                                                               rhs=B["BK"][:, s_, 0:64], start=True, stop=True),
                              reads=[B["BK"], B["ARp"][hp]], writes=[pm])
                    yield
                    for g in range(4):
                        kb.op("dve", lambda e, g=g, pms=pms: e.tensor_tensor(out=B["M"][:, g * 4:(g + 1) * 4, :].rearrange("p h t -> p (h t)"),
                                                                         in0=pms[g][:, :], in1=mask[:, :], op=ALU.mult), reads=[pms[g], mask], writes=[B["M"]])
                    for g in range(2):
                        kb.op("dve", lambda e, g=g, pts=pts: e.tensor_tensor(out=B["AT"][:, g * 8:(g + 1) * 8, :].rearrange("p h t -> p (h t)"),
                                                                           in0=pts[g][0:64, :], in1=maskT[:, g * 512:(g + 1) * 512], op=ALU.mult),
                              reads=[pts[g], maskT], writes=[B["AT"]])
                    release(pall)
                    ptt = yield from galloc(2)
                    for s_ in range(8):
                        kb.op("pe", lambda e, s_=s_, ptt=ptt: e.matmul(ptt[s_ // 4][:, (s_ % 4) * 128:(s_ % 4 + 1) * 128], lhsT=B["BK"][:, s_, :], rhs=C["ident"][:, :],
                                                                       start=True, stop=True), reads=[B["BK"], C["ident"]], writes=[ptt[s_ // 4]])
                    yield
                    kb.op("act", lambda e: e.copy(out=B["A"][:], in_=B["M"][0:64, :, 0:64]), reads=[B["M"]], writes=[B["A"]])
                    for g in range(2):
                        kb.op("act", lambda e, g=g, ptt=ptt: e.copy(out=B["BKT"][:, g * 4:(g + 1) * 4, :].rearrange("p s t -> p (s t)"), in_=ptt[g][:, :]),
                              reads=[ptt[g]], writes=[B["BKT"]])
                    release(ptt)
                    yield
                    kb.op("pool", lambda e: e.tensor_tensor(out=B["X"][:].rearrange("p h t -> p (h t)"), in0=B["A"][:].rearrange("p h t -> p (h t)"),
                                                            in1=C["eye16"][:, :], op=ALU.add), reads=[B["A"], C["eye16"]], writes=[B["X"]])
                    for lvl in range(5):
                        lastl = (lvl == 4)
                        pab = yield from galloc(2 if lastl else 4)
                        pa = pab[0:2]
                        for h in range(16):
                            kb.op("pe", lambda e, h=h, pa=pa: e.matmul(pa[h // 8][0:64, (h % 8) * 64:(h % 8 + 1) * 64], lhsT=B["A"][:, h, :], rhs=B["AT"][:, h, :],
                                                                       start=True, stop=True), reads=[B["A"], B["AT"]], writes=[pa[h // 8]])
                        if not lastl:
                            pb = pab[2:4]
                            for h in range(16):
                                kb.op("pe", lambda e, h=h, pb=pb: e.matmul(pb[h // 8][0:64, (h % 8) * 64:(h % 8 + 1) * 64], lhsT=B["AT"][:, h, :], rhs=B["A"][:, h, :],
                                                                           start=True, stop=True), reads=[B["A"], B["AT"]], writes=[pb[h // 8]])
                        yield
                        for g in range(2):
                            kb.op("act", lambda e, g=g, pa=pa: e.copy(out=B["AT"][:, g * 8:(g + 1) * 8, :].rearrange("p h t -> p (h t)"), in_=pa[g][0:64, :]),
                                  reads=[pa[g]], writes=[B["AT"]])
                        if not lastl:
                            for g in range(2):
                                kb.op("dve", lambda e, g=g, pb=pb: e.tensor_copy(out=B["A"][:, g * 8:(g + 1) * 8, :].rearrange("p h t -> p (h t)"), in_=pb[g][0:64, :]),
                                      reads=[pb[g]], writes=[B["A"]])
                        release(pab)
                        yield
                        px = yield from galloc(2)
                        for h in range(16):
                            kb.op("pe", lambda e, h=h, px=px: e.matmul(px[h // 8][0:64, (h % 8) * 64:(h % 8 + 1) * 64], lhsT=B["AT"][:, h, :], rhs=B["X"][:, h, :],
                                                                       start=True, stop=True), reads=[B["AT"], B["X"]], writes=[px[h // 8]])
                        yield
                        for g in range(2):
                            kb.op("dve", lambda e, g=g, px=px: e.tensor_tensor(out=B["X"][:, g * 8:(g + 1) * 8, :].rearrange("p h t -> p (h t)"),
                                                                               in0=px[g][0:64, :], in1=B["X"][:, g * 8:(g + 1) * 8, :].rearrange("p h t -> p (h t)"),
                                                                               op=ALU.add), reads=[px[g], B["X"]], writes=[B["X"]])
                        release(px)
                        yield
                    pz = yield from galloc(2)
                    for h in range(16):
                        s_, hp = h // 2, h % 2
                        oz = pz[h // 8][0:64, (h % 8) * 64:(h % 8 + 1) * 64]
                        kb.op("pe", lambda e, oz=oz, s_=s_, hp=hp: e.matmul(oz, lhsT=B["ARp"][hp][:, s_, 0:64], rhs=B["STb"][:, s_, :], start=True, stop=False),
                              reads=[B["ARp"][hp], B["STb"]], writes=[pz[h // 8]])
                        kb.op("pe", lambda e, oz=oz, h=h: e.matmul(oz, lhsT=B["M"][:, h, 0:64], rhs=B["vz"][:, h * 64:(h + 1) * 64], start=False, stop=True),
                              reads=[B["M"], B["vz"]], writes=[pz[h // 8]])
                    yield
                    for g in range(2):
                        evac(g, lambda e, g=g, pz=pz: e.copy(out=B["Z"][:, g * 512:(g + 1) * 512], in_=pz[g][0:64, :]),
                             lambda e, g=g, pz=pz: e.tensor_copy(out=B["Z"][:, g * 512:(g + 1) * 512], in_=pz[g][0:64, :]), [pz[g]], [B["Z"]])
                    release(pz)
                    yield
                    pu = yield from galloc(2)
                    for h in range(16):
                        kb.op("pe", lambda e, h=h, pu=pu: e.matmul(pu[h // 8][0:64, (h % 8) * 64:(h % 8 + 1) * 64], lhsT=B["X"][:, h, :], rhs=B["Z"][:, h * 64:(h + 1) * 64],
                                                            start=True, stop=True), reads=[B["X"], B["Z"]], writes=[pu[h // 8]])
                    yield
                    for g in range(2):
                        evac(g, lambda e, g=g, pu=pu: e.copy(out=B["vu"][0:64, g * 512:(g + 1) * 512], in_=pu[g][0:64, :]),
                             lambda e, g=g, pu=pu: e.tensor_copy(out=B["vu"][0:64, g * 512:(g + 1) * 512], in_=pu[g][0:64, :]), [pu[g]], [B["vu"]])
                    release(pu)
                    yield
                    pod = yield from galloc(4)
                    po = pod[0:2]
                    for h in range(16):
                        s_, hp = h // 2, h % 2
                        oo = po[h // 8][0:64, (h % 8) * 64:(h % 8 + 1) * 64]
                        kb.op("pe", lambda e, oo=oo, s_=s_, hp=hp: e.matmul(oo, lhsT=B["ARp"][hp][:, s_, 64:128], rhs=B["STb"][:, s_, :], start=True, stop=False),
                              reads=[B["ARp"][hp], B["STb"]], writes=[po[h // 8]])
                        kb.op("pe", lambda e, oo=oo, h=h: e.matmul(oo, lhsT=B["M"][:, h, 64:128], rhs=B["vu"][:, h * 64:(h + 1) * 64], start=False, stop=True),
                              reads=[B["M"], B["vu"]], writes=[po[h // 8]])
                    pd = pod[2:4]
                    for s_ in range(8):
                        kb.op("pe", lambda e, s_=s_, pd=pd: e.matmul(pd[s_ // 4][:, (s_ % 4) * 128:(s_ % 4 + 1) * 128], lhsT=B["BKT"][:, s_, :], rhs=B["vu"][:, s_ * 128:(s_ + 1) * 128],
                                                              start=True, stop=True), reads=[B["BKT"], B["vu"]], writes=[pd[s_ // 4]])
                    yield
                    for g in range(2):
                        kb.op("act", lambda e, g=g, po=po: e.copy(out=B["o"][:, g * 512:(g + 1) * 512], in_=po[g][0:64, :]), reads=[po[g]], writes=[B["o"]])
                    kb.dma("sp", S["o"][z, c0:c0 + nt, :], B["o"][0:nt, :], reads=[B["o"]])
                    for g in range(2):
                        pv = pd[g][:, :].rearrange("p (s t) -> p s t", t=128)
                        for hp in range(2):
                            ps_ = slice(hp * 64, hp * 64 + 64)
                            kb.op("dve", lambda e, g=g, pv=pv, hp=hp, ps_=ps_: e.tensor_tensor(out=B["ST"][ps_, g * 4:(g + 1) * 4, :], in0=pv[ps_, :, hp * 64:(hp + 1) * 64],
                                                                                             in1=B["ST"][ps_, g * 4:(g + 1) * 4, :], op=ALU.add),
                                  reads=[pd[g], B["ST"]], writes=[B["ST"]])
                    release(pod)
                    yield
                    for s_ in range(8):
                        kb.op("pool" if s_ % 2 else "dve", lambda e, s_=s_: e.tensor_scalar(out=B["ST"][:, s_, :], in0=B["ST"][:, s_, :], scalar1=B["PC"][:, s_:s_ + 1], scalar2=None, op0=ALU.mult),
                              reads=[B["ST"], B["PC"]], writes=[B["ST"]])
                    yield
                    kb.op("act", lambda e: e.copy(out=B["STb"][:], in_=B["ST"][:]), reads=[B["ST"]], writes=[B["STb"]])
                    yield

            NS = 4
            Bs = [mkstream("s%d_" % i) for i in range(NS)]
            free_sets = []
            for i in range(2):
                free_sets.append({nm: sb("t%d_%s" % (i, nm), [128, 8, 64], F32)
                                  for nm in ("r", "kk", "k", "b", "sg", "cum", "e1", "eP", "eN", "eA")})
            todo = [(si, z) for si in range(len(self.seqs)) for z in range(2)]
            active = [None] * NS
            while todo or any(a is not None for a in active):
                for i in range(NS):
                    if active[i] is None and todo:
                        si, z = todo.pop(0)
                        active[i] = stream(Bs[i], si, z)
                    if active[i] is not None:
                        try:
                            next(active[i])
                        except StopIteration:
                            active[i] = None
            kb.barrier()

    def phase3(self):
        self.phase3a()
        self.ffn_pass(0)

    def phase4(self):
        self.phase4a()
        self.ffn_pass(1)

    def phase3a(self):
        kb, nc, I, C = self.kb, self.nc, self.I, self.C
        with contextlib.ExitStack() as st:
            sb = lambda name, shape, dt: self.sb(st, "p3_" + name, shape, dt)
            gnw = self.rowvec(st, "gnw", I["gn_w"][0:1, :])
            gnb = self.rowvec(st, "gnb", I["gn_b"][0:1, :])
            wo = sb("wo", [128, 8, D], BF16)
            for c in range(8):
                kb.dma("pool", wo[:, c, :], I["w_o"][c * 128:(c + 1) * 128, :], writes=[wo])
            NB = 4
            sets = []
            for i in range(NB):
                sets.append(dict(a=sb("o0_%d" % i, [128, D], F32), b=sb("o1_%d" % i, [128, D], F32), v=sb("v_%d" % i, [128, D], BF16),
                                 g=sb("g_%d" % i, [128, D], BF16), bn=sb("b_%d" % i, [128, NH], F32), x=sb("x_%d" % i, [128, D], F32),
                                 s=sb("st_%d" % i, [128, 6, NH], F32), h=sb("h1_%d" % i, [128, D], F32),
                                 ogb=sb("ogb%d" % i, [128, D], BF16), ogT=sb("ogT%d" % i, [128, 8, 128], BF16)))
                kb.op("pool", lambda e, t_=sets[i]["ogb"]: e.memset(t_[:], 0.0), writes=[sets[i]["ogb"]])
            free_banks = list(self.psb)

            def galloc(nb_):
                while len(free_banks) < nb_:
                    yield
                return [free_banks.pop(0) for _ in range(nb_)]

            def tile_gen(Q, si, t0, n):
                S = self.S[si]
                a, b_, v_, g_, bn, x_, s_, h_, ogb, ogT = Q["a"], Q["b"], Q["v"], Q["g"], Q["bn"], Q["x"], Q["s"], Q["h"], Q["ogb"], Q["ogT"]
                kb.dma("sp", a[:n, :], S["o"][0, t0:t0 + n, :], writes=[a])
                kb.dma("act", b_[:n, :], S["o"][1, t0:t0 + n, :], writes=[b_])
                kb.dma("sp", v_[:n, :], S["v"][t0:t0 + n, :], writes=[v_])
                kb.dma("act", g_[:n, :], S["g"][t0:t0 + n, :], writes=[g_])
                kb.dma("sp", bn[:n, :], S["bonus"][t0:t0 + n, :], writes=[bn])
                for (src, ro, nr) in self.xrows(si, t0, n):
                    kb.dma("act", x_[ro:ro + nr, :], src, writes=[x_])
                yield
                kb.op("pool", lambda e: e.tensor_tensor(out=a[:n, :], in0=a[:n, :], in1=b_[:n, :], op=ALU.add), reads=[b_], writes=[a])
                kb.op("pool", lambda e: e.tensor_tensor(out=b_[:n, :], in0=a[:n, :], in1=a[:n, :], op=ALU.mult), reads=[a], writes=[b_])
                yield
                a3 = a[:n, :].rearrange("p (h j) -> p h j", j=HS)
                b3 = b_[:n, :].rearrange("p (h j) -> p h j", j=HS)
                kb.op("dve", lambda e: e.reduce_sum(out=s_[:n, 0, :], in_=a3, axis=AX.X), reads=[a], writes=[s_])
                kb.op("dve", lambda e: e.reduce_sum(out=s_[:n, 1, :], in_=b3, axis=AX.X), reads=[b_], writes=[s_])
                kb.op("dve", lambda e: e.tensor_scalar(out=s_[:n, 2, :], in0=s_[:n, 0, :], scalar1=1.0 / HS, scalar2=None, op0=ALU.mult), reads=[s_], writes=[s_])
                kb.op("dve", lambda e: e.tensor_tensor(out=s_[:n, 3, :], in0=s_[:n, 2, :], in1=s_[:n, 2, :], op=ALU.mult), reads=[s_], writes=[s_])
                kb.op("dve", lambda e: e.scalar_tensor_tensor(out=s_[:n, 4, :], in0=s_[:n, 1, :], scalar=1.0 / HS, in1=s_[:n, 3, :], op0=ALU.mult, op1=ALU.subtract), reads=[s_], writes=[s_])
                kb.op("dve", lambda e: e.tensor_scalar(out=s_[:n, 4, :], in0=s_[:n, 4, :], scalar1=64e-5, scalar2=None, op0=ALU.add), reads=[s_], writes=[s_])
                yield
                kb.op("act", lambda e: e.activation(out=s_[:n, 5, :], in_=s_[:n, 4, :], func=AF.Sqrt), reads=[s_], writes=[s_])
                yield
                kb.op("dve", lambda e: e.reciprocal(out=s_[:n, 5, :], in_=s_[:n, 5, :]), reads=[s_], writes=[s_])
                for h in range(NH):
                    hs_ = slice(h * HS, (h + 1) * HS)
                    kb.op("dve" if h % 2 == 0 else "pool", lambda e, h=h, hs_=hs_: e.tensor_scalar(
                        out=a[:n, hs_], in0=a[:n, hs_], scalar1=s_[:n, 2, h:h + 1], scalar2=s_[:n, 5, h:h + 1],
                        op0=ALU.subtract, op1=ALU.mult), reads=[a, s_], writes=[a])
                yield
                kb.op("pool", lambda e: e.tensor_tensor(out=a[:n, :], in0=a[:n, :], in1=gnw[:n, :], op=ALU.mult), reads=[gnw], writes=[a])
                kb.op("pool", lambda e: e.tensor_tensor(out=a[:n, :], in0=a[:n, :], in1=gnb[:n, :], op=ALU.add), reads=[gnb], writes=[a])
                yield
                for h in range(NH):
                    hs_ = slice(h * HS, (h + 1) * HS)
                    kb.op("dve", lambda e, h=h, hs_=hs_: e.scalar_tensor_tensor(
                        out=a[:n, hs_], in0=v_[:n, hs_], scalar=bn[:n, h:h + 1], in1=a[:n, hs_], op0=ALU.mult, op1=ALU.add),
                        reads=[v_, bn], writes=[a])
                yield
                kb.op("pool", lambda e: e.tensor_tensor(out=ogb[:n, :], in0=a[:n, :], in1=g_[:n, :], op=ALU.mult), reads=[a, g_], writes=[ogb])
                yield
                (pt,) = yield from galloc(1)
                ptb = pt.t[:].bitcast(BF16)
                for c in range(8):
                    kb.op("pe", lambda e, c=c: e.transpose(out=ptb[:, c * 128:c * 128 + n], in_=ogb[:n, c * 128:(c + 1) * 128],
                                                           identity=C["ident"][:n, :n]), reads=[ogb, C["ident"]], writes=[pt])
                yield
                src3 = ptb.rearrange("p (c t) -> p c t", t=128)
                kb.op("act", lambda e: e.copy(out=ogT[:, 0:8, 0:n], in_=src3[:, 0:8, 0:n]), reads=[pt], writes=[ogT])
                free_banks.append(pt)
                yield
                pp = yield from galloc(2)
                for hf in range(2):
                    hs_ = slice(hf * 512, (hf + 1) * 512)
                    for c in range(8):
                        kb.op("pe", lambda e, hf=hf, c=c, hs_=hs_: e.matmul(pp[hf][:n, :], lhsT=ogT[:, c, 0:n], rhs=wo[:, c, hs_], start=(c == 0), stop=(c == 7)),
                              reads=[ogT, wo], writes=[pp[hf]])
                yield
                for hf in range(2):
                    hs_ = slice(hf * 512, (hf + 1) * 512)
                    kb.op("dve", lambda e, hf=hf, hs_=hs_: e.tensor_tensor(out=h_[:n, hs_], in0=pp[hf][:n, :], in1=x_[:n, hs_], op=ALU.add),
                          reads=[pp[hf], x_], writes=[h_])
                free_banks.extend(pp)
                kb.dma("sp", S["h1"][t0:t0 + n, :], h_[:n, :], reads=[h_])
                yield

            todo = [(si, t0, n) for si, (kind, bi, T) in enumerate(self.seqs) for (t0, n) in tiles_of(T)]
            active = [None] * NB
            while todo or any(x is not None for x in active):
                for i in range(NB):
                    if active[i] is None and todo:
                        si, t0, n = todo.pop(0)
                        active[i] = tile_gen(sets[i], si, t0, n)
                    if active[i] is not None:
                        try:
                            next(active[i])
                        except StopIteration:
                            active[i] = None
            kb.barrier()

    def ffn_pass(self, l):
        kb, nc, I, C = self.kb, self.nc, self.I, self.C
        NF = DFF // 128
        with contextlib.ExitStack() as st:
            sb = lambda name, shape, dt: self.sb(st, "f%d_" % l + name, shape, dt)
            nf = self.rowvec(st, "nf%d" % l, I["norm_ffn"][l:l + 1, :])
            if l == 0:
                n2 = self.rowvec(st, "nm1", I["norm_mix"][1:2, :])
                dftc = sb("dftc", [128, 256], BF16)
                kb.dma("pool", dftc[:], I["c_dftc"], writes=[dftc])
            else:
                n2 = self.rowvec(st, "nfin", I["norm_final"][0:1, :])
            win = sb("win", [128, 8, 2 * DFF], BF16)
            for c in range(8):
                kb.dma("pool", win[:, c, :], I["ffn_w_in"][l, c * 128:(c + 1) * 128, :], writes=[win])
            wout = sb("wout", [128, NF, D], BF16)
            for f in range(NF):
                kb.dma("pool", wout[:, f, :], I["ffn_w_out"][l, f * 128:(f + 1) * 128, :], writes=[wout])
            cw = sb("cw", [128, 3 * NF], F32)
            cb = sb("cb", [128, NF], F32)
            for k in range(3):
                kb.dma("sp", cw[:, k * NF:(k + 1) * NF], I["conv_w"][l, k].rearrange("(f p) -> p f", p=128), writes=[cw])
            kb.dma("sp", cb[:, :], I["conv_b"][l].rearrange("(f p) -> p f", p=128), writes=[cb])
            ht = [sb("ht%d" % i, [128, D], F32) for i in range(4)]
            ssb = sb("ssb", [128, 4], F32)
            hnb = sb("hnb", [128, D], BF16)
            junk = hnb
            hnT = sb("hnT", [128, 8, 512], BF16)
            uaL = [sb("ua%d" % i, [128, 512], F32) for i in range(2)]
            tmpL = [sb("tmp%d" % i, [128, 512], F32) for i in range(2)]
            slL = [sb("sl%d" % i, [128, 512], F32) for i in range(2)]
            zT = sb("zT", [128, NF, 512], BF16)
            ot = sb("ot", [128, 2, D], BF16) if l == 0 else sb("ot", [128, 1, D], F32)
            kb.op("pool", lambda e: e.memset(zT[:], 0.0), writes=[zT])
            kb.op("pool", lambda e: e.memset(hnb[:], 0.0), writes=[hnb])
            for t_ in ht:
                kb.op("pool", lambda e, t_=t_: e.memset(t_[:], 0.0), writes=[t_])
            for si, (kind, bi, T) in enumerate(self.seqs):
                S = self.S[si]
                Hs = S["h1"] if l == 0 else S["h3"]
                yout = self.y_p if kind == "p" else self.y_s
                for (t0, nb) in blocks_of(T):
                    ws, we = t0 - 1, t0 + nb + 1
                    W = we - ws
                    n = nb
                    wt = tiles_of(W)
                    for ti, (o, cnt) in enumerate(wt):
                        a, b = ws + o, ws + o + cnt
                        a2_, b2_ = max(a, 0), min(b, T)
                        kb.dma("sp" if ti % 2 == 0 else "act", ht[ti][a2_ - a:b2_ - a, :], Hs[a2_:b2_, :], writes=[ht[ti]])
                        self.norm_transpose(ws, ht[ti], cnt, nf, hnT, o, junk, ssb, hnb)
                    if ws < 0:
                        kb.op("pool", lambda e: e.memset(hnT[:, :, 0:1], 0.0), writes=[hnT])
                    if we > T:
                        kb.op("pool", lambda e, W=W: e.memset(hnT[:, :, W - 1:W], 0.0), writes=[hnT])
                    pls = {}
                    for i in range(NF + 3):
                        if i < NF:
                            f = i
                            fa = slice(f * 128, (f + 1) * 128)
                            fl = slice(DFF + f * 128, DFF + (f + 1) * 128)
                            pa = self.psum()
                            for c in range(8):
                                kb.op("pe", lambda e, pa=pa, c=c, fa=fa, W=W: e.matmul(pa[:, 0:W], lhsT=win[:, c, fa], rhs=hnT[:, c, 0:W], start=(c == 0), stop=(c == 7)),
                                      reads=[win, hnT], writes=[pa])
                            pl = self.psum()
                            pls[f] = pl
                            for c in range(8):
                                kb.op("pe", lambda e, pl=pl, c=c, fl=fl, W=W: e.matmul(pl[:, 0:W], lhsT=win[:, c, fl], rhs=hnT[:, c, 0:W], start=(c == 0), stop=(c == 7)),
                                      reads=[win, hnT], writes=[pl])
                        if 2 <= i <= NF + 1:
                            f = i - 2
                            tmp, sl = tmpL[f % 2], slL[f % 2]
                            kb.op("act", lambda e, f=f, n=n, tmp=tmp, sl=sl: e.activation(out=sl[:, 0:n], in_=tmp[:, 0:n], func=AF.Silu, bias=cb[:, f:f + 1], scale=1.0),
                                  reads=[tmp, cb], writes=[sl])
                        if i < NF:
                            ua = uaL[i % 2]
                            kb.op("act", lambda e, pa=pa, W=W, ua=ua: e.copy(out=ua[:, 0:W], in_=pa[:, 0:W]), reads=[pa], writes=[ua])
                        if 3 <= i <= NF + 2:
                            f = i - 3
                            sl = slL[f % 2]
                            pl_ = pls.pop(f)
                            kb.op("dve", lambda e, f=f, n=n, pl_=pl_, sl=sl: e.tensor_tensor(out=zT[:, f, 1:n + 1], in0=pl_[:, 1:n + 1], in1=sl[:, 0:n], op=ALU.mult),
                                  reads=[pl_, sl], writes=[zT])
                        if 1 <= i <= NF:
                            f = i - 1
                            ua, tmp = uaL[f % 2], tmpL[f % 2]
                            kb.op("dve", lambda e, f=f, n=n, ua=ua, tmp=tmp: e.tensor_scalar(out=tmp[:, 0:n], in0=ua[:, 0:n], scalar1=cw[:, f:f + 1], scalar2=None, op0=ALU.mult),
                                  reads=[ua, cw], writes=[tmp])
                            kb.op("dve", lambda e, f=f, n=n, ua=ua, tmp=tmp: e.scalar_tensor_tensor(out=tmp[:, 0:n], in0=ua[:, 1:n + 1], scalar=cw[:, NF + f:NF + f + 1], in1=tmp[:, 0:n],
                                                                                    op0=ALU.mult, op1=ALU.add), reads=[ua, cw], writes=[tmp])
                            kb.op("dve", lambda e, f=f, n=n, ua=ua, tmp=tmp: e.scalar_tensor_tensor(out=tmp[:, 0:n], in0=ua[:, 2:n + 2], scalar=cw[:, 2 * NF + f:2 * NF + f + 1], in1=tmp[:, 0:n],
                                                                                    op0=ALU.mult, op1=ALU.add), reads=[ua, cw], writes=[tmp])
                    for ti, (o, cnt) in enumerate(wt):
                        h_ = ht[ti]
                        for hf in range(2):
                            hs_ = slice(hf * 512, (hf + 1) * 512)
                            py = self.psum()
                            for f in range(NF):
                                kb.op("pe", lambda e, py=py, f=f, o=o, cnt=cnt, hs_=hs_: e.matmul(py[:cnt, :], lhsT=zT[:, f, o:o + cnt], rhs=wout[:, f, hs_],
                                                                                                  start=(f == 0), stop=(f == NF - 1)), reads=[zT, wout], writes=[py])
                            kb.op("dve", lambda e, py=py, cnt=cnt, hs_=hs_, h_=h_: e.tensor_tensor(out=h_[:cnt, hs_], in0=py[:cnt, :], in1=h_[:cnt, hs_], op=ALU.add),
                                  reads=[py], writes=[h_])
                        lo = max(o, 1) - o
                        hi = min(o + cnt, W - 1) - o
                        tok0 = ws + o + lo
                        if l == 0:
                            if hi > lo:
                                kb.dma("sp", S["h2"][tok0:tok0 + hi - lo, :], h_[lo:hi, :], reads=[h_])
                            self.norm_transpose(ws, h_, cnt, n2, hnT, o, junk, ssb, hnb)
                            ob = ot
                            for gp in range(4):
                                pd = self.psum()
                                for gg in range(2):
                                    g = gp * 2 + gg
                                    kb.op("pe", lambda e, pd=pd, g=g, gg=gg, o=o, cnt=cnt: e.matmul(pd[:cnt, gg * 256:(gg + 1) * 256], lhsT=hnT[:, g, o:o + cnt], rhs=dftc[:, :],
                                                                                                    start=True, stop=True), reads=[hnT, dftc], writes=[pd])
                                pd4 = pd[:cnt, :].rearrange("p (g k c) -> p g k c", g=2, k=2)
                                for k in range(2):
                                    dst = ob[:cnt, k, gp * 256:(gp + 1) * 256].rearrange("p (g c) -> p g c", g=2)
                                    if k == 0:
                                        kb.op("act", lambda e, dst=dst, pd4=pd4, k=k: e.copy(out=dst, in_=pd4[:, :, k, :]), reads=[pd], writes=[ob])
                                    else:
                                        kb.op("dve", lambda e, dst=dst, pd4=pd4, k=k: e.tensor_copy(out=dst, in_=pd4[:, :, k, :]), reads=[pd], writes=[ob])
                            if hi > lo:
                                kb.dma("sp", S["yc"][0, tok0:tok0 + hi - lo, :], ob[lo:hi, 0, :], reads=[ob])
                                kb.dma("act", S["yc"][1, tok0:tok0 + hi - lo, :], ob[lo:hi, 1, :], reads=[ob])
                        else:
                            kb.op("dve", lambda e, h_=h_, cnt=cnt: e.scalar_tensor_tensor(out=junk[:cnt, :], in0=h_[:cnt, :], scalar=1.0, in1=h_[:cnt, :],
                                                                                         op0=ALU.mult, op1=ALU.mult, accum_out=ssb[:cnt, 0:1]), reads=[h_], writes=[junk, ssb])
                            kb.op("dve", lambda e, cnt=cnt: e.tensor_scalar(out=ssb[:cnt, 1:2], in0=ssb[:cnt, 0:1], scalar1=1.0 / D, scalar2=1e-6, op0=ALU.mult, op1=ALU.add),
                                  reads=[ssb], writes=[ssb])
                            kb.op("act", lambda e, cnt=cnt: e.activation(out=ssb[:cnt, 2:3], in_=ssb[:cnt, 1:2], func=AF.Sqrt), reads=[ssb], writes=[ssb])
                            kb.op("dve", lambda e, cnt=cnt: e.reciprocal(out=ssb[:cnt, 3:4], in_=ssb[:cnt, 2:3]), reads=[ssb], writes=[ssb])
                            kb.op("dve", lambda e, h_=h_, cnt=cnt: e.scalar_tensor_tensor(out=ot[:cnt, 0, :], in0=h_[:cnt, :], scalar=ssb[:cnt, 3:4], in1=n2[:cnt, :],
                                                                                         op0=ALU.mult, op1=ALU.mult), reads=[h_, ssb, n2], writes=[ot])
                            lo2 = max(lo, NMETA - (ws + o))
                            if hi > lo2:
                                tk = ws + o + lo2 - NMETA
                                kb.dma("sp", yout[bi, tk:tk + hi - lo2, :], ot[lo2:hi, 0, :], reads=[ot])
            kb.barrier()

    def phase4a(self):
        kb, nc, I, C = self.kb, self.nc, self.I, self.C
        NTmax = -(-max(T for _, _, T in self.seqs) // 128)
        with contextlib.ExitStack() as st:
            sb = lambda name, shape, dt: self.sb(st, "p4_" + name, shape, dt)
            wf = sb("wf", [128, 8, D], BF16)
            for c in range(8):
                kb.dma("pool", wf[:, c, :], I["w_f"][c * 128:(c + 1) * 128, :], writes=[wf])
            Y = [sb("Y%d" % k, [128, NTmax, D], BF16) for k in range(2)]
            M = [[sb("M%d_%d" % (k, j), [128, NTmax, 128], BF16) for k in range(2)] for j in range(2)]
            fb = sb("fb", [128, D], BF16)
            fT = sb("fT", [128, 8, 128], BF16)
            h2t = [sb("h2t%d" % i, [128, D], F32) for i in range(2)]
            h3t = [sb("h3t%d" % i, [128, D], F32) for i in range(2)]
            kb.op("pool", lambda e: e.memset(fb[:], 0.0), writes=[fb])
            it = 0
            for si, (kind, bi, T) in enumerate(self.seqs):
                S = self.S[si]
                dft = I["c_dft_p"] if kind == "p" else I["c_dft_s"]
                NT = -(-T // 128)
                nfull = T // 128
                rem = T - nfull * 128
                for k in range(2):
                    if nfull:
                        kb.dma("sp" if k == 0 else "act", Y[k][:, 0:nfull, :], S["yc"][k, 0:nfull * 128, :].rearrange("(c p) d -> p c d", p=128), writes=[Y[k]])
                    if rem:
                        kb.dma("sp" if k == 0 else "act", Y[k][0:rem, nfull, :], S["yc"][k, nfull * 128:T, :], writes=[Y[k]])
                for (k0, kn) in tiles_of(T):
                    j = it % 2
                    it += 1
                    for k in range(2):
                        if nfull:
                            kb.dma("sp" if k == 0 else "act", M[j][k][:, 0:nfull, 0:kn], dft[k, 0:nfull * 128, k0:k0 + kn].rearrange("(c p) q -> p c q", p=128), writes=[M[j][k]])
                        if rem:
                            kb.dma("sp" if k == 0 else "act", M[j][k][0:rem, nfull, 0:kn], dft[k, nfull * 128:T, k0:k0 + kn], writes=[M[j][k]])
                    kb.dma("sp", h2t[j][:kn, :], S["h2"][k0:k0 + kn, :], writes=[h2t[j]])
                    for hf in range(2):
                        hs_ = slice(hf * 512, (hf + 1) * 512)
                        pf = self.psum()
                        for ch in range(NT):
                            cc = 128 if ch < nfull else rem
                            for k in range(2):
                                kb.op("pe", lambda e, pf=pf, ch=ch, cc=cc, k=k, j=j, kn=kn, hs_=hs_: e.matmul(pf[:kn, :], lhsT=M[j][k][0:cc, ch, 0:kn], rhs=Y[k][0:cc, ch, hs_],
                                                                                                           start=(ch == 0 and k == 0), stop=(ch == NT - 1 and k == 1)),
                                      reads=[M[j][k], Y[k]], writes=[pf])
                        if hf == 0:
                            kb.op("act", lambda e, pf=pf, kn=kn, hs_=hs_: e.copy(out=fb[:kn, hs_], in_=pf[:kn, :]), reads=[pf], writes=[fb])
                        else:
                            kb.op("dve", lambda e, pf=pf, kn=kn, hs_=hs_: e.tensor_copy(out=fb[:kn, hs_], in_=pf[:kn, :]), reads=[pf], writes=[fb])
                    self.transpose_into(fb, kn, fT, 0)
                    for hf in range(2):
                        hs_ = slice(hf * 512, (hf + 1) * 512)
                        p = self.psum()
                        for c in range(8):
                            kb.op("pe", lambda e, p=p, c=c, kn=kn, hs_=hs_: e.matmul(p[:kn, :], lhsT=fT[:, c, 0:kn], rhs=wf[:, c, hs_], start=(c == 0), stop=(c == 7)),
                                  reads=[fT, wf], writes=[p])
                        kb.op("dve", lambda e, p=p, kn=kn, hs_=hs_, j=j: e.tensor_tensor(out=h3t[j][:kn, hs_], in0=p[:kn, :], in1=h2t[j][:kn, hs_], op=ALU.add),
                              reads=[p, h2t[j]], writes=[h3t[j]])
                    kb.dma("act", S["h3"][k0:k0 + kn, :], h3t[j][:kn, :], reads=[h3t[j]])
            kb.barrier()


def host_consts(Tp, Ts, dft=False):
    c = {}
    c["c_ident"] = np.eye(128, dtype=np.float32)
    blk = np.zeros((128, 128), np.float32)
    blk[:64, :64] = 1; blk[64:, 64:] = 1
    c["c_blk"] = blk
    hs = np.zeros((128, 2), np.float32); hs[:64, 0] = 1; hs[64:, 1] = 1
    c["c_hsel"] = hs
    s_ = np.arange(64)[:, None]; t_ = np.arange(64)[None, :]
    m = np.zeros((2, 128, 128), np.float32)
    for r0 in (0, 64):
        m[0, r0:r0 + 64, 0:64] = (s_ < t_); m[0, r0:r0 + 64, 64:128] = (s_ <= t_)
        m[1, r0:r0 + 64, 0:64] = (s_ > t_); m[1, r0:r0 + 64, 64:128] = (s_ >= t_)
    c["c_mask"] = np.tile(m, (1, 1, 4))
    mt = np.zeros((2, 64, 64), np.float32)
    mt[0] = (s_ > t_); mt[1] = (s_ < t_)
    c["c_maskT"] = np.tile(mt, (1, 1, 16))
    cm = np.ones((128, 512), np.float32); cm[:, ::64] = 0
    c["c_cm"] = cm
    c["c_eye2"] = np.eye(128, dtype=np.float32)
    c["c_eye16"] = np.tile(np.eye(64, dtype=np.float32), (1, 16))
    cc = np.arange(128)
    ang = 2 * np.pi * np.outer(cc, cc) / 128
    c["c_dftc"] = np.concatenate([np.cos(ang), np.sin(ang)], 1).astype(np.float32) / np.sqrt(128)
    for nm, T in ((("c_dft_p", Tp), ("c_dft_s", Ts)) if dft else ()):
        t = np.arange(T, dtype=np.int64)
        ph = (np.outer(t, t) % T).astype(np.float64) * (2 * np.pi / T)
        c[nm] = np.stack([np.cos(ph), -np.sin(ph)]).astype(np.float32) / np.float32(np.sqrt(T))
        c[nm] = c[nm].astype(ml_dtypes.bfloat16)
    return c


PHASES = ("p1", "scan", "p3", "p4")
NCORES = 8


def kernel(x_prompt, x_sample, meta_tokens, norm_mix, norm_ffn, norm_final,
           rwkv_mu, rwkv_w_rkv, rwkv_w0, rwkv_w1, rwkv_w2, rwkv_a0, rwkv_a1, rwkv_a2,
           rwkv_g1, rwkv_g2, rwkv_k_k, rwkv_k_a, rwkv_r_k, rwkv_gn_w, rwkv_gn_b, rwkv_w_o,
           fnet_w_o, ffn_w_in, ffn_conv_w, ffn_conv_b, ffn_w_out):
    f = lambda a: np.ascontiguousarray(np.asarray(a, dtype=np.float32))
    x_prompt, x_sample = f(x_prompt), f(x_sample)
    Bp, Sp, _ = x_prompt.shape
    Bs, Ss, _ = x_sample.shape
    npc, nsc = Bp // NCORES, Bs // NCORES
    Tp, Ts = Sp + NMETA, Ss + NMETA
    prog = Prog(npc, Tp, nsc, Ts, debug=False, phases=PHASES)
    nc = prog.build()
    shared = {
        "meta": f(meta_tokens), "norm_mix": f(norm_mix), "norm_ffn": f(norm_ffn),
        "norm_final": f(norm_final).reshape(1, D), "mu": f(rwkv_mu)[0], "w_rkv": f(rwkv_w_rkv)[0],
        "w0": f(rwkv_w0)[0], "w1": f(rwkv_w1)[0], "w2": f(rwkv_w2)[0],
        "a0": f(rwkv_a0)[0], "a1": f(rwkv_a1)[0], "a2": f(rwkv_a2)[0],
        "g1": f(rwkv_g1)[0], "g2": f(rwkv_g2)[0], "k_k": f(rwkv_k_k), "k_a": f(rwkv_k_a),
        "r_k": f(rwkv_r_k).reshape(1, D), "gn_w": f(rwkv_gn_w), "gn_b": f(rwkv_gn_b), "w_o": f(rwkv_w_o)[0],
        "w_f": f(fnet_w_o)[0], "ffn_w_in": f(ffn_w_in), "conv_w": f(ffn_conv_w), "conv_b": f(ffn_conv_b),
        "ffn_w_out": f(ffn_w_out),
    }
    shared.update(host_consts(Tp, Ts, dft=("p4" in PHASES)))
    in_maps = []
    for c in range(NCORES):
        m = dict(shared)
        m["x_p"] = x_prompt[c * npc:(c + 1) * npc]
        m["x_s"] = x_sample[c * nsc:(c + 1) * nsc]
        in_maps.append(m)
    res = run_bass_kernel_spmd(nc, in_maps, core_ids=list(range(NCORES)))
    y_p = np.concatenate([np.asarray(r["y_p"], dtype=np.float32) for r in res.results], axis=0)
    y_s = np.concatenate([np.asarray(r["y_s"], dtype=np.float32) for r in res.results], axis=0)
    return (y_p, y_s)
```

```python
import contextlib
import math
import numpy as np
import ml_dtypes
import concourse.bass as bass
import concourse.mybir as mybir
from concourse.bass_utils import run_bass_kernel_spmd

F32 = mybir.dt.float32
BF16 = mybir.dt.bfloat16
AF = mybir.ActivationFunctionType
ALU = mybir.AluOpType
AX = mybir.AxisListType

D = 1024
NH = 16
HS = 64
DFF = 2816
NMETA = 16
CH = 64
LWC = math.exp(-0.5)
NDS = 48
NSW = 8


class Buf:
    __slots__ = ("t", "w", "r")

    def __init__(self, t):
        self.t = t
        self.w = None
        self.r = []

    def __getitem__(self, k):
        return self.t[k]


class KB:
    def __init__(self, nc, stack):
        self.nc = nc
        self.names = ["pe", "act", "dve", "pool", "sp"]
        self.q = {e: [] for e in self.names}
        self.count = {e: 0 for e in self.names}
        self.mile = {e: [] for e in self.names}
        self.mileset = {e: set() for e in self.names}
        self.semval = {e: 0 for e in self.names}
        self.milemap = {e: {} for e in self.names}
        self.waited = {e: {} for e in self.names}
        self.flushed = {e: 0 for e in self.names}
        self.milekeys = {e: [] for e in self.names}
        self.esem = {e: stack.enter_context(nc.semaphore("s_" + e)) for e in self.names}
        self.dsem = [stack.enter_context(nc.semaphore("d%d" % i)) for i in range(NDS)]
        self.dval = [0] * NDS
        self.dnext = 0
        self.dnext_sw = 0
        self.rr = 0

    def _wait(self, eng, deps):
        wd = self.waited[eng]
        for dep in deps:
            if dep[0] == "eng":
                _, f, idx = dep
                if f == eng and eng == "pe":
                    continue
                if idx <= self.flushed[f] and idx not in self.milemap[f]:
                    idx = min(k for k in self.milekeys[f] if k >= idx)
                if wd.get(("eng", f), 0) >= idx:
                    continue
                wd[("eng", f)] = idx
                if idx not in self.mileset[f] and idx not in self.milemap[f]:
                    self.mileset[f].add(idx)
                self.q[eng].append(("weng", f, idx))
            else:
                _, si, val = dep
                if wd.get(("dma", si), 0) >= val:
                    continue
                wd[("dma", si)] = val
                self.q[eng].append(("wdma", si, val))

    def _deps(self, reads, writes):
        deps = []
        for b in reads:
            if b is not None and b.w is not None:
                deps.append(b.w)
        for b in writes:
            if b is not None:
                if b.w is not None:
                    deps.append(b.w)
                deps.extend(b.r)
        return deps

    def _mark(self, tok, reads, writes):
        for b in reads:
            if b is None:
                continue
            if tok[0] == "eng":
                b.r = [t for t in b.r if not (t[0] == "eng" and t[1] == tok[1])]
            b.r.append(tok)
        for b in writes:
            if b is None:
                continue
            b.w = tok
            b.r = []

    def op(self, eng, fn, reads=(), writes=()):
        self._wait(eng, self._deps(reads, writes))
        self.count[eng] += 1
        idx = self.count[eng]
        tok = ("eng", eng, idx)
        self.q[eng].append(("op", fn, idx))
        self._mark(tok, reads, writes)
        return tok

    def dma(self, eng, out, in_, reads=(), writes=()):
        if eng == "pool":
            si = self.dnext_sw
            self.dnext_sw = (self.dnext_sw + 1) % NSW
        else:
            si = NSW + self.dnext
            self.dnext = (self.dnext + 1) % (NDS - NSW)
        deps = self._deps(reads, writes)
        if self.dval[si] > 0:
            deps.append(("dma", si, self.dval[si]))
        self._wait(eng, deps)
        self.dval[si] += 16
        tok = ("dma", si, self.dval[si])
        self.q[eng].append(("dma", out, in_, si))
        self._mark(tok, reads, writes)
        return tok

    def dma_rr(self, out, in_, reads=(), writes=(), cast=False):
        if cast:
            return self.dma("pool", out, in_, reads, writes)
        eng = ("sp", "act")[self.rr % 2]
        self.rr += 1
        return self.dma("sp", out, in_, reads, writes)

    def barrier(self):
        toks = [("eng", e, self.count[e]) for e in self.names if self.count[e] > 0]
        toks += [("dma", si, self.dval[si]) for si in range(NDS) if self.dval[si] > 0]
        for e in self.names:
            self._wait(e, toks)

    def flush(self):
        nc = self.nc
        for e in self.names:
            v = self.semval[e]
            if self.count[e] > self.flushed[e]:
                self.mileset[e].add(self.count[e])
            self.milekeys[e] = self.milekeys[e][-1:] + sorted(self.mileset[e])
            self.flushed[e] = self.count[e]
            for idx in sorted(self.mileset[e]):
                v += 1
                self.milemap[e][idx] = v
            self.semval[e] = v
        q = self.q
        kb = self

        def replay(name, eng):
            for ent in q[name]:
                k = ent[0]
                if k == "weng":
                    eng.wait_ge(kb.esem[ent[1]], kb.milemap[ent[1]][ent[2]])
                elif k == "wdma":
                    eng.wait_ge(kb.dsem[ent[1]], ent[2])
                elif k == "op":
                    ins = ent[1](eng)
                    if ent[2] in kb.mileset[name]:
                        ins.then_inc(kb.esem[name], 1)
                else:
                    eng.dma_start(out=ent[1], in_=ent[2]).then_inc(kb.dsem[ent[3]], 16)

        with nc.allow_non_contiguous_dma(reason="small per-feature vectors"), nc.Block() as block:
            @block.tensor
            def _(eng):
                replay("pe", eng)

            @block.scalar
            def _(eng):
                replay("act", eng)

            @block.vector
            def _(eng):
                replay("dve", eng)

            @block.gpsimd
            def _(eng):
                replay("pool", eng)

            @block.sync
            def _(eng):
                replay("sp", eng)

        self.q = {e: [] for e in self.names}
        self.mileset = {e: set() for e in self.names}


def blocks_of(T, maxn=510):
    nb = -(-T // maxn)
    base, rem = divmod(T, nb)
    out = []
    t = 0
    for i in range(nb):
        n = base + (1 if i < rem else 0)
        out.append((t, n))
        t += n
    return out


def tiles_of(n, p=128):
    return [(i, min(p, n - i)) for i in range(0, n, p)]


class Prog:
    def __init__(self, seqs_p, T_p, seqs_s, T_s, debug=False, phases=("p1", "scan", "p3", "p4")):
        self.np_, self.Tp, self.ns_, self.Ts = seqs_p, T_p, seqs_s, T_s
        self.debug = debug
        self.phases = phases
        self._dbg = set()
        self.seqs = [("p", i, T_p) for i in range(seqs_p)] + [("s", i, T_s) for i in range(seqs_s)]

    def build(self):
        nc = bass.Bass("TRN2", target_bir_lowering=False)
        self.nc = nc
        self.stack = contextlib.ExitStack()
        st = self.stack
        kb = KB(nc, st)
        self.kb = kb
        I = {}

        def din(name, shape, dt=F32):
            I[name] = nc.dram_tensor(name, list(shape), dt, kind="ExternalInput").ap()
            return I[name]

        self.I = I
        din("x_p", [self.np_, self.Tp - NMETA, D])
        din("x_s", [self.ns_, self.Ts - NMETA, D])
        din("meta", [NMETA, D])
        din("norm_mix", [2, D]); din("norm_ffn", [2, D]); din("norm_final", [1, D])
        din("mu", [6, D]); din("w_rkv", [3, D, D])
        din("w0", [2, D]); din("w1", [2, D, 64]); din("w2", [2, 64, D])
        din("a0", [2, D]); din("a1", [2, D, 64]); din("a2", [2, 64, D])
        din("g1", [D, 160]); din("g2", [160, D])
        din("k_k", [1, D]); din("k_a", [1, D]); din("r_k", [1, D])
        din("gn_w", [1, D]); din("gn_b", [1, D]); din("w_o", [D, D])
        din("w_f", [D, D]); din("ffn_w_in", [2, D, 2 * DFF]); din("conv_w", [2, 3, DFF])
        din("conv_b", [2, DFF]); din("ffn_w_out", [2, DFF, D])
        din("c_ident", [128, 128]); din("c_blk", [128, 128]); din("c_hsel", [128, 2])
        din("c_mask", [2, 128, 512]); din("c_maskT", [2, 64, 1024]); din("c_cm", [128, 512]); din("c_eye16", [64, 1024]); din("c_eye2", [128, 128])
        din("c_dftc", [128, 256])
        if "p4" in self.phases:
            din("c_dft_p", [2, self.Tp, self.Tp], BF16)
            din("c_dft_s", [2, self.Ts, self.Ts], BF16)
        okind = "ExternalOutput"
        self.y_p = nc.dram_tensor("y_p", [self.np_, self.Tp - NMETA, D], F32, kind=okind).ap()
        self.y_s = nc.dram_tensor("y_s", [self.ns_, self.Ts - NMETA, D], F32, kind=okind).ap()
        skind = "ExternalOutput" if self.debug else "Internal"
        self.S = []
        for si, (kind, bi, T) in enumerate(self.seqs):
            d = {}
            d["fm"] = nc.dram_tensor("fm%d" % si, [8, D, T], F32, kind=skind).ap()
            d["v"] = nc.dram_tensor("sv%d" % si, [T, D], BF16, kind=skind).ap()
            d["g"] = nc.dram_tensor("sg%d" % si, [T, D], BF16, kind=skind).ap()
            d["bonus"] = nc.dram_tensor("bonus%d" % si, [T, NH], F32, kind=skind).ap()
            d["o"] = nc.dram_tensor("so%d" % si, [2, T, D], F32, kind=skind).ap()
            d["h2"] = nc.dram_tensor("h2_%d" % si, [T, D], F32, kind=skind).ap()
            d["h1"] = nc.dram_tensor("h1_%d" % si, [T, D], F32, kind=skind).ap()
            d["h3"] = nc.dram_tensor("h3_%d" % si, [T, D], F32, kind=skind).ap()
            d["yc"] = nc.dram_tensor("yc%d" % si, [2, T, D], BF16, kind=skind).ap()
            self.S.append(d)

        self.consts()
        if "p1" in self.phases:
            self.phase1()
        if "scan" in self.phases:
            self.phase_scan()
        if "p3" in self.phases:
            self.phase3()
        if "p4" in self.phases:
            self.phase4()
        kb.barrier()
        kb.flush()
        st.close()
        return nc

    def sb(self, stack, name, shape, dt):
        return Buf(stack.enter_context(self.nc.sbuf_tensor("sb_" + name, list(shape), dt)))

    def ps(self, stack, name, shape, dt):
        return Buf(stack.enter_context(self.nc.psum_tensor("ps_" + name, list(shape), dt)))

    def xrows(self, si, t0, n):
        kind, bi, T = self.seqs[si]
        x = self.I["x_p"] if kind == "p" else self.I["x_s"]
        out = []
        if t0 < NMETA:
            m = min(NMETA, t0 + n) - t0
            out.append((self.I["meta"][t0:t0 + m, :], 0, m))
            if n > m:
                out.append((x[bi, 0:n - m, :], m, n - m))
        else:
            out.append((x[bi, t0 - NMETA:t0 - NMETA + n, :], 0, n))
        return out

    def consts(self):
        kb, nc, st = self.kb, self.nc, self.stack
        I = self.I
        C = {}
        self.C = C

        def ld(name, shape, dt, src, cast=False):
            b = self.sb(st, "k_" + name, shape, dt)
            kb.dma_rr(b[:], src, writes=[b], cast=cast)
            C[name] = b
            return b

        ld("ident", [128, 128], BF16, I["c_ident"], cast=True)
        ld("blk", [128, 128], BF16, I["c_blk"], cast=True)
        ld("hsel", [128, 2], BF16, I["c_hsel"], cast=True)
        ld("eye2", [128, 128], BF16, I["c_eye2"], cast=True)
        def colvec(name, src, n):
            b = self.sb(st, "k_" + name, [128, n * 8], F32)
            with nc.allow_non_contiguous_dma(reason="tiny per-feature vectors"):
                for i in range(n):
                    kb.dma("sp", b[:, i * 8:(i + 1) * 8], src[i].rearrange("(c p) -> p c", p=128), writes=[b])
            C[name] = b
        colvec("muT", I["mu"], 6)
        colvec("w0T", I["w0"], 2)
        colvec("a0T", I["a0"], 2)
        colvec("kkT", I["k_k"], 1)
        colvec("kaT", I["k_a"], 1)
        colvec("rkT", I["r_k"], 1)
        b = self.sb(st, "k_omka", [128, 8], F32)
        kb.op("dve", lambda e, b=b: e.tensor_scalar(out=b[:], in0=C["kaT"][:], scalar1=-1.0, scalar2=1.0,
                                                    op0=ALU.mult, op1=ALU.add), reads=[C["kaT"]], writes=[b])
        C["omka"] = b
        self.psb = [self.ps(st, "psb%d" % i, [128, 512], F32) for i in range(8)]
        self.psi = 0

    def dbg(self, name, buf, ap, shape, dt):
        if not self.debug or name in self._dbg:
            return
        self._dbg.add(name)
        d = self.nc.dram_tensor("dbg_" + name, list(shape), dt, kind="ExternalOutput").ap()
        self.kb.dma("sp", d, ap, reads=[buf])

    def rowvec(self, st, name, src):
        b = self.sb(st, "rv_" + name, [128, D], F32)
        self.kb.dma("sp", b[:], src.partition_broadcast(128), writes=[b])
        self.C[name] = b
        return b

    def psum(self):
        b = self.psb[self.psi % 8]
        self.psi += 1
        return b

    def norm_transpose(self, ws, xt, n, grow, hnT, col0, junk, ssb, hnb):
        kb = self.kb
        C = self.C
        kb.op("dve", lambda e: e.scalar_tensor_tensor(out=junk[:n, :], in0=xt[:n, :], scalar=1.0, in1=xt[:n, :],
                                                      op0=ALU.mult, op1=ALU.mult, accum_out=ssb[:n, 0:1]),
              reads=[xt], writes=[junk, ssb])
        kb.op("dve", lambda e: e.tensor_scalar(out=ssb[:n, 1:2], in0=ssb[:n, 0:1], scalar1=1.0 / D, scalar2=1e-6,
                                               op0=ALU.mult, op1=ALU.add), reads=[ssb], writes=[ssb])
        kb.op("act", lambda e: e.activation(out=ssb[:n, 2:3], in_=ssb[:n, 1:2], func=AF.Sqrt), reads=[ssb], writes=[ssb])
        kb.op("dve", lambda e: e.reciprocal(out=ssb[:n, 3:4], in_=ssb[:n, 2:3]), reads=[ssb], writes=[ssb])
        kb.op("dve", lambda e: e.scalar_tensor_tensor(out=hnb[:n, :], in0=xt[:n, :], scalar=ssb[:n, 3:4], in1=grow[:n, :],
                                                      op0=ALU.mult, op1=ALU.mult), reads=[xt, ssb, grow], writes=[hnb])
        self.transpose_into(hnb, n, hnT, col0)

    def norm_transpose_multi(self, items, grow, hnT, ssbs, hnbs):
        kb, C = self.kb, self.C
        for i, (xt, n, c0) in enumerate(items):
            kb.op("dve", lambda e, xt=xt, n=n, i=i: e.scalar_tensor_tensor(out=hnbs[i][:n, :], in0=xt[:n, :], scalar=1.0, in1=xt[:n, :],
                                                                         op0=ALU.mult, op1=ALU.mult, accum_out=ssbs[i][:n, 0:1]),
                  reads=[xt], writes=[hnbs[i], ssbs[i]])
        for i, (xt, n, c0) in enumerate(items):
            kb.op("dve", lambda e, n=n, i=i: e.tensor_scalar(out=ssbs[i][:n, 1:2], in0=ssbs[i][:n, 0:1], scalar1=1.0 / D, scalar2=1e-6,
                                                           op0=ALU.mult, op1=ALU.add), reads=[ssbs[i]], writes=[ssbs[i]])
        for i, (xt, n, c0) in enumerate(items):
            kb.op("act", lambda e, n=n, i=i: e.activation(out=ssbs[i][:n, 2:3], in_=ssbs[i][:n, 1:2], func=AF.Sqrt), reads=[ssbs[i]], writes=[ssbs[i]])
        for i, (xt, n, c0) in enumerate(items):
            kb.op("dve", lambda e, n=n, i=i: e.reciprocal(out=ssbs[i][:n, 3:4], in_=ssbs[i][:n, 2:3]), reads=[ssbs[i]], writes=[ssbs[i]])
        for i, (xt, n, c0) in enumerate(items):
            kb.op("dve", lambda e, xt=xt, n=n, i=i: e.scalar_tensor_tensor(out=hnbs[i][:n, :], in0=xt[:n, :], scalar=ssbs[i][:n, 3:4], in1=grow[:n, :],
                                                                         op0=ALU.mult, op1=ALU.mult), reads=[xt, ssbs[i], grow], writes=[hnbs[i]])
        pts = []
        for i, (xt, n, c0) in enumerate(items):
            pt = self.psum()
            pts.append(pt)
            ptb = pt.t[:].bitcast(BF16)
            for c in range(8):
                kb.op("pe", lambda e, c=c, n=n, i=i, ptb=ptb: e.transpose(out=ptb[:, c * 128:c * 128 + n], in_=hnbs[i][:n, c * 128:(c + 1) * 128],
                                                                        identity=C["ident"][:n, :n]), reads=[hnbs[i], C["ident"]], writes=[pt])
        for i, (xt, n, c0) in enumerate(items):
            src3 = pts[i].t[:].bitcast(BF16).rearrange("p (c t) -> p c t", t=128)
            kb.op("act", lambda e, n=n, c0=c0, src3=src3: e.copy(out=hnT[:, 0:8, c0:c0 + n], in_=src3[:, 0:8, 0:n]), reads=[pts[i]], writes=[hnT])

    def transpose_into(self, src, n, dstT, col0, nchunk=8, eng="act"):
        kb = self.kb
        C = self.C
        pt = self.psum()
        ptb = pt.t[:].bitcast(BF16)
        for c in range(nchunk):
            kb.op("pe", lambda e, c=c: e.transpose(out=ptb[:, c * 128:c * 128 + n], in_=src[:n, c * 128:(c + 1) * 128],
                                                   identity=C["ident"][:n, :n]), reads=[src, C["ident"]], writes=[pt])
        src3 = ptb.rearrange("p (c t) -> p c t", t=128)
        if eng == "act":
            kb.op("act", lambda e: e.copy(out=dstT[:, 0:nchunk, col0:col0 + n], in_=src3[:, 0:nchunk, 0:n]),
                  reads=[pt], writes=[dstT])
        else:
            kb.op("dve", lambda e: e.tensor_copy(out=dstT[:, 0:nchunk, col0:col0 + n], in_=src3[:, 0:nchunk, 0:n]),
                  reads=[pt], writes=[dstT])

    def phase1(self):
        kb, nc, I, C = self.kb, self.nc, self.I, self.C
        with contextlib.ExitStack() as st:
            sb = lambda name, shape, dt: self.sb(st, name, shape, dt)
            self.rowvec(st, "nm0", I["norm_mix"][0:1, :])
            wr = [sb("wrkv%d" % i, [128, 8, D], BF16) for i in range(3)]
            for i in range(3):
                for c in range(8):
                    kb.dma("pool", wr[i][:, c, :], I["w_rkv"][i, c * 128:(c + 1) * 128, :], writes=[wr[i]])
            w1c = sb("w1c", [128, 8, 128], BF16)
            a1c = sb("a1c", [128, 8, 128], BF16)
            for z in range(2):
                kb.dma("pool", w1c[:, :, z * 64:(z + 1) * 64], I["w1"][z].rearrange("(c p) r -> p c r", p=128), writes=[w1c])
                kb.dma("pool", a1c[:, :, z * 64:(z + 1) * 64], I["a1"][z].rearrange("(c p) r -> p c r", p=128), writes=[a1c])
            w2c = sb("w2c", [128, D], BF16)
            a2c = sb("a2c", [128, D], BF16)
            for z in range(2):
                kb.dma("pool", w2c[z * 64:(z + 1) * 64, :], I["w2"][z], writes=[w2c])
                kb.dma("pool", a2c[z * 64:(z + 1) * 64, :], I["a2"][z], writes=[a2c])
            g1c = sb("g1c", [128, 8, 160], BF16)
            kb.dma("pool", g1c[:], I["g1"].rearrange("(c p) r -> p c r", p=128), writes=[g1c])
            g2a = sb("g2a", [128, D], BF16)
            g2b = sb("g2b", [32, D], BF16)
            kb.dma("pool", g2a[:], I["g2"][0:128, :], writes=[g2a])
            kb.dma("pool", g2b[:], I["g2"][128:160, :], writes=[g2b])

            xt = [sb("xt%d" % i, [128, D], F32) for i in range(2)]
            hnbs = [sb("hnb%d" % i, [128, D], BF16) for i in range(2)]
            ssbs = [sb("ssb%d" % i, [128, 4], F32) for i in range(2)]
            hnb = hnbs[0]
            hnT = sb("hnT", [128, 8, 512], BF16)
            tmp = sb("tmp", [128, 8, 512], BF16)
            xx = tmp
            xm = [sb("xm%d" % i, [128, 8, 512], BF16) for i in range(5)]
            tw = sb("tw", [128, 512], BF16)
            ta = sb("ta", [128, 512], BF16)
            sg1 = sb("sg1", [128, 512], BF16)
            sg2 = sb("sg2", [32, 512], BF16)
            soL = [[sb("so%d_%d" % (i, j), [128, 512], F32) for i in range(8)] for j in range(2)]
            kunL = [sb("kun%d" % j, [128, 512], F32) for j in range(2)]
            ksbL = [sb("ksb%d" % j, [128, 512], F32) for j in range(2)]
            sqb = sb("sqb", [128, 512], BF16)
            rnL = [sb("rn0", [128, 512], F32)] * 2
            t1L = [sb("t1_0", [128, 512], F32)] * 2
            av = [sb("av%d" % i, [128, 512], F32) for i in range(2)]
            rkp = sb("rkp", [128, 8, 512], BF16)
            vst = sb("vst", [128, 2, D], BF16)
            gst = sb("gst", [128, 2, D], BF16)
            bst = sb("bst", [128, 4, NH], F32)
            for b_ in (xt[0], xt[1], hnbs[0], hnbs[1]):
                kb.op("pool", lambda e, b_=b_: e.memset(b_[:], 0.0), writes=[b_])

            for si, (kind, bi, T) in enumerate(self.seqs):
                S = self.S[si]
                for (t0, nb) in blocks_of(T):
                    ws, we = t0 - 1, t0 + nb + 1
                    W = we - ws
                    n = nb
                    wtl = tiles_of(W)
                    for p0 in range(0, len(wtl), 2):
                        items = []
                        for ti in range(p0, min(p0 + 2, len(wtl))):
                            o, cnt = wtl[ti]
                            a, b = ws + o, ws + o + cnt
                            a2_, b2_ = max(a, 0), min(b, T)
                            x_ = xt[ti % 2]
                            for (src, ro, nr) in self.xrows(si, a2_, b2_ - a2_):
                                kb.dma("sp" if ti % 2 == 0 else "act", x_[a2_ - a + ro:a2_ - a + ro + nr, :], src, writes=[x_])
                            items.append((x_, cnt, o))
                        self.norm_transpose_multi(items, C["nm0"], hnT, ssbs, hnbs)
                    if ws < 0:
                        kb.op("pool", lambda e: e.memset(hnT[:, :, 0:1], 0.0), writes=[hnT])
                    if we > T:
                        kb.op("pool", lambda e, W=W: e.memset(hnT[:, :, W - 1:W], 0.0), writes=[hnT])
                    kb.op("dve", lambda e, n=n: e.tensor_tensor(out=tmp[:, :, 0:n], in0=hnT[:, :, 0:n], in1=hnT[:, :, 2:n + 2],
                                                                op=ALU.add), reads=[hnT], writes=[tmp])
                    kb.op("dve", lambda e, n=n: e.scalar_tensor_tensor(out=tmp[:, :, 0:n], in0=tmp[:, :, 0:n], scalar=0.5,
                                                                        in1=hnT[:, :, 1:n + 1], op0=ALU.mult, op1=ALU.subtract),
                          reads=[hnT], writes=[tmp])
                    def mix(m, dst):
                        for c in range(8):
                            kb.op("dve", lambda e, m=m, c=c, n=n, dst=dst: e.scalar_tensor_tensor(
                                out=dst[:, c, 0:n], in0=xx[:, c, 0:n], scalar=C["muT"][:, m * 8 + c:m * 8 + c + 1],
                                in1=hnT[:, c, 1:n + 1], op0=ALU.mult, op1=ALU.add), reads=[xx, hnT, C["muT"]], writes=[dst])
                    xw, xa, xg, xr, xk = xm[0], xm[1], xm[2], xm[3], xm[4]
                    xv = xm[0]
                    mix(1, xw); mix(4, xa); mix(5, xg); mix(0, xr); mix(2, xk)
                    p = self.psum()
                    for c in range(8):
                        kb.op("pe", lambda e, c=c, p=p, n=n: e.matmul(p[:, 0:n], lhsT=w1c[:, c, :], rhs=xw[:, c, 0:n],
                                                                      start=(c == 0), stop=(c == 7)), reads=[w1c, xw], writes=[p])
                    kb.op("act", lambda e, p=p, n=n: e.activation(out=tw[:, 0:n], in_=p[:, 0:n], func=AF.Tanh), reads=[p], writes=[tw])
                    p = self.psum()
                    for c in range(8):
                        kb.op("pe", lambda e, c=c, p=p, n=n: e.matmul(p[:, 0:n], lhsT=a1c[:, c, :], rhs=xa[:, c, 0:n],
                                                                      start=(c == 0), stop=(c == 7)), reads=[a1c, xa], writes=[p])
                    kb.op("act", lambda e, p=p, n=n: e.copy(out=ta[:, 0:n], in_=p[:, 0:n]), reads=[p], writes=[ta])
                    p = self.psum()
                    for c in range(8):
                        kb.op("pe", lambda e, c=c, p=p, n=n: e.matmul(p[:, 0:n], lhsT=g1c[:, c, 0:128], rhs=xg[:, c, 0:n],
                                                                      start=(c == 0), stop=(c == 7)), reads=[g1c, xg], writes=[p])
                    kb.op("act", lambda e, p=p, n=n: e.activation(out=sg1[:, 0:n], in_=p[:, 0:n], func=AF.Sigmoid), reads=[p], writes=[sg1])
                    p = self.psum()
                    for c in range(8):
                        kb.op("pe", lambda e, c=c, p=p, n=n: e.matmul(p[0:32, 0:n], lhsT=g1c[:, c, 128:160], rhs=xg[:, c, 0:n],
                                                                      start=(c == 0), stop=(c == 7)), reads=[g1c, xg], writes=[p])
                    kb.op("act", lambda e, p=p, n=n: e.activation(out=sg2[:, 0:n], in_=p[0:32, 0:n], func=AF.Sigmoid), reads=[p], writes=[sg2])
                    mix(3, xv)
                    for s in range(8):
                        so, kun, ksb, rn, t1 = soL[s % 2], kunL[s % 2], ksbL[s % 2], rnL[s % 2], t1L[s % 2]
                        sl = slice(s * 128, (s + 1) * 128)
                        p = self.psum()
                        for c in range(8):
                            kb.op("pe", lambda e, so=so, kun=kun, ksb=ksb, rn=rn, t1=t1, c=c, p=p, n=n, sl=sl: e.matmul(p[:, 0:n], lhsT=wr[0][:, c, sl], rhs=xr[:, c, 0:n],
                                                                                 start=(c == 0), stop=(c == 7)), reads=[wr[0], xr], writes=[p])
                        kb.op("act", lambda e, so=so, kun=kun, ksb=ksb, rn=rn, t1=t1, p=p, n=n, s=s: e.copy(out=so[0][:, 0:n], in_=p[:, 0:n]), reads=[p], writes=[so[0]])
                        p = self.psum()
                        for c in range(8):
                            kb.op("pe", lambda e, so=so, kun=kun, ksb=ksb, rn=rn, t1=t1, c=c, p=p, n=n, sl=sl: e.matmul(p[:, 0:n], lhsT=wr[1][:, c, sl], rhs=xk[:, c, 0:n],
                                                                                 start=(c == 0), stop=(c == 7)), reads=[wr[1], xk], writes=[p])
                        kb.op("act", lambda e, so=so, kun=kun, ksb=ksb, rn=rn, t1=t1, p=p, n=n: e.copy(out=ksb[:, 0:n], in_=p[:, 0:n]), reads=[p], writes=[ksb])
                        kb.op("dve", lambda e, so=so, kun=kun, ksb=ksb, rn=rn, t1=t1, n=n, s=s: e.tensor_scalar(out=kun[:, 0:n], in0=ksb[:, 0:n], scalar1=C["kkT"][:, s:s + 1],
                                                                         scalar2=None, op0=ALU.mult), reads=[ksb, C["kkT"]], writes=[kun])
                        kb.op("pool", lambda e, so=so, kun=kun, ksb=ksb, rn=rn, t1=t1, n=n: e.tensor_tensor(out=sqb[:, 0:n], in0=kun[:, 0:n], in1=kun[:, 0:n], op=ALU.mult),
                              reads=[kun], writes=[sqb])
                        p2 = self.psum()
                        kb.op("pe", lambda e, so=so, kun=kun, ksb=ksb, rn=rn, t1=t1, p2=p2, n=n: e.matmul(p2[:, 0:n], lhsT=C["blk"][:, :], rhs=sqb[:, 0:n], start=True, stop=True),
                              reads=[C["blk"], sqb], writes=[p2])
                        kb.op("act", lambda e, so=so, kun=kun, ksb=ksb, rn=rn, t1=t1, p2=p2, n=n: e.activation(out=rn[:, 0:n], in_=p2[:, 0:n], func=AF.Sqrt), reads=[p2], writes=[rn])
                        kb.op("dve", lambda e, so=so, kun=kun, ksb=ksb, rn=rn, t1=t1, n=n: e.tensor_scalar(out=rn[:, 0:n], in0=rn[:, 0:n], scalar1=1e-12, scalar2=None, op0=ALU.max),
                              reads=[rn], writes=[rn])
                        kb.op("dve", lambda e, so=so, kun=kun, ksb=ksb, rn=rn, t1=t1, n=n: e.reciprocal(out=rn[:, 0:n], in_=rn[:, 0:n]), reads=[rn], writes=[rn])
                        kb.op("dve", lambda e, so=so, kun=kun, ksb=ksb, rn=rn, t1=t1, n=n, s=s: e.tensor_tensor(out=so[1][:, 0:n], in0=kun[:, 0:n], in1=rn[:, 0:n], op=ALU.mult),
                              reads=[kun, rn], writes=[so[1]])
                        for z in range(2):
                            zs = slice(z * 64, (z + 1) * 64)
                            p = self.psum()
                            kb.op("pe", lambda e, so=so, kun=kun, ksb=ksb, rn=rn, t1=t1, p=p, n=n, sl=sl, zs=zs: e.matmul(p[:, 0:n], lhsT=w2c[zs, sl], rhs=tw[zs, 0:n], start=True, stop=True),
                                  reads=[w2c, tw], writes=[p])
                            kb.op("act", lambda e, so=so, kun=kun, ksb=ksb, rn=rn, t1=t1, p=p, n=n, s=s, z=z: e.activation(out=so[6 + z][:, 0:n], in_=p[:, 0:n], func=AF.Sigmoid,
                                                                                    bias=C["w0T"][:, z * 8 + s:z * 8 + s + 1], scale=1.0),
                                  reads=[p, C["w0T"]], writes=[so[6 + z]])
                            p = self.psum()
                            kb.op("pe", lambda e, so=so, kun=kun, ksb=ksb, rn=rn, t1=t1, p=p, n=n, sl=sl, zs=zs: e.matmul(p[:, 0:n], lhsT=a2c[zs, sl], rhs=ta[zs, 0:n], start=True, stop=True),
                                  reads=[a2c, ta], writes=[p])
                            kb.op("act", lambda e, so=so, kun=kun, ksb=ksb, rn=rn, t1=t1, p=p, n=n, s=s, z=z: e.activation(out=av[z][:, 0:n], in_=p[:, 0:n], func=AF.Sigmoid,
                                                                                    bias=C["a0T"][:, z * 8 + s:z * 8 + s + 1], scale=1.0),
                                  reads=[p, C["a0T"]], writes=[av[z]])
                            kb.op("dve", lambda e, so=so, kun=kun, ksb=ksb, rn=rn, t1=t1, n=n, s=s, z=z: e.tensor_scalar(out=t1[:, 0:n], in0=av[z][:, 0:n], scalar1=C["kaT"][:, s:s + 1],
                                                                                  scalar2=C["omka"][:, s:s + 1], op0=ALU.mult, op1=ALU.add),
                                  reads=[av[z], C["kaT"], C["omka"]], writes=[t1])
                            kb.op("pool", lambda e, so=so, kun=kun, ksb=ksb, rn=rn, t1=t1, n=n, s=s, z=z: e.tensor_tensor(out=so[2 + z][:, 0:n], in0=t1[:, 0:n], in1=ksb[:, 0:n], op=ALU.mult),
                                  reads=[t1, ksb], writes=[so[2 + z]])
                            kb.op("pool", lambda e, so=so, kun=kun, ksb=ksb, rn=rn, t1=t1, n=n, s=s, z=z: e.tensor_tensor(out=so[4 + z][:, 0:n], in0=so[1][:, 0:n], in1=av[z][:, 0:n], op=ALU.mult),
                                  reads=[so[1], av[z]], writes=[so[4 + z]])
                        kb.op("dve", lambda e, so=so, kun=kun, ksb=ksb, rn=rn, t1=t1, n=n, s=s: e.tensor_tensor(out=t1[:, 0:n], in0=so[2][:, 0:n], in1=so[3][:, 0:n], op=ALU.add),
                              reads=[so[2], so[3]], writes=[t1])
                        kb.op("dve", lambda e, so=so, kun=kun, ksb=ksb, rn=rn, t1=t1, n=n, s=s: e.scalar_tensor_tensor(out=rkp[:, s, 0:n], in0=t1[:, 0:n], scalar=C["rkT"][:, s:s + 1],
                                                                                in1=so[0][:, 0:n], op0=ALU.mult, op1=ALU.mult),
                              reads=[t1, so[0], C["rkT"]], writes=[rkp])
                        for q in range(8):
                            kb.dma("sp" if q % 2 == 0 else "act", S["fm"][q, s * 128:(s + 1) * 128, t0:t0 + n],
                                   so[q][:, 0:n], reads=[so[q]])
                    tl = tiles_of(n)
                    for ti, (o, cnt) in enumerate(tl):
                        for hf in range(2):
                            hs_ = slice(hf * 512, (hf + 1) * 512)
                            p = self.psum()
                            for c in range(8):
                                kb.op("pe", lambda e, c=c, p=p, o=o, cnt=cnt, hs_=hs_: e.matmul(p[:cnt, :], lhsT=xv[:, c, o:o + cnt], rhs=wr[2][:, c, hs_],
                                                                                                start=(c == 0), stop=(c == 7)), reads=[xv, wr[2]], writes=[p])
                            kb.op("act", lambda e, p=p, cnt=cnt, ti=ti, hs_=hs_: e.copy(out=vst[:cnt, ti % 2, hs_], in_=p[:cnt, :]), reads=[p], writes=[vst])
                            p = self.psum()
                            kb.op("pe", lambda e, p=p, o=o, cnt=cnt, hs_=hs_: e.matmul(p[:cnt, :], lhsT=sg1[:, o:o + cnt], rhs=g2a[:, hs_], start=True, stop=False),
                                  reads=[sg1, g2a], writes=[p])
                            kb.op("pe", lambda e, p=p, o=o, cnt=cnt, hs_=hs_: e.matmul(p[:cnt, :], lhsT=sg2[:, o:o + cnt], rhs=g2b[:, hs_], start=False, stop=True),
                                  reads=[sg2, g2b], writes=[p])
                            kb.op("dve", lambda e, p=p, cnt=cnt, ti=ti, hs_=hs_: e.tensor_copy(out=gst[:cnt, ti % 2, hs_], in_=p[:cnt, :]), reads=[p], writes=[gst])
                        p = self.psum()
                        for s in range(8):
                            kb.op("pe", lambda e, p=p, s=s, o=o, cnt=cnt: e.matmul(p[:cnt, 2 * s:2 * s + 2], lhsT=rkp[:, s, o:o + cnt], rhs=C["hsel"][:, :],
                                                                                   start=True, stop=True), reads=[rkp, C["hsel"]], writes=[p])
                        kb.op("dve", lambda e, p=p, cnt=cnt, ti=ti: e.tensor_copy(out=bst[:cnt, ti, :], in_=p[:cnt, 0:NH]), reads=[p], writes=[bst])
                        kb.dma("sp", S["v"][t0 + o:t0 + o + cnt, :], vst[:cnt, ti % 2, :], reads=[vst])
                        kb.dma("act", S["g"][t0 + o:t0 + o + cnt, :], gst[:cnt, ti % 2, :], reads=[gst])
                        kb.dma("sp", S["bonus"][t0 + o:t0 + o + cnt, :], bst[:cnt, ti, :], reads=[bst])
            kb.barrier()

    def phase_scan(self):
        kb, nc, I, C = self.kb, self.nc, self.I, self.C
        with contextlib.ExitStack() as st:
            sb = lambda name, shape, dt: self.sb(st, name, shape, dt)

            def ld(name, shape, dt, src, cast=False):
                b = sb("k_" + name, shape, dt)
                kb.dma("pool" if cast else "sp", b[:], src, writes=[b])
                C[name] = b
            ld("mask0", [128, 512], F32, I["c_mask"][0])
            ld("mask1", [128, 512], F32, I["c_mask"][1])
            ld("maskT0", [64, 1024], F32, I["c_maskT"][0])
            ld("maskT1", [64, 1024], F32, I["c_maskT"][1])
            ld("eye16", [64, 1024], BF16, I["c_eye16"], cast=True)
            ld("cm", [128, 512], F32, I["c_cm"])

            def mkstream(tag):
                B = {}
                B["BK"] = sb(tag + "BK", [128, 8, 128], BF16)
                B["AR"] = sb(tag + "AR", [128, 8, 128], BF16)
                B["BKp"] = [sb(tag + "BKp%d" % i, [128, 8, 128], BF16) for i in range(2)]
                B["ARp"] = [sb(tag + "ARp%d" % i, [128, 8, 128], BF16) for i in range(2)]
                B["vz"] = sb(tag + "vz", [128, 1024], BF16)
                for t_ in B["BKp"] + B["ARp"] + [B["vz"]]:
                    kb.op("pool", lambda e, t_=t_: e.memset(t_[:], 0.0), writes=[t_])
                B["BKT"] = sb(tag + "BKT", [128, 8, 128], BF16)
                B["M"] = sb(tag + "M", [128, 16, 128], BF16)
                B["A"] = sb(tag + "A", [64, 16, 64], BF16)
                B["AT"] = sb(tag + "AT", [64, 16, 64], BF16)
                B["X"] = sb(tag + "X", [64, 16, 64], BF16)
                B["Z"] = sb(tag + "Z", [64, 1024], BF16)
                B["vu"] = sb(tag + "vu", [128, 1024], BF16)
                B["o"] = sb(tag + "o", [64, 1024], F32)
                B["ST"] = sb(tag + "ST", [128, 8, 64], F32)
                B["STb"] = sb(tag + "STb", [128, 8, 64], BF16)
                B["PC"] = sb(tag + "PC", [128, 8], F32)
                return B

            def evac(i, fn_act, fn_dve, reads, writes):
                if i % 2 == 0:
                    kb.op("act", fn_act, reads=reads, writes=writes)
                else:
                    kb.op("dve", fn_dve, reads=reads, writes=writes)

            import os
            STOP = int(os.environ.get("SCAN_STOP", "9"))
            SUB = int(os.environ.get("SCAN_SUB", "9"))
            HACK = int(os.environ.get("HACK", "0"))

            free_banks = list(self.psb)

            def galloc(n):
                while len(free_banks) < n:
                    yield
                return [free_banks.pop(0) for _ in range(n)]

            def release(bs):
                free_banks.extend(bs)

            def stream(B, si, z):
                S = self.S[si]
                T = self.seqs[si][2]
                nch = -(-T // CH)
                mask = C["mask%d" % z]
                maskT = C["maskT%d" % z]
                kb.op("pool", lambda e: e.memset(B["ST"][:], 0.0), writes=[B["ST"]])
                kb.op("pool", lambda e: e.memset(B["STb"][:], 0.0), writes=[B["STb"]])
                order = range(nch) if z == 0 else range(nch - 1, -1, -1)
                for c in order:
                    c0 = c * CH
                    nt = min(CH, T - c0)
                    while not free_sets:
                        yield
                    Tt = free_sets.pop(0)
                    srcs = (("r", 0), ("kk", 1), ("k", 2 + z), ("b", 4 + z), ("sg", 6 + z))
                    for qi, (nm, q) in enumerate(srcs):
                        if nt < CH:
                            kb.op("pool", lambda e, nm=nm, Tt=Tt: e.memset(Tt[nm][:], 0.0), writes=[Tt[nm]])
                        kb.dma("sp" if qi % 2 == 0 else "act", Tt[nm][:, :, 0:nt],
                               S["fm"][q].rearrange("(s p) t -> p s t", p=128)[:, :, c0:c0 + nt], writes=[Tt[nm]])
                    if nt < CH:
                        kb.op("pool", lambda e: e.memset(B["vu"][64:128, :], 0.0), writes=[B["vu"]])
                    kb.dma("sp", B["vu"][64:64 + nt, :], S["v"][c0:c0 + nt, :], writes=[B["vu"]])
                    if nt < CH:
                        kb.op("pool", lambda e: e.memset(B["vz"][64:128, :], 0.0), writes=[B["vz"]])
                    kb.dma("act", B["vz"][64:64 + nt, :], S["v"][c0:c0 + nt, :], writes=[B["vz"]])
                    yield
                    cum2 = Tt["cum"][:].rearrange("p s t -> p (s t)")
                    sg2 = Tt["sg"][:].rearrange("p s t -> p (s t)")
                    kb.op("dve", lambda e, cum2=cum2, sg2=sg2: e.tensor_tensor_scan(out=cum2, data0=C["cm"][:, :], data1=sg2, initial=0.0,
                                                                op0=ALU.mult, op1=ALU.add), reads=[Tt["sg"], C["cm"]], writes=[Tt["cum"]])
                    if z == 0:
                        E1 = Tt["cum"]
                        last = CH - 1
                    else:
                        E1 = Tt["e1"]
                        last = 0
                        kb.op("pool", lambda e, Tt=Tt: e.tensor_tensor(out=Tt["e1"][:], in0=Tt["sg"][:], in1=Tt["cum"][:], op=ALU.subtract),
                              reads=[Tt["sg"], Tt["cum"]], writes=[Tt["e1"]])
                        for s_ in range(8):
                            kb.op("dve", lambda e, s_=s_, Tt=Tt: e.tensor_scalar(out=Tt["e1"][:, s_, :], in0=Tt["e1"][:, s_, :],
                                                                         scalar1=Tt["cum"][:, s_, CH - 1:CH], scalar2=None, op0=ALU.add),
                                  reads=[Tt["e1"], Tt["cum"]], writes=[Tt["e1"]])
                    kb.op("act", lambda e, Tt=Tt, E1=E1: e.activation(out=Tt["eP"][:], in_=E1[:], func=AF.Exp, scale=-LWC), reads=[E1], writes=[Tt["eP"]])
                    kb.op("act", lambda e, Tt=Tt, E1=E1: e.activation(out=Tt["eN"][:], in_=E1[:], func=AF.Exp, scale=LWC), reads=[E1], writes=[Tt["eN"]])
                    kb.op("pool", lambda e, Tt=Tt, E1=E1: e.tensor_tensor(out=Tt["eA"][:], in0=E1[:], in1=Tt["sg"][:], op=ALU.subtract),
                          reads=[E1, Tt["sg"]], writes=[Tt["eA"]])
                    kb.op("act", lambda e, Tt=Tt: e.activation(out=Tt["eA"][:], in_=Tt["eA"][:], func=AF.Exp, scale=-LWC), reads=[Tt["eA"]], writes=[Tt["eA"]])
                    kb.op("dve", lambda e, Tt=Tt: e.scalar_tensor_tensor(out=B["AR"][:, :, 0:64], in0=Tt["kk"][:], scalar=-1.0, in1=Tt["eA"][:],
                                                                  op0=ALU.mult, op1=ALU.mult), reads=[Tt["kk"], Tt["eA"]], writes=[B["AR"]])
                    kb.op("pool", lambda e, Tt=Tt: e.tensor_tensor(out=B["AR"][:, :, 64:128], in0=Tt["r"][:], in1=Tt["eP"][:], op=ALU.mult),
                          reads=[Tt["r"], Tt["eP"]], writes=[B["AR"]])
                    kb.op("dve", lambda e, Tt=Tt: e.tensor_tensor(out=B["BK"][:, :, 0:64], in0=Tt["b"][:], in1=Tt["eN"][:], op=ALU.mult),
                          reads=[Tt["b"], Tt["eN"]], writes=[B["BK"]])
                    kb.op("pool", lambda e, Tt=Tt: e.tensor_tensor(out=B["BK"][:, :, 64:128], in0=Tt["k"][:], in1=Tt["eN"][:], op=ALU.mult),
                          reads=[Tt["k"], Tt["eN"]], writes=[B["BK"]])
                    kb.op("dve", lambda e, Tt=Tt, last=last: e.tensor_copy(out=B["PC"][:, :], in_=Tt["eP"][:, :, last]), reads=[Tt["eP"]], writes=[B["PC"]])
                    for hp_ in range(2):
                        pq = slice(hp_ * 64, hp_ * 64 + 64)
                        kb.op("pool", lambda e, hp_=hp_, pq=pq: e.tensor_copy(out=B["BKp"][hp_][pq, :, :], in_=B["BK"][pq, :, :]), reads=[B["BK"]], writes=[B["BKp"][hp_]])
                        kb.op("act", lambda e, hp_=hp_, pq=pq: e.copy(out=B["ARp"][hp_][pq, :, :], in_=B["AR"][pq, :, :]), reads=[B["AR"]], writes=[B["ARp"][hp_]])
                    free_sets.append(Tt)
                    yield
                    if STOP <= 1:
                        continue
                    pall = yield from galloc(6)
                    pms = pall[0:4]
                    for h in range(16):
                        s_, hp = h // 2, h % 2
                        pm = pms[h // 4]
                        kb.op("pe", lambda e, pm=pm, h=h, s_=s_, hp=hp: e.matmul(pm[:, (h % 4) * 128:(h % 4 + 1) * 128], lhsT=B["BKp"][hp][:, s_, :],
                                                                               rhs=B["AR"][:, s_, :], start=True, stop=True),
                              reads=[B["BKp"][hp], B["AR"]], writes=[pm])
                    pts = pall[4:6]
                    for h in range(16):
                        s_, hp = h // 2, h % 2
                        pm = pts[h // 8]
                        kb.op("pe", lambda e, pm=pm, h=h, s_=s_, hp=hp: e.matmul(pm[0:64, (h % 8) * 64:(h % 8 + 1) * 64], lhsT=B["ARp"][hp][:, s_, 0:64],
                                                                               rhs=B["BK"][:, s_, 0:64], start=True, stop=True),
                              reads=[B["BK"], B["ARp"][hp]], writes=[pm])
                    yield
                    for g in range(4):
                        kb.op("dve", lambda e, g=g, pms=pms: e.tensor_tensor(out=B["M"][:, g * 4:(g + 1) * 4, :].rearrange("p h t -> p (h t)"),
                                                                         in0=pms[g][:, :], in1=mask[:, :], op=ALU.mult), reads=[pms[g], mask], writes=[B["M"]])
                    for g in range(2):
                        kb.op("dve", lambda e, g=g, pts=pts: e.tensor_tensor(out=B["AT"][:, g * 8:(g + 1) * 8, :].rearrange("p h t -> p (h t)"),
                                                                           in0=pts[g][0:64, :], in1=maskT[:, g * 512:(g + 1) * 512], op=ALU.mult),
                              reads=[pts[g], maskT], writes=[B["AT"]])
                    release(pall)
                    ptt = yield from galloc(2)
                    for s_ in range(8):
                        kb.op("pe", lambda e, s_=s_, ptt=ptt: e.matmul(ptt[s_ // 4][:, (s_ % 4) * 128:(s_ % 4 + 1) * 128], lhsT=B["BK"][:, s_, :], rhs=C["ident"][:, :],
                                                                       start=True, stop=True), reads=[B["BK"], C["ident"]], writes=[ptt[s_ // 4]])
                    yield
                    kb.op("act", lambda e: e.copy(out=B["A"][:], in_=B["M"][0:64, :, 0:64]), reads=[B["M"]], writes=[B["A"]])
                    for g in range(2):
                        kb.op("act", lambda e, g=g, ptt=ptt: e.copy(out=B["BKT"][:, g * 4:(g + 1) * 4, :].rearrange("p s t -> p (s t)"), in_=ptt[g][:, :]),
                              reads=[ptt[g]], writes=[B["BKT"]])
                    release(ptt)
                    yield
                    kb.op("pool", lambda e: e.tensor_tensor(out=B["X"][:].rearrange("p h t -> p (h t)"), in0=B["A"][:].rearrange("p h t -> p (h t)"),
                                                            in1=C["eye16"][:, :], op=ALU.add), reads=[B["A"], C["eye16"]], writes=[B["X"]])
                    for lvl in range(5):
                        lastl = (lvl == 4)
                        pab = yield from galloc(2 if lastl else 4)
                        pa = pab[0:2]
                        for h in range(16):
                            kb.op("pe", lambda e, h=h, pa=pa: e.matmul(pa[h // 8][0:64, (h % 8) * 64:(h % 8 + 1) * 64], lhsT=B["A"][:, h, :], rhs=B["AT"][:, h, :],
                                                                       start=True, stop=True), reads=[B["A"], B["AT"]], writes=[pa[h // 8]])
                        if not lastl:
                            pb = pab[2:4]
                            for h in range(16):
                                kb.op("pe", lambda e, h=h, pb=pb: e.matmul(pb[h // 8][0:64, (h % 8) * 64:(h % 8 + 1) * 64], lhsT=B["AT"][:, h, :], rhs=B["A"][:, h, :],
                                                                           start=True, stop=True), reads=[B["A"], B["AT"]], writes=[pb[h // 8]])
                        yield
                        for g in range(2):
                            kb.op("act", lambda e, g=g, pa=pa: e.copy(out=B["AT"][:, g * 8:(g + 1) * 8, :].rearrange("p h t -> p (h t)"), in_=pa[g][0:64, :]),
                                  reads=[pa[g]], writes=[B["AT"]])
                        if not lastl:
                            for g in range(2):
                                kb.op("dve", lambda e, g=g, pb=pb: e.tensor_copy(out=B["A"][:, g * 8:(g + 1) * 8, :].rearrange("p h t -> p (h t)"), in_=pb[g][0:64, :]),
                                      reads=[pb[g]], writes=[B["A"]])
                        release(pab)
                        yield
                        px = yield from galloc(2)
                        for h in range(16):
                            kb.op("pe", lambda e, h=h, px=px: e.matmul(px[h // 8][0:64, (h % 8) * 64:(h % 8 + 1) * 64], lhsT=B["AT"][:, h, :], rhs=B["X"][:, h, :],
                                                                       start=True, stop=True), reads=[B["AT"], B["X"]], writes=[px[h // 8]])
                        yield
                        for g in range(2):
                            kb.op("dve", lambda e, g=g, px=px: e.tensor_tensor(out=B["X"][:, g * 8:(g + 1) * 8, :].rearrange("p h t -> p (h t)"),
                                                                               in0=px[g][0:64, :], in1=B["X"][:, g * 8:(g + 1) * 8, :].rearrange("p h t -> p (h t)"),
                                                                               op=ALU.add), reads=[px[g], B["X"]], writes=[B["X"]])
                        release(px)
                        yield
                    pz = yield from galloc(2)
                    for h in range(16):
                        s_, hp = h // 2, h % 2
                        oz = pz[h // 8][0:64, (h % 8) * 64:(h % 8 + 1) * 64]
                        kb.op("pe", lambda e, oz=oz, s_=s_, hp=hp: e.matmul(oz, lhsT=B["ARp"][hp][:, s_, 0:64], rhs=B["STb"][:, s_, :], start=True, stop=False),
                              reads=[B["ARp"][hp], B["STb"]], writes=[pz[h // 8]])
                        kb.op("pe", lambda e, oz=oz, h=h: e.matmul(oz, lhsT=B["M"][:, h, 0:64], rhs=B["vz"][:, h * 64:(h + 1) * 64], start=False, stop=True),
                              reads=[B["M"], B["vz"]], writes=[pz[h // 8]])
                    yield
                    for g in range(2):
                        evac(g, lambda e, g=g, pz=pz: e.copy(out=B["Z"][:, g * 512:(g + 1) * 512], in_=pz[g][0:64, :]),
                             lambda e, g=g, pz=pz: e.tensor_copy(out=B["Z"][:, g * 512:(g + 1) * 512], in_=pz[g][0:64, :]), [pz[g]], [B["Z"]])
                    release(pz)
                    yield
                    pu = yield from galloc(2)
                    for h in range(16):
                        kb.op("pe", lambda e, h=h, pu=pu: e.matmul(pu[h // 8][0:64, (h % 8) * 64:(h % 8 + 1) * 64], lhsT=B["X"][:, h, :], rhs=B["Z"][:, h * 64:(h + 1) * 64],
                                                            start=True, stop=True), reads=[B["X"], B["Z"]], writes=[pu[h // 8]])
                    yield
                    for g in range(2):
                        evac(g, lambda e, g=g, pu=pu: e.copy(out=B["vu"][0:64, g * 512:(g + 1) * 512], in_=pu[g][0:64, :]),
                             lambda e, g=g, pu=pu: e.tensor_copy(out=B["vu"][0:64, g * 512:(g + 1) * 512], in_=pu[g][0:64, :]), [pu[g]], [B["vu"]])
                    release(pu)
                    yield
                    pod = yield from galloc(4)
                    po = pod[0:2]
                    for h in range(16):
                        s_, hp = h // 2, h % 2
                        oo = po[h // 8][0:64, (h % 8) * 64:(h % 8 + 1) * 64]
                        kb.op("pe", lambda e, oo=oo, s_=s_, hp=hp: e.matmul(oo, lhsT=B["ARp"][hp][:, s_, 64:128], rhs=B["STb"][:, s_, :], start=True, stop=False),
                              reads=[B["ARp"][hp], B["STb"]], writes=[po[h // 8]])
                        kb.op("pe", lambda e, oo=oo, h=h: e.matmul(oo, lhsT=B["M"][:, h, 64:128], rhs=B["vu"][:, h * 64:(h + 1) * 64], start=False, stop=True),
                              reads=[B["M"], B["vu"]], writes=[po[h // 8]])
                    pd = pod[2:4]
                    for s_ in range(8):
                        kb.op("pe", lambda e, s_=s_, pd=pd: e.matmul(pd[s_ // 4][:, (s_ % 4) * 128:(s_ % 4 + 1) * 128], lhsT=B["BKT"][:, s_, :], rhs=B["vu"][:, s_ * 128:(s_ + 1) * 128],
                                                              start=True, stop=True), reads=[B["BKT"], B["vu"]], writes=[pd[s_ // 4]])
                    yield
                    for g in range(2):
                        kb.op("act", lambda e, g=g, po=po: e.copy(out=B["o"][:, g * 512:(g + 1) * 512], in_=po[g][0:64, :]), reads=[po[g]], writes=[B["o"]])
                    kb.dma("sp", S["o"][z, c0:c0 + nt, :], B["o"][0:nt, :], reads=[B["o"]])
                    for g in range(2):
                        pv = pd[g][:, :].rearrange("p (s t) -> p s t", t=128)
                        for hp in range(2):
                            ps_ = slice(hp * 64, hp * 64 + 64)
                            kb.op("dve", lambda e, g=g, pv=pv, hp=hp, ps_=ps_: e.tensor_tensor(out=B["ST"][ps_, g * 4:(g + 1) * 4, :], in0=pv[ps_, :, hp * 64:(hp + 1) * 64],
                                                                                             in1=B["ST"][ps_, g * 4:(g + 1) * 4, :], op=ALU.add),
                                  reads=[pd[g], B["ST"]], writes=[B["ST"]])
                    release(pod)
                    yield
                    for s_ in range(8):
                        kb.op("pool" if s_ % 2 else "dve", lambda e, s_=s_: e.tensor_scalar(out=B["ST"][:, s_, :], in0=B["ST"][:, s_, :], scalar1=B["PC"][:, s_:s_ + 1], scalar2=None, op0=ALU.mult),
                              reads=[B["ST"], B["PC"]], writes=[B["ST"]])
                    yield
                    kb.op("act", lambda e: e.copy(out=B["STb"][:], in_=B["ST"][:]), reads=[B["ST"]], writes=[B["STb"]])
                    yield

            NS = 4
            Bs = [mkstream("s%d_" % i) for i in range(NS)]
            free_sets = []
            for i in range(2):
                free_sets.append({nm: sb("t%d_%s" % (i, nm), [128, 8, 64], F32)
                                  for nm in ("r", "kk", "k", "b", "sg", "cum", "e1", "eP", "eN", "eA")})
            todo = [(si, z) for si in range(len(self.seqs)) for z in range(2)]
            active = [None] * NS
            while todo or any(a is not None for a in active):
                for i in range(NS):
                    if active[i] is None and todo:
                        si, z = todo.pop(0)
                        active[i] = stream(Bs[i], si, z)
                    if active[i] is not None:
                        try:
                            next(active[i])
                        except StopIteration:
                            active[i] = None
            kb.barrier()

    def phase3(self):
        self.phase3a()
        self.ffn_pass(0)

    def phase4(self):
        self.phase4a()
        self.ffn_pass(1)

    def phase3a(self):
        kb, nc, I, C = self.kb, self.nc, self.I, self.C
        with contextlib.ExitStack() as st:
            sb = lambda name, shape, dt: self.sb(st, "p3_" + name, shape, dt)
            gnw = self.rowvec(st, "gnw", I["gn_w"][0:1, :])
            gnb = self.rowvec(st, "gnb", I["gn_b"][0:1, :])
            wo = sb("wo", [128, 8, D], BF16)
            for c in range(8):
                kb.dma("pool", wo[:, c, :], I["w_o"][c * 128:(c + 1) * 128, :], writes=[wo])
            NB = 4
            sets = []
            for i in range(NB):
                sets.append(dict(a=sb("o0_%d" % i, [128, D], F32), b=sb("o1_%d" % i, [128, D], F32), v=sb("v_%d" % i, [128, D], BF16),
                                 g=sb("g_%d" % i, [128, D], BF16), bn=sb("b_%d" % i, [128, NH], F32), x=sb("x_%d" % i, [128, D], F32),
                                 s=sb("st_%d" % i, [128, 6, NH], F32), h=sb("h1_%d" % i, [128, D], F32),
                                 ogb=sb("ogb%d" % i, [128, D], BF16), ogT=sb("ogT%d" % i, [128, 8, 128], BF16)))
                kb.op("pool", lambda e, t_=sets[i]["ogb"]: e.memset(t_[:], 0.0), writes=[sets[i]["ogb"]])
            free_banks = list(self.psb)

            def galloc(nb_):
                while len(free_banks) < nb_:
                    yield
                return [free_banks.pop(0) for _ in range(nb_)]

            def tile_gen(Q, si, t0, n):
                S = self.S[si]
                a, b_, v_, g_, bn, x_, s_, h_, ogb, ogT = Q["a"], Q["b"], Q["v"], Q["g"], Q["bn"], Q["x"], Q["s"], Q["h"], Q["ogb"], Q["ogT"]
                kb.dma("sp", a[:n, :], S["o"][0, t0:t0 + n, :], writes=[a])
                kb.dma("act", b_[:n, :], S["o"][1, t0:t0 + n, :], writes=[b_])
                kb.dma("sp", v_[:n, :], S["v"][t0:t0 + n, :], writes=[v_])
                kb.dma("act", g_[:n, :], S["g"][t0:t0 + n, :], writes=[g_])
                kb.dma("sp", bn[:n, :], S["bonus"][t0:t0 + n, :], writes=[bn])
                for (src, ro, nr) in self.xrows(si, t0, n):
                    kb.dma("act", x_[ro:ro + nr, :], src, writes=[x_])
                yield
                kb.op("pool", lambda e: e.tensor_tensor(out=a[:n, :], in0=a[:n, :], in1=b_[:n, :], op=ALU.add), reads=[b_], writes=[a])
                kb.op("pool", lambda e: e.tensor_tensor(out=b_[:n, :], in0=a[:n, :], in1=a[:n, :], op=ALU.mult), reads=[a], writes=[b_])
                yield
                a3 = a[:n, :].rearrange("p (h j) -> p h j", j=HS)
                b3 = b_[:n, :].rearrange("p (h j) -> p h j", j=HS)
                kb.op("dve", lambda e: e.reduce_sum(out=s_[:n, 0, :], in_=a3, axis=AX.X), reads=[a], writes=[s_])
                kb.op("dve", lambda e: e.reduce_sum(out=s_[:n, 1, :], in_=b3, axis=AX.X), reads=[b_], writes=[s_])
                kb.op("dve", lambda e: e.tensor_scalar(out=s_[:n, 2, :], in0=s_[:n, 0, :], scalar1=1.0 / HS, scalar2=None, op0=ALU.mult), reads=[s_], writes=[s_])
                kb.op("dve", lambda e: e.tensor_tensor(out=s_[:n, 3, :], in0=s_[:n, 2, :], in1=s_[:n, 2, :], op=ALU.mult), reads=[s_], writes=[s_])
                kb.op("dve", lambda e: e.scalar_tensor_tensor(out=s_[:n, 4, :], in0=s_[:n, 1, :], scalar=1.0 / HS, in1=s_[:n, 3, :], op0=ALU.mult, op1=ALU.subtract), reads=[s_], writes=[s_])
                kb.op("dve", lambda e: e.tensor_scalar(out=s_[:n, 4, :], in0=s_[:n, 4, :], scalar1=64e-5, scalar2=None, op0=ALU.add), reads=[s_], writes=[s_])
                yield
                kb.op("act", lambda e: e.activation(out=s_[:n, 5, :], in_=s_[:n, 4, :], func=AF.Sqrt), reads=[s_], writes=[s_])
                yield
                kb.op("dve", lambda e: e.reciprocal(out=s_[:n, 5, :], in_=s_[:n, 5, :]), reads=[s_], writes=[s_])
                for h in range(NH):
                    hs_ = slice(h * HS, (h + 1) * HS)
                    kb.op("dve" if h % 2 == 0 else "pool", lambda e, h=h, hs_=hs_: e.tensor_scalar(
                        out=a[:n, hs_], in0=a[:n, hs_], scalar1=s_[:n, 2, h:h + 1], scalar2=s_[:n, 5, h:h + 1],
                        op0=ALU.subtract, op1=ALU.mult), reads=[a, s_], writes=[a])
                yield
                kb.op("pool", lambda e: e.tensor_tensor(out=a[:n, :], in0=a[:n, :], in1=gnw[:n, :], op=ALU.mult), reads=[gnw], writes=[a])
                kb.op("pool", lambda e: e.tensor_tensor(out=a[:n, :], in0=a[:n, :], in1=gnb[:n, :], op=ALU.add), reads=[gnb], writes=[a])
                yield
                for h in range(NH):
                    hs_ = slice(h * HS, (h + 1) * HS)
                    kb.op("dve", lambda e, h=h, hs_=hs_: e.scalar_tensor_tensor(
                        out=a[:n, hs_], in0=v_[:n, hs_], scalar=bn[:n, h:h + 1], in1=a[:n, hs_], op0=ALU.mult, op1=ALU.add),
                        reads=[v_, bn], writes=[a])
                yield
                kb.op("pool", lambda e: e.tensor_tensor(out=ogb[:n, :], in0=a[:n, :], in1=g_[:n, :], op=ALU.mult), reads=[a, g_], writes=[ogb])
                yield
                (pt,) = yield from galloc(1)
                ptb = pt.t[:].bitcast(BF16)
                for c in range(8):
                    kb.op("pe", lambda e, c=c: e.transpose(out=ptb[:, c * 128:c * 128 + n], in_=ogb[:n, c * 128:(c + 1) * 128],
                                                           identity=C["ident"][:n, :n]), reads=[ogb, C["ident"]], writes=[pt])
                yield
                src3 = ptb.rearrange("p (c t) -> p c t", t=128)
                kb.op("act", lambda e: e.copy(out=ogT[:, 0:8, 0:n], in_=src3[:, 0:8, 0:n]), reads=[pt], writes=[ogT])
                free_banks.append(pt)
                yield
                pp = yield from galloc(2)
                for hf in range(2):
                    hs_ = slice(hf * 512, (hf + 1) * 512)
                    for c in range(8):
                        kb.op("pe", lambda e, hf=hf, c=c, hs_=hs_: e.matmul(pp[hf][:n, :], lhsT=ogT[:, c, 0:n], rhs=wo[:, c, hs_], start=(c == 0), stop=(c == 7)),
                              reads=[ogT, wo], writes=[pp[hf]])
                yield
                for hf in range(2):
                    hs_ = slice(hf * 512, (hf + 1) * 512)
                    kb.op("dve", lambda e, hf=hf, hs_=hs_: e.tensor_tensor(out=h_[:n, hs_], in0=pp[hf][:n, :], in1=x_[:n, hs_], op=ALU.add),
                          reads=[pp[hf], x_], writes=[h_])
                free_banks.extend(pp)
                kb.dma("sp", S["h1"][t0:t0 + n, :], h_[:n, :], reads=[h_])
                yield

            todo = [(si, t0, n) for si, (kind, bi, T) in enumerate(self.seqs) for (t0, n) in tiles_of(T)]
            active = [None] * NB
            while todo or any(x is not None for x in active):
                for i in range(NB):
                    if active[i] is None and todo:
                        si, t0, n = todo.pop(0)
                        active[i] = tile_gen(sets[i], si, t0, n)
                    if active[i] is not None:
                        try:
                            next(active[i])
                        except StopIteration:
                            active[i] = None
            kb.barrier()

    def ffn_pass(self, l):
        kb, nc, I, C = self.kb, self.nc, self.I, self.C
        NF = DFF // 128
        with contextlib.ExitStack() as st:
            sb = lambda name, shape, dt: self.sb(st, "f%d_" % l + name, shape, dt)
            nf = self.rowvec(st, "nf%d" % l, I["norm_ffn"][l:l + 1, :])
            if l == 0:
                n2 = self.rowvec(st, "nm1", I["norm_mix"][1:2, :])
                dftc = sb("dftc", [128, 256], BF16)
                kb.dma("pool", dftc[:], I["c_dftc"], writes=[dftc])
            else:
                n2 = self.rowvec(st, "nfin", I["norm_final"][0:1, :])
            win = sb("win", [128, 8, 2 * DFF], BF16)
            for c in range(8):
                kb.dma("pool", win[:, c, :], I["ffn_w_in"][l, c * 128:(c + 1) * 128, :], writes=[win])
            wout = sb("wout", [128, NF, D], BF16)
            for f in range(NF):
                kb.dma("pool", wout[:, f, :], I["ffn_w_out"][l, f * 128:(f + 1) * 128, :], writes=[wout])
            cw = sb("cw", [128, 3 * NF], F32)
            cb = sb("cb", [128, NF], F32)
            for k in range(3):
                kb.dma("sp", cw[:, k * NF:(k + 1) * NF], I["conv_w"][l, k].rearrange("(f p) -> p f", p=128), writes=[cw])
            kb.dma("sp", cb[:, :], I["conv_b"][l].rearrange("(f p) -> p f", p=128), writes=[cb])
            ht = [sb("ht%d" % i, [128, D], F32) for i in range(4)]
            ssb = sb("ssb", [128, 4], F32)
            hnb = sb("hnb", [128, D], BF16)
            junk = hnb
            hnT = sb("hnT", [128, 8, 512], BF16)
            uaL = [sb("ua%d" % i, [128, 512], F32) for i in range(2)]
            tmpL = [sb("tmp%d" % i, [128, 512], F32) for i in range(2)]
            slL = [sb("sl%d" % i, [128, 512], F32) for i in range(2)]
            zT = sb("zT", [128, NF, 512], BF16)
            ot = sb("ot", [128, 2, D], BF16) if l == 0 else sb("ot", [128, 1, D], F32)
            kb.op("pool", lambda e: e.memset(zT[:], 0.0), writes=[zT])
            kb.op("pool", lambda e: e.memset(hnb[:], 0.0), writes=[hnb])
            for t_ in ht:
                kb.op("pool", lambda e, t_=t_: e.memset(t_[:], 0.0), writes=[t_])
            for si, (kind, bi, T) in enumerate(self.seqs):
                S = self.S[si]
                Hs = S["h1"] if l == 0 else S["h3"]
                yout = self.y_p if kind == "p" else self.y_s
                for (t0, nb) in blocks_of(T):
                    ws, we = t0 - 1, t0 + nb + 1
                    W = we - ws
                    n = nb
                    wt = tiles_of(W)
                    for ti, (o, cnt) in enumerate(wt):
                        a, b = ws + o, ws + o + cnt
                        a2_, b2_ = max(a, 0), min(b, T)
                        kb.dma("sp" if ti % 2 == 0 else "act", ht[ti][a2_ - a:b2_ - a, :], Hs[a2_:b2_, :], writes=[ht[ti]])
                        self.norm_transpose(ws, ht[ti], cnt, nf, hnT, o, junk, ssb, hnb)
                    if ws < 0:
                        kb.op("pool", lambda e: e.memset(hnT[:, :, 0:1], 0.0), writes=[hnT])
                    if we > T:
                        kb.op("pool", lambda e, W=W: e.memset(hnT[:, :, W - 1:W], 0.0), writes=[hnT])
                    pls = {}
                    for i in range(NF + 3):
                        if i < NF:
                            f = i
                            fa = slice(f * 128, (f + 1) * 128)
                            fl = slice(DFF + f * 128, DFF + (f + 1) * 128)
                            pa = self.psum()
                            for c in range(8):
                                kb.op("pe", lambda e, pa=pa, c=c, fa=fa, W=W: e.matmul(pa[:, 0:W], lhsT=win[:, c, fa], rhs=hnT[:, c, 0:W], start=(c == 0), stop=(c == 7)),
                                      reads=[win, hnT], writes=[pa])
                            pl = self.psum()
                            pls[f] = pl
                            for c in range(8):
                                kb.op("pe", lambda e, pl=pl, c=c, fl=fl, W=W: e.matmul(pl[:, 0:W], lhsT=win[:, c, fl], rhs=hnT[:, c, 0:W], start=(c == 0), stop=(c == 7)),
                                      reads=[win, hnT], writes=[pl])
                        if 2 <= i <= NF + 1:
                            f = i - 2
                            tmp, sl = tmpL[f % 2], slL[f % 2]
                            kb.op("act", lambda e, f=f, n=n, tmp=tmp, sl=sl: e.activation(out=sl[:, 0:n], in_=tmp[:, 0:n], func=AF.Silu, bias=cb[:, f:f + 1], scale=1.0),
                                  reads=[tmp, cb], writes=[sl])
                        if i < NF:
                            ua = uaL[i % 2]
                            kb.op("act", lambda e, pa=pa, W=W, ua=ua: e.copy(out=ua[:, 0:W], in_=pa[:, 0:W]), reads=[pa], writes=[ua])
                        if 3 <= i <= NF + 2:
                            f = i - 3
                            sl = slL[f % 2]
                            pl_ = pls.pop(f)
                            kb.op("dve", lambda e, f=f, n=n, pl_=pl_, sl=sl: e.tensor_tensor(out=zT[:, f, 1:n + 1], in0=pl_[:, 1:n + 1], in1=sl[:, 0:n], op=ALU.mult),
                                  reads=[pl_, sl], writes=[zT])
                        if 1 <= i <= NF:
                            f = i - 1
                            ua, tmp = uaL[f % 2], tmpL[f % 2]
                            kb.op("dve", lambda e, f=f, n=n, ua=ua, tmp=tmp: e.tensor_scalar(out=tmp[:, 0:n], in0=ua[:, 0:n], scalar1=cw[:, f:f + 1], scalar2=None, op0=ALU.mult),
                                  reads=[ua, cw], writes=[tmp])
                            kb.op("dve", lambda e, f=f, n=n, ua=ua, tmp=tmp: e.scalar_tensor_tensor(out=tmp[:, 0:n], in0=ua[:, 1:n + 1], scalar=cw[:, NF + f:NF + f + 1], in1=tmp[:, 0:n],
                                                                                    op0=ALU.mult, op1=ALU.add), reads=[ua, cw], writes=[tmp])
                            kb.op("dve", lambda e, f=f, n=n, ua=ua, tmp=tmp: e.scalar_tensor_tensor(out=tmp[:, 0:n], in0=ua[:, 2:n + 2], scalar=cw[:, 2 * NF + f:2 * NF + f + 1], in1=tmp[:, 0:n],
                                                                                    op0=ALU.mult, op1=ALU.add), reads=[ua, cw], writes=[tmp])
                    for ti, (o, cnt) in enumerate(wt):
                        h_ = ht[ti]
                        for hf in range(2):
                            hs_ = slice(hf * 512, (hf + 1) * 512)
                            py = self.psum()
                            for f in range(NF):
                                kb.op("pe", lambda e, py=py, f=f, o=o, cnt=cnt, hs_=hs_: e.matmul(py[:cnt, :], lhsT=zT[:, f, o:o + cnt], rhs=wout[:, f, hs_],
                                                                                                  start=(f == 0), stop=(f == NF - 1)), reads=[zT, wout], writes=[py])
                            kb.op("dve", lambda e, py=py, cnt=cnt, hs_=hs_, h_=h_: e.tensor_tensor(out=h_[:cnt, hs_], in0=py[:cnt, :], in1=h_[:cnt, hs_], op=ALU.add),
                                  reads=[py], writes=[h_])
                        lo = max(o, 1) - o
                        hi = min(o + cnt, W - 1) - o
                        tok0 = ws + o + lo
                        if l == 0:
                            if hi > lo:
                                kb.dma("sp", S["h2"][tok0:tok0 + hi - lo, :], h_[lo:hi, :], reads=[h_])
                            self.norm_transpose(ws, h_, cnt, n2, hnT, o, junk, ssb, hnb)
                            ob = ot
                            for gp in range(4):
                                pd = self.psum()
                                for gg in range(2):
                                    g = gp * 2 + gg
                                    kb.op("pe", lambda e, pd=pd, g=g, gg=gg, o=o, cnt=cnt: e.matmul(pd[:cnt, gg * 256:(gg + 1) * 256], lhsT=hnT[:, g, o:o + cnt], rhs=dftc[:, :],
                                                                                                    start=True, stop=True), reads=[hnT, dftc], writes=[pd])
                                pd4 = pd[:cnt, :].rearrange("p (g k c) -> p g k c", g=2, k=2)
                                for k in range(2):
                                    dst = ob[:cnt, k, gp * 256:(gp + 1) * 256].rearrange("p (g c) -> p g c", g=2)
                                    if k == 0:
                                        kb.op("act", lambda e, dst=dst, pd4=pd4, k=k: e.copy(out=dst, in_=pd4[:, :, k, :]), reads=[pd], writes=[ob])
                                    else:
                                        kb.op("dve", lambda e, dst=dst, pd4=pd4, k=k: e.tensor_copy(out=dst, in_=pd4[:, :, k, :]), reads=[pd], writes=[ob])
                            if hi > lo:
                                kb.dma("sp", S["yc"][0, tok0:tok0 + hi - lo, :], ob[lo:hi, 0, :], reads=[ob])
                                kb.dma("act", S["yc"][1, tok0:tok0 + hi - lo, :], ob[lo:hi, 1, :], reads=[ob])
                        else:
                            kb.op("dve", lambda e, h_=h_, cnt=cnt: e.scalar_tensor_tensor(out=junk[:cnt, :], in0=h_[:cnt, :], scalar=1.0, in1=h_[:cnt, :],
                                                                                         op0=ALU.mult, op1=ALU.mult, accum_out=ssb[:cnt, 0:1]), reads=[h_], writes=[junk, ssb])
                            kb.op("dve", lambda e, cnt=cnt: e.tensor_scalar(out=ssb[:cnt, 1:2], in0=ssb[:cnt, 0:1], scalar1=1.0 / D, scalar2=1e-6, op0=ALU.mult, op1=ALU.add),
                                  reads=[ssb], writes=[ssb])
                            kb.op("act", lambda e, cnt=cnt: e.activation(out=ssb[:cnt, 2:3], in_=ssb[:cnt, 1:2], func=AF.Sqrt), reads=[ssb], writes=[ssb])
                            kb.op("dve", lambda e, cnt=cnt: e.reciprocal(out=ssb[:cnt, 3:4], in_=ssb[:cnt, 2:3]), reads=[ssb], writes=[ssb])
                            kb.op("dve", lambda e, h_=h_, cnt=cnt: e.scalar_tensor_tensor(out=ot[:cnt, 0, :], in0=h_[:cnt, :], scalar=ssb[:cnt, 3:4], in1=n2[:cnt, :],
                                                                                         op0=ALU.mult, op1=ALU.mult), reads=[h_, ssb, n2], writes=[ot])
                            lo2 = max(lo, NMETA - (ws + o))
                            if hi > lo2:
                                tk = ws + o + lo2 - NMETA
                                kb.dma("sp", yout[bi, tk:tk + hi - lo2, :], ot[lo2:hi, 0, :], reads=[ot])
            kb.barrier()

    def phase4a(self):
        kb, nc, I, C = self.kb, self.nc, self.I, self.C
        NTmax = -(-max(T for _, _, T in self.seqs) // 128)
        with contextlib.ExitStack() as st:
            sb = lambda name, shape, dt: self.sb(st, "p4_" + name, shape, dt)
            wf = sb("wf", [128, 8, D], BF16)
            for c in range(8):
                kb.dma("pool", wf[:, c, :], I["w_f"][c * 128:(c + 1) * 128, :], writes=[wf])
            Y = [sb("Y%d" % k, [128, NTmax, D], BF16) for k in range(2)]
            M = [[sb("M%d_%d" % (k, j), [128, NTmax, 128], BF16) for k in range(2)] for j in range(2)]
            fb = sb("fb", [128, D], BF16)
            fT = sb("fT", [128, 8, 128], BF16)
            h2t = [sb("h2t%d" % i, [128, D], F32) for i in range(2)]
            h3t = [sb("h3t%d" % i, [128, D], F32) for i in range(2)]
            kb.op("pool", lambda e: e.memset(fb[:], 0.0), writes=[fb])
            it = 0
            for si, (kind, bi, T) in enumerate(self.seqs):
                S = self.S[si]
                dft = I["c_dft_p"] if kind == "p" else I["c_dft_s"]
                NT = -(-T // 128)
                nfull = T // 128
                rem = T - nfull * 128
                for k in range(2):
                    if nfull:
                        kb.dma("sp" if k == 0 else "act", Y[k][:, 0:nfull, :], S["yc"][k, 0:nfull * 128, :].rearrange("(c p) d -> p c d", p=128), writes=[Y[k]])
                    if rem:
                        kb.dma("sp" if k == 0 else "act", Y[k][0:rem, nfull, :], S["yc"][k, nfull * 128:T, :], writes=[Y[k]])
                for (k0, kn) in tiles_of(T):
                    j = it % 2
                    it += 1
                    for k in range(2):
                        if nfull:
                            kb.dma("sp" if k == 0 else "act", M[j][k][:, 0:nfull, 0:kn], dft[k, 0:nfull * 128, k0:k0 + kn].rearrange("(c p) q -> p c q", p=128), writes=[M[j][k]])
                        if rem:
                            kb.dma("sp" if k == 0 else "act", M[j][k][0:rem, nfull, 0:kn], dft[k, nfull * 128:T, k0:k0 + kn], writes=[M[j][k]])
                    kb.dma("sp", h2t[j][:kn, :], S["h2"][k0:k0 + kn, :], writes=[h2t[j]])
                    for hf in range(2):
                        hs_ = slice(hf * 512, (hf + 1) * 512)
                        pf = self.psum()
                        for ch in range(NT):
                            cc = 128 if ch < nfull else rem
                            for k in range(2):
                                kb.op("pe", lambda e, pf=pf, ch=ch, cc=cc, k=k, j=j, kn=kn, hs_=hs_: e.matmul(pf[:kn, :], lhsT=M[j][k][0:cc, ch, 0:kn], rhs=Y[k][0:cc, ch, hs_],
                                                                                                           start=(ch == 0 and k == 0), stop=(ch == NT - 1 and k == 1)),
                                      reads=[M[j][k], Y[k]], writes=[pf])
                        if hf == 0:
                            kb.op("act", lambda e, pf=pf, kn=kn, hs_=hs_: e.copy(out=fb[:kn, hs_], in_=pf[:kn, :]), reads=[pf], writes=[fb])
                        else:
                            kb.op("dve", lambda e, pf=pf, kn=kn, hs_=hs_: e.tensor_copy(out=fb[:kn, hs_], in_=pf[:kn, :]), reads=[pf], writes=[fb])
                    self.transpose_into(fb, kn, fT, 0)
                    for hf in range(2):
                        hs_ = slice(hf * 512, (hf + 1) * 512)
                        p = self.psum()
                        for c in range(8):
                            kb.op("pe", lambda e, p=p, c=c, kn=kn, hs_=hs_: e.matmul(p[:kn, :], lhsT=fT[:, c, 0:kn], rhs=wf[:, c, hs_], start=(c == 0), stop=(c == 7)),
                                  reads=[fT, wf], writes=[p])
                        kb.op("dve", lambda e, p=p, kn=kn, hs_=hs_, j=j: e.tensor_tensor(out=h3t[j][:kn, hs_], in0=p[:kn, :], in1=h2t[j][:kn, hs_], op=ALU.add),
                              reads=[p, h2t[j]], writes=[h3t[j]])
                    kb.dma("act", S["h3"][k0:k0 + kn, :], h3t[j][:kn, :], reads=[h3t[j]])
            kb.barrier()


def host_consts(Tp, Ts, dft=False):
    c = {}
    c["c_ident"] = np.eye(128, dtype=np.float32)
    blk = np.zeros((128, 128), np.float32)
    blk[:64, :64] = 1; blk[64:, 64:] = 1
    c["c_blk"] = blk
    hs = np.zeros((128, 2), np.float32); hs[:64, 0] = 1; hs[64:, 1] = 1
    c["c_hsel"] = hs
    s_ = np.arange(64)[:, None]; t_ = np.arange(64)[None, :]
    m = np.zeros((2, 128, 128), np.float32)
    for r0 in (0, 64):
        m[0, r0:r0 + 64, 0:64] = (s_ < t_); m[0, r0:r0 + 64, 64:128] = (s_ <= t_)
        m[1, r0:r0 + 64, 0:64] = (s_ > t_); m[1, r0:r0 + 64, 64:128] = (s_ >= t_)
    c["c_mask"] = np.tile(m, (1, 1, 4))
    mt = np.zeros((2, 64, 64), np.float32)
    mt[0] = (s_ > t_); mt[1] = (s_ < t_)
    c["c_maskT"] = np.tile(mt, (1, 1, 16))
    cm = np.ones((128, 512), np.float32); cm[:, ::64] = 0
    c["c_cm"] = cm
    c["c_eye2"] = np.eye(128, dtype=np.float32)
    c["c_eye16"] = np.tile(np.eye(64, dtype=np.float32), (1, 16))
    cc = np.arange(128)
    ang = 2 * np.pi * np.outer(cc, cc) / 128
    c["c_dftc"] = np.concatenate([np.cos(ang), np.sin(ang)], 1).astype(np.float32) / np.sqrt(128)
    for nm, T in ((("c_dft_p", Tp), ("c_dft_s", Ts)) if dft else ()):
        t = np.arange(T, dtype=np.int64)
        ph = (np.outer(t, t) % T).astype(np.float64) * (2 * np.pi / T)
        c[nm] = np.stack([np.cos(ph), -np.sin(ph)]).astype(np.float32) / np.float32(np.sqrt(T))
        c[nm] = c[nm].astype(ml_dtypes.bfloat16)
    return c


PHASES = ("p1", "scan", "p3", "p4")
NCORES = 8


def kernel(x_prompt, x_sample, meta_tokens, norm_mix, norm_ffn, norm_final,
           rwkv_mu, rwkv_w_rkv, rwkv_w0, rwkv_w1, rwkv_w2, rwkv_a0, rwkv_a1, rwkv_a2,
           rwkv_g1, rwkv_g2, rwkv_k_k, rwkv_k_a, rwkv_r_k, rwkv_gn_w, rwkv_gn_b, rwkv_w_o,
           fnet_w_o, ffn_w_in, ffn_conv_w, ffn_conv_b, ffn_w_out):
    f = lambda a: np.ascontiguousarray(np.asarray(a, dtype=np.float32))
    x_prompt, x_sample = f(x_prompt), f(x_sample)
    Bp, Sp, _ = x_prompt.shape
    Bs, Ss, _ = x_sample.shape
    npc, nsc = Bp // NCORES, Bs // NCORES
    Tp, Ts = Sp + NMETA, Ss + NMETA
    prog = Prog(npc, Tp, nsc, Ts, debug=False, phases=PHASES)
    nc = prog.build()
    shared = {
        "meta": f(meta_tokens), "norm_mix": f(norm_mix), "norm_ffn": f(norm_ffn),
        "norm_final": f(norm_final).reshape(1, D), "mu": f(rwkv_mu)[0], "w_rkv": f(rwkv_w_rkv)[0],
        "w0": f(rwkv_w0)[0], "w1": f(rwkv_w1)[0], "w2": f(rwkv_w2)[0],
        "a0": f(rwkv_a0)[0], "a1": f(rwkv_a1)[0], "a2": f(rwkv_a2)[0],
        "g1": f(rwkv_g1)[0], "g2": f(rwkv_g2)[0], "k_k": f(rwkv_k_k), "k_a": f(rwkv_k_a),
        "r_k": f(rwkv_r_k).reshape(1, D), "gn_w": f(rwkv_gn_w), "gn_b": f(rwkv_gn_b), "w_o": f(rwkv_w_o)[0],
        "w_f": f(fnet_w_o)[0], "ffn_w_in": f(ffn_w_in), "conv_w": f(ffn_conv_w), "conv_b": f(ffn_conv_b),
        "ffn_w_out": f(ffn_w_out),
    }
    shared.update(host_consts(Tp, Ts, dft=("p4" in PHASES)))
    in_maps = []
    for c in range(NCORES):
        m = dict(shared)
        m["x_p"] = x_prompt[c * npc:(c + 1) * npc]
        m["x_s"] = x_sample[c * nsc:(c + 1) * nsc]
        in_maps.append(m)
    res = run_bass_kernel_spmd(nc, in_maps, core_ids=list(range(NCORES)))
    y_p = np.concatenate([np.asarray(r["y_p"], dtype=np.float32) for r in res.results], axis=0)
    y_s = np.concatenate([np.asarray(r["y_s"], dtype=np.float32) for r in res.results], axis=0)
    return (y_p, y_s)
```

```python
import contextlib
import math
import numpy as np
import ml_dtypes
import concourse.bass as bass
import concourse.mybir as mybir
from concourse.bass_utils import run_bass_kernel_spmd

F32 = mybir.dt.float32
BF16 = mybir.dt.bfloat16
AF = mybir.ActivationFunctionType
ALU = mybir.AluOpType
AX = mybir.AxisListType

D = 1024
NH = 16
HS = 64
DFF = 2816
NMETA = 16
CH = 64
LWC = math.exp(-0.5)
NDS = 56
NSW = 16


class Buf:
    __slots__ = ("t", "w", "r")

    def __init__(self, t):
        self.t = t
        self.w = None
        self.r = []

    def __getitem__(self, k):
        return self.t[k]


class KB:
    def __init__(self, nc, stack):
        self.nc = nc
        self.names = ["pe", "act", "dve", "pool", "sp"]
        self.q = {e: [] for e in self.names}
        self.count = {e: 0 for e in self.names}
        self.mile = {e: [] for e in self.names}
        self.mileset = {e: set() for e in self.names}
        self.semval = {e: 0 for e in self.names}
        self.milemap = {e: {} for e in self.names}
        self.waited = {e: {} for e in self.names}
        self.flushed = {e: 0 for e in self.names}
        self.milekeys = {e: [] for e in self.names}
        self.esem = {e: stack.enter_context(nc.semaphore("s_" + e)) for e in self.names}
        self.dsem = [stack.enter_context(nc.semaphore("d%d" % i)) for i in range(NDS)]
        self.dval = [0] * NDS
        self.dnext = 0
        self.dnext_sw = 0
        self.rr = 0

    def _wait(self, eng, deps):
        wd = self.waited[eng]
        for dep in deps:
            if dep[0] == "eng":
                _, f, idx = dep
                if f == eng and eng == "pe":
                    continue
                if idx <= self.flushed[f] and idx not in self.milemap[f]:
                    idx = min(k for k in self.milekeys[f] if k >= idx)
                if wd.get(("eng", f), 0) >= idx:
                    continue
                wd[("eng", f)] = idx
                if idx not in self.mileset[f] and idx not in self.milemap[f]:
                    self.mileset[f].add(idx)
                self.q[eng].append(("weng", f, idx))
            else:
                _, si, val = dep
                if wd.get(("dma", si), 0) >= val:
                    continue
                wd[("dma", si)] = val
                self.q[eng].append(("wdma", si, val))

    def _deps(self, reads, writes):
        deps = []
        for b in reads:
            if b is not None and b.w is not None:
                deps.append(b.w)
        for b in writes:
            if b is not None:
                if b.w is not None:
                    deps.append(b.w)
                deps.extend(b.r)
        return deps

    def _mark(self, tok, reads, writes):
        for b in reads:
            if b is None:
                continue
            if tok[0] == "eng":
                b.r = [t for t in b.r if not (t[0] == "eng" and t[1] == tok[1])]
            b.r.append(tok)
        for b in writes:
            if b is None:
                continue
            b.w = tok
            b.r = []

    def op(self, eng, fn, reads=(), writes=()):
        self._wait(eng, self._deps(reads, writes))
        self.count[eng] += 1
        idx = self.count[eng]
        tok = ("eng", eng, idx)
        self.q[eng].append(("op", fn, idx))
        self._mark(tok, reads, writes)
        return tok

    def dma(self, eng, out, in_, reads=(), writes=()):
        if eng == "pool":
            si = self.dnext_sw
            self.dnext_sw = (self.dnext_sw + 1) % NSW
        else:
            si = NSW + self.dnext
            self.dnext = (self.dnext + 1) % (NDS - NSW)
        deps = self._deps(reads, writes)
        if self.dval[si] > 0:
            deps.append(("dma", si, self.dval[si]))
        self._wait(eng, deps)
        self.dval[si] += 16
        tok = ("dma", si, self.dval[si])
        self.q[eng].append(("dma", out, in_, si))
        self._mark(tok, reads, writes)
        return tok

    def dma_rr(self, out, in_, reads=(), writes=(), cast=False):
        if cast:
            return self.dma("pool", out, in_, reads, writes)
        eng = ("sp", "act")[self.rr % 2]
        self.rr += 1
        return self.dma("sp", out, in_, reads, writes)

    def barrier(self):
        toks = [("eng", e, self.count[e]) for e in self.names if self.count[e] > 0]
        toks += [("dma", si, self.dval[si]) for si in range(NDS) if self.dval[si] > 0]
        for e in self.names:
            self._wait(e, toks)

    def flush(self):
        nc = self.nc
        for e in self.names:
            v = self.semval[e]
            if self.count[e] > self.flushed[e]:
                self.mileset[e].add(self.count[e])
            self.milekeys[e] = self.milekeys[e][-1:] + sorted(self.mileset[e])
            self.flushed[e] = self.count[e]
            for idx in sorted(self.mileset[e]):
                v += 1
                self.milemap[e][idx] = v
            self.semval[e] = v
        q = self.q
        kb = self

        def replay(name, eng):
            for ent in q[name]:
                k = ent[0]
                if k == "weng":
                    eng.wait_ge(kb.esem[ent[1]], kb.milemap[ent[1]][ent[2]])
                elif k == "wdma":
                    eng.wait_ge(kb.dsem[ent[1]], ent[2])
                elif k == "op":
                    ins = ent[1](eng)
                    if ent[2] in kb.mileset[name]:
                        ins.then_inc(kb.esem[name], 1)
                else:
                    eng.dma_start(out=ent[1], in_=ent[2]).then_inc(kb.dsem[ent[3]], 16)

        with nc.allow_non_contiguous_dma(reason="small per-feature vectors"), nc.Block() as block:
            @block.tensor
            def _(eng):
                replay("pe", eng)

            @block.scalar
            def _(eng):
                replay("act", eng)

            @block.vector
            def _(eng):
                replay("dve", eng)

            @block.gpsimd
            def _(eng):
                replay("pool", eng)

            @block.sync
            def _(eng):
                replay("sp", eng)

        self.q = {e: [] for e in self.names}
        self.mileset = {e: set() for e in self.names}


def blocks_of(T, maxn=510):
    nb = -(-T // maxn)
    base, rem = divmod(T, nb)
    out = []
    t = 0
    for i in range(nb):
        n = base + (1 if i < rem else 0)
        out.append((t, n))
        t += n
    return out


def tiles_of(n, p=128):
    return [(i, min(p, n - i)) for i in range(0, n, p)]


class Prog:
    def __init__(self, seqs_p, T_p, seqs_s, T_s, debug=False, phases=("p1", "scan", "p3", "p4")):
        self.np_, self.Tp, self.ns_, self.Ts = seqs_p, T_p, seqs_s, T_s
        self.debug = debug
        self.phases = phases
        self._dbg = set()
        self.seqs = [("p", i, T_p) for i in range(seqs_p)] + [("s", i, T_s) for i in range(seqs_s)]

    def build(self):
        nc = bass.Bass("TRN2", target_bir_lowering=False)
        self.nc = nc
        self.stack = contextlib.ExitStack()
        st = self.stack
        kb = KB(nc, st)
        self.kb = kb
        I = {}

        def din(name, shape, dt=F32):
            I[name] = nc.dram_tensor(name, list(shape), dt, kind="ExternalInput").ap()
            return I[name]

        self.I = I
        din("x_p", [self.np_, self.Tp - NMETA, D])
        din("x_s", [self.ns_, self.Ts - NMETA, D])
        din("meta", [NMETA, D])
        din("norm_mix", [2, D]); din("norm_ffn", [2, D]); din("norm_final", [1, D])
        din("mu", [6, D]); din("w_rkv", [3, D, D])
        din("w0", [2, D]); din("w1", [2, D, 64]); din("w2", [2, 64, D])
        din("a0", [2, D]); din("a1", [2, D, 64]); din("a2", [2, 64, D])
        din("g1", [D, 160]); din("g2", [160, D])
        din("k_k", [1, D]); din("k_a", [1, D]); din("r_k", [1, D])
        din("gn_w", [1, D]); din("gn_b", [1, D]); din("w_o", [D, D])
        din("w_f", [D, D]); din("ffn_w_in", [2, D, 2 * DFF]); din("conv_w", [2, 3, DFF])
        din("conv_b", [2, DFF]); din("ffn_w_out", [2, DFF, D])
        din("c_ident", [128, 128]); din("c_blk", [128, 128]); din("c_hsel", [128, 2])
        din("c_mask", [2, 128, 512]); din("c_maskT", [2, 64, 1024]); din("c_cm", [128, 512]); din("c_eye16", [64, 1024]); din("c_eye2", [128, 128])
        din("c_dftc", [128, 256])
        if "p4" in self.phases:
            din("c_dft_p", [2, self.Tp, self.Tp], BF16)
            din("c_dft_s", [2, self.Ts, self.Ts], BF16)
        okind = "ExternalOutput"
        self.y_p = nc.dram_tensor("y_p", [self.np_, self.Tp - NMETA, D], F32, kind=okind).ap()
        self.y_s = nc.dram_tensor("y_s", [self.ns_, self.Ts - NMETA, D], F32, kind=okind).ap()
        skind = "ExternalOutput" if self.debug else "Internal"
        self.S = []
        for si, (kind, bi, T) in enumerate(self.seqs):
            d = {}
            d["fm"] = nc.dram_tensor("fm%d" % si, [8, D, T], F32, kind=skind).ap()
            d["v"] = nc.dram_tensor("sv%d" % si, [T, D], BF16, kind=skind).ap()
            d["g"] = nc.dram_tensor("sg%d" % si, [T, D], BF16, kind=skind).ap()
            d["bonus"] = nc.dram_tensor("bonus%d" % si, [T, NH], F32, kind=skind).ap()
            d["o"] = nc.dram_tensor("so%d" % si, [2, T, D], F32, kind=skind).ap()
            d["h2"] = nc.dram_tensor("h2_%d" % si, [T, D], F32, kind=skind).ap()
            d["h1"] = nc.dram_tensor("h1_%d" % si, [T, D], F32, kind=skind).ap()
            d["h3"] = nc.dram_tensor("h3_%d" % si, [T, D], F32, kind=skind).ap()
            d["yc"] = nc.dram_tensor("yc%d" % si, [2, T, D], BF16, kind=skind).ap()
            self.S.append(d)

        self.consts()
        if "p1" in self.phases:
            self.phase1()
        if "scan" in self.phases:
            self.phase_scan()
        if "p3" in self.phases:
            self.phase3()
        if "p4" in self.phases:
            self.phase4()
        kb.barrier()
        kb.flush()
        st.close()
        return nc

    def sb(self, stack, name, shape, dt):
        return Buf(stack.enter_context(self.nc.sbuf_tensor("sb_" + name, list(shape), dt)))

    def ps(self, stack, name, shape, dt):
        return Buf(stack.enter_context(self.nc.psum_tensor("ps_" + name, list(shape), dt)))

    def xrows(self, si, t0, n):
        kind, bi, T = self.seqs[si]
        x = self.I["x_p"] if kind == "p" else self.I["x_s"]
        out = []
        if t0 < NMETA:
            m = min(NMETA, t0 + n) - t0
            out.append((self.I["meta"][t0:t0 + m, :], 0, m))
            if n > m:
                out.append((x[bi, 0:n - m, :], m, n - m))
        else:
            out.append((x[bi, t0 - NMETA:t0 - NMETA + n, :], 0, n))
        return out

    def consts(self):
        kb, nc, st = self.kb, self.nc, self.stack
        I = self.I
        C = {}
        self.C = C

        def ld(name, shape, dt, src, cast=False):
            b = self.sb(st, "k_" + name, shape, dt)
            kb.dma_rr(b[:], src, writes=[b], cast=cast)
            C[name] = b
            return b

        ld("ident", [128, 128], BF16, I["c_ident"], cast=True)
        ld("blk", [128, 128], BF16, I["c_blk"], cast=True)
        ld("hsel", [128, 2], BF16, I["c_hsel"], cast=True)
        ld("eye2", [128, 128], BF16, I["c_eye2"], cast=True)
        def colvec(name, src, n):
            b = self.sb(st, "k_" + name, [128, n * 8], F32)
            with nc.allow_non_contiguous_dma(reason="tiny per-feature vectors"):
                for i in range(n):
                    kb.dma("sp", b[:, i * 8:(i + 1) * 8], src[i].rearrange("(c p) -> p c", p=128), writes=[b])
            C[name] = b
        colvec("muT", I["mu"], 6)
        colvec("w0T", I["w0"], 2)
        colvec("a0T", I["a0"], 2)
        colvec("kkT", I["k_k"], 1)
        colvec("kaT", I["k_a"], 1)
        colvec("rkT", I["r_k"], 1)
        b = self.sb(st, "k_omka", [128, 8], F32)
        kb.op("dve", lambda e, b=b: e.tensor_scalar(out=b[:], in0=C["kaT"][:], scalar1=-1.0, scalar2=1.0,
                                                    op0=ALU.mult, op1=ALU.add), reads=[C["kaT"]], writes=[b])
        C["omka"] = b
        self.psb = [self.ps(st, "psb%d" % i, [128, 512], F32) for i in range(8)]
        self.psi = 0

    def dbg(self, name, buf, ap, shape, dt):
        if not self.debug or name in self._dbg:
            return
        self._dbg.add(name)
        d = self.nc.dram_tensor("dbg_" + name, list(shape), dt, kind="ExternalOutput").ap()
        self.kb.dma("sp", d, ap, reads=[buf])

    def rowvec(self, st, name, src):
        b = self.sb(st, "rv_" + name, [128, D], F32)
        self.kb.dma("sp", b[:], src.partition_broadcast(128), writes=[b])
        self.C[name] = b
        return b

    def psum(self):
        b = self.psb[self.psi % 8]
        self.psi += 1
        return b

    def norm_transpose(self, ws, xt, n, grow, hnT, col0, junk, ssb, hnb):
        kb = self.kb
        C = self.C
        kb.op("dve", lambda e: e.scalar_tensor_tensor(out=junk[:n, :], in0=xt[:n, :], scalar=1.0, in1=xt[:n, :],
                                                      op0=ALU.mult, op1=ALU.mult, accum_out=ssb[:n, 0:1]),
              reads=[xt], writes=[junk, ssb])
        kb.op("dve", lambda e: e.tensor_scalar(out=ssb[:n, 1:2], in0=ssb[:n, 0:1], scalar1=1.0 / D, scalar2=1e-6,
                                               op0=ALU.mult, op1=ALU.add), reads=[ssb], writes=[ssb])
        kb.op("act", lambda e: e.activation(out=ssb[:n, 2:3], in_=ssb[:n, 1:2], func=AF.Sqrt), reads=[ssb], writes=[ssb])
        kb.op("dve", lambda e: e.reciprocal(out=ssb[:n, 3:4], in_=ssb[:n, 2:3]), reads=[ssb], writes=[ssb])
        kb.op("dve", lambda e: e.scalar_tensor_tensor(out=hnb[:n, :], in0=xt[:n, :], scalar=ssb[:n, 3:4], in1=grow[:n, :],
                                                      op0=ALU.mult, op1=ALU.mult), reads=[xt, ssb, grow], writes=[hnb])
        self.transpose_into(hnb, n, hnT, col0)

    def norm_transpose_multi(self, items, grow, hnT, ssbs, hnbs):
        kb, C = self.kb, self.C
        for i, (xt, n, c0) in enumerate(items):
            kb.op("dve", lambda e, xt=xt, n=n, i=i: e.scalar_tensor_tensor(out=hnbs[i][:n, :], in0=xt[:n, :], scalar=1.0, in1=xt[:n, :],
                                                                         op0=ALU.mult, op1=ALU.mult, accum_out=ssbs[i][:n, 0:1]),
                  reads=[xt], writes=[hnbs[i], ssbs[i]])
        for i, (xt, n, c0) in enumerate(items):
            kb.op("dve", lambda e, n=n, i=i: e.tensor_scalar(out=ssbs[i][:n, 1:2], in0=ssbs[i][:n, 0:1], scalar1=1.0 / D, scalar2=1e-6,
                                                           op0=ALU.mult, op1=ALU.add), reads=[ssbs[i]], writes=[ssbs[i]])
        for i, (xt, n, c0) in enumerate(items):
            kb.op("act", lambda e, n=n, i=i: e.activation(out=ssbs[i][:n, 2:3], in_=ssbs[i][:n, 1:2], func=AF.Sqrt), reads=[ssbs[i]], writes=[ssbs[i]])
        for i, (xt, n, c0) in enumerate(items):
            kb.op("dve", lambda e, n=n, i=i: e.reciprocal(out=ssbs[i][:n, 3:4], in_=ssbs[i][:n, 2:3]), reads=[ssbs[i]], writes=[ssbs[i]])
        for i, (xt, n, c0) in enumerate(items):
            kb.op("dve", lambda e, xt=xt, n=n, i=i: e.scalar_tensor_tensor(out=hnbs[i][:n, :], in0=xt[:n, :], scalar=ssbs[i][:n, 3:4], in1=grow[:n, :],
                                                                         op0=ALU.mult, op1=ALU.mult), reads=[xt, ssbs[i], grow], writes=[hnbs[i]])
        pts = []
        for i, (xt, n, c0) in enumerate(items):
            pt = self.psum()
            pts.append(pt)
            ptb = pt.t[:].bitcast(BF16)
            for c in range(8):
                kb.op("pe", lambda e, c=c, n=n, i=i, ptb=ptb: e.transpose(out=ptb[:, c * 128:c * 128 + n], in_=hnbs[i][:n, c * 128:(c + 1) * 128],
                                                                        identity=C["ident"][:n, :n]), reads=[hnbs[i], C["ident"]], writes=[pt])
        for i, (xt, n, c0) in enumerate(items):
            src3 = pts[i].t[:].bitcast(BF16).rearrange("p (c t) -> p c t", t=128)
            kb.op("act", lambda e, n=n, c0=c0, src3=src3: e.copy(out=hnT[:, 0:8, c0:c0 + n], in_=src3[:, 0:8, 0:n]), reads=[pts[i]], writes=[hnT])

    def transpose_into(self, src, n, dstT, col0, nchunk=8, eng="act"):
        kb = self.kb
        C = self.C
        pt = self.psum()
        ptb = pt.t[:].bitcast(BF16)
        for c in range(nchunk):
            kb.op("pe", lambda e, c=c: e.transpose(out=ptb[:, c * 128:c * 128 + n], in_=src[:n, c * 128:(c + 1) * 128],
                                                   identity=C["ident"][:n, :n]), reads=[src, C["ident"]], writes=[pt])
        src3 = ptb.rearrange("p (c t) -> p c t", t=128)
        if eng == "act":
            kb.op("act", lambda e: e.copy(out=dstT[:, 0:nchunk, col0:col0 + n], in_=src3[:, 0:nchunk, 0:n]),
                  reads=[pt], writes=[dstT])
        else:
            kb.op("dve", lambda e: e.tensor_copy(out=dstT[:, 0:nchunk, col0:col0 + n], in_=src3[:, 0:nchunk, 0:n]),
                  reads=[pt], writes=[dstT])

    def phase1(self):
        kb, nc, I, C = self.kb, self.nc, self.I, self.C
        with contextlib.ExitStack() as st:
            sb = lambda name, shape, dt: self.sb(st, name, shape, dt)
            self.rowvec(st, "nm0", I["norm_mix"][0:1, :])
            wr = [sb("wrkv%d" % i, [128, 8, D], BF16) for i in range(3)]
            for i in range(3):
                for c in range(8):
                    kb.dma("pool", wr[i][:, c, :], I["w_rkv"][i, c * 128:(c + 1) * 128, :], writes=[wr[i]])
            w1c = sb("w1c", [128, 8, 128], BF16)
            a1c = sb("a1c", [128, 8, 128], BF16)
            for z in range(2):
                kb.dma("pool", w1c[:, :, z * 64:(z + 1) * 64], I["w1"][z].rearrange("(c p) r -> p c r", p=128), writes=[w1c])
                kb.dma("pool", a1c[:, :, z * 64:(z + 1) * 64], I["a1"][z].rearrange("(c p) r -> p c r", p=128), writes=[a1c])
            w2c = sb("w2c", [128, D], BF16)
            a2c = sb("a2c", [128, D], BF16)
            for z in range(2):
                kb.dma("pool", w2c[z * 64:(z + 1) * 64, :], I["w2"][z], writes=[w2c])
                kb.dma("pool", a2c[z * 64:(z + 1) * 64, :], I["a2"][z], writes=[a2c])
            g1c = sb("g1c", [128, 8, 160], BF16)
            kb.dma("pool", g1c[:], I["g1"].rearrange("(c p) r -> p c r", p=128), writes=[g1c])
            g2a = sb("g2a", [128, D], BF16)
            g2b = sb("g2b", [32, D], BF16)
            kb.dma("pool", g2a[:], I["g2"][0:128, :], writes=[g2a])
            kb.dma("pool", g2b[:], I["g2"][128:160, :], writes=[g2b])

            xt = [sb("xt%d" % i, [128, D], F32) for i in range(2)]
            hnbs = [sb("hnb%d" % i, [128, D], BF16) for i in range(2)]
            ssbs = [sb("ssb%d" % i, [128, 4], F32) for i in range(2)]
            hnb = hnbs[0]
            hnT = sb("hnT", [128, 8, 512], BF16)
            tmp = sb("tmp", [128, 8, 512], BF16)
            xx = tmp
            xm = [sb("xm%d" % i, [128, 8, 512], BF16) for i in range(5)]
            tw = sb("tw", [128, 512], BF16)
            ta = sb("ta", [128, 512], BF16)
            sg1 = sb("sg1", [128, 512], BF16)
            sg2 = sb("sg2", [32, 512], BF16)
            soL = [[sb("so%d_%d" % (i, j), [128, 512], F32) for i in range(8)] for j in range(2)]
            kunL = [sb("kun%d" % j, [128, 512], F32) for j in range(2)]
            ksbL = [sb("ksb%d" % j, [128, 512], F32) for j in range(2)]
            sqb = sb("sqb", [128, 512], BF16)
            rnL = [sb("rn0", [128, 512], F32)] * 2
            t1L = [sb("t1_0", [128, 512], F32)] * 2
            av = [sb("av%d" % i, [128, 512], F32) for i in range(2)]
            rkp = sb("rkp", [128, 8, 512], BF16)
            vst = sb("vst", [128, 2, D], BF16)
            gst = sb("gst", [128, 2, D], BF16)
            bst = sb("bst", [128, 4, NH], F32)
            for b_ in (xt[0], xt[1], hnbs[0], hnbs[1]):
                kb.op("pool", lambda e, b_=b_: e.memset(b_[:], 0.0), writes=[b_])

            for si, (kind, bi, T) in enumerate(self.seqs):
                S = self.S[si]
                for (t0, nb) in blocks_of(T):
                    ws, we = t0 - 1, t0 + nb + 1
                    W = we - ws
                    n = nb
                    wtl = tiles_of(W)
                    for p0 in range(0, len(wtl), 2):
                        items = []
                        for ti in range(p0, min(p0 + 2, len(wtl))):
                            o, cnt = wtl[ti]
                            a, b = ws + o, ws + o + cnt
                            a2_, b2_ = max(a, 0), min(b, T)
                            x_ = xt[ti % 2]
                            for (src, ro, nr) in self.xrows(si, a2_, b2_ - a2_):
                                kb.dma("sp" if ti % 2 == 0 else "act", x_[a2_ - a + ro:a2_ - a + ro + nr, :], src, writes=[x_])
                            items.append((x_, cnt, o))
                        self.norm_transpose_multi(items, C["nm0"], hnT, ssbs, hnbs)
                    if ws < 0:
                        kb.op("pool", lambda e: e.memset(hnT[:, :, 0:1], 0.0), writes=[hnT])
                    if we > T:
                        kb.op("pool", lambda e, W=W: e.memset(hnT[:, :, W - 1:W], 0.0), writes=[hnT])
                    kb.op("dve", lambda e, n=n: e.tensor_tensor(out=tmp[:, :, 0:n], in0=hnT[:, :, 0:n], in1=hnT[:, :, 2:n + 2],
                                                                op=ALU.add), reads=[hnT], writes=[tmp])
                    kb.op("dve", lambda e, n=n: e.scalar_tensor_tensor(out=tmp[:, :, 0:n], in0=tmp[:, :, 0:n], scalar=0.5,
                                                                        in1=hnT[:, :, 1:n + 1], op0=ALU.mult, op1=ALU.subtract),
                          reads=[hnT], writes=[tmp])
                    def mix(m, dst):
                        for c in range(8):
                            kb.op("dve", lambda e, m=m, c=c, n=n, dst=dst: e.scalar_tensor_tensor(
                                out=dst[:, c, 0:n], in0=xx[:, c, 0:n], scalar=C["muT"][:, m * 8 + c:m * 8 + c + 1],
                                in1=hnT[:, c, 1:n + 1], op0=ALU.mult, op1=ALU.add), reads=[xx, hnT, C["muT"]], writes=[dst])
                    xw, xa, xg, xr, xk = xm[0], xm[1], xm[2], xm[3], xm[4]
                    xv = xm[0]
                    mix(1, xw); mix(4, xa); mix(5, xg); mix(0, xr); mix(2, xk)
                    p = self.psum()
                    for c in range(8):
                        kb.op("pe", lambda e, c=c, p=p, n=n: e.matmul(p[:, 0:n], lhsT=w1c[:, c, :], rhs=xw[:, c, 0:n],
                                                                      start=(c == 0), stop=(c == 7)), reads=[w1c, xw], writes=[p])
                    kb.op("act", lambda e, p=p, n=n: e.activation(out=tw[:, 0:n], in_=p[:, 0:n], func=AF.Tanh), reads=[p], writes=[tw])
                    p = self.psum()
                    for c in range(8):
                        kb.op("pe", lambda e, c=c, p=p, n=n: e.matmul(p[:, 0:n], lhsT=a1c[:, c, :], rhs=xa[:, c, 0:n],
                                                                      start=(c == 0), stop=(c == 7)), reads=[a1c, xa], writes=[p])
                    kb.op("act", lambda e, p=p, n=n: e.copy(out=ta[:, 0:n], in_=p[:, 0:n]), reads=[p], writes=[ta])
                    p = self.psum()
                    for c in range(8):
                        kb.op("pe", lambda e, c=c, p=p, n=n: e.matmul(p[:, 0:n], lhsT=g1c[:, c, 0:128], rhs=xg[:, c, 0:n],
                                                                      start=(c == 0), stop=(c == 7)), reads=[g1c, xg], writes=[p])
                    kb.op("act", lambda e, p=p, n=n: e.activation(out=sg1[:, 0:n], in_=p[:, 0:n], func=AF.Sigmoid), reads=[p], writes=[sg1])
                    p = self.psum()
                    for c in range(8):
                        kb.op("pe", lambda e, c=c, p=p, n=n: e.matmul(p[0:32, 0:n], lhsT=g1c[:, c, 128:160], rhs=xg[:, c, 0:n],
                                                                      start=(c == 0), stop=(c == 7)), reads=[g1c, xg], writes=[p])
                    kb.op("act", lambda e, p=p, n=n: e.activation(out=sg2[:, 0:n], in_=p[0:32, 0:n], func=AF.Sigmoid), reads=[p], writes=[sg2])
                    mix(3, xv)
                    for s in range(8):
                        so, kun, ksb, rn, t1 = soL[s % 2], kunL[s % 2], ksbL[s % 2], rnL[s % 2], t1L[s % 2]
                        sl = slice(s * 128, (s + 1) * 128)
                        p = self.psum()
                        for c in range(8):
                            kb.op("pe", lambda e, so=so, kun=kun, ksb=ksb, rn=rn, t1=t1, c=c, p=p, n=n, sl=sl: e.matmul(p[:, 0:n], lhsT=wr[0][:, c, sl], rhs=xr[:, c, 0:n],
                                                                                 start=(c == 0), stop=(c == 7)), reads=[wr[0], xr], writes=[p])
                        kb.op("act", lambda e, so=so, kun=kun, ksb=ksb, rn=rn, t1=t1, p=p, n=n, s=s: e.copy(out=so[0][:, 0:n], in_=p[:, 0:n]), reads=[p], writes=[so[0]])
                        p = self.psum()
                        for c in range(8):
                            kb.op("pe", lambda e, so=so, kun=kun, ksb=ksb, rn=rn, t1=t1, c=c, p=p, n=n, sl=sl: e.matmul(p[:, 0:n], lhsT=wr[1][:, c, sl], rhs=xk[:, c, 0:n],
                                                                                 start=(c == 0), stop=(c == 7)), reads=[wr[1], xk], writes=[p])
                        kb.op("act", lambda e, so=so, kun=kun, ksb=ksb, rn=rn, t1=t1, p=p, n=n: e.copy(out=ksb[:, 0:n], in_=p[:, 0:n]), reads=[p], writes=[ksb])
                        kb.op("dve", lambda e, so=so, kun=kun, ksb=ksb, rn=rn, t1=t1, n=n, s=s: e.tensor_scalar(out=kun[:, 0:n], in0=ksb[:, 0:n], scalar1=C["kkT"][:, s:s + 1],
                                                                         scalar2=None, op0=ALU.mult), reads=[ksb, C["kkT"]], writes=[kun])
                        kb.op("pool", lambda e, so=so, kun=kun, ksb=ksb, rn=rn, t1=t1, n=n: e.tensor_tensor(out=sqb[:, 0:n], in0=kun[:, 0:n], in1=kun[:, 0:n], op=ALU.mult),
                              reads=[kun], writes=[sqb])
                        p2 = self.psum()
                        kb.op("pe", lambda e, so=so, kun=kun, ksb=ksb, rn=rn, t1=t1, p2=p2, n=n: e.matmul(p2[:, 0:n], lhsT=C["blk"][:, :], rhs=sqb[:, 0:n], start=True, stop=True),
                              reads=[C["blk"], sqb], writes=[p2])
                        kb.op("act", lambda e, so=so, kun=kun, ksb=ksb, rn=rn, t1=t1, p2=p2, n=n: e.activation(out=rn[:, 0:n], in_=p2[:, 0:n], func=AF.Sqrt), reads=[p2], writes=[rn])
                        kb.op("dve", lambda e, so=so, kun=kun, ksb=ksb, rn=rn, t1=t1, n=n: e.tensor_scalar(out=rn[:, 0:n], in0=rn[:, 0:n], scalar1=1e-12, scalar2=None, op0=ALU.max),
                              reads=[rn], writes=[rn])
                        kb.op("dve", lambda e, so=so, kun=kun, ksb=ksb, rn=rn, t1=t1, n=n: e.reciprocal(out=rn[:, 0:n], in_=rn[:, 0:n]), reads=[rn], writes=[rn])
                        kb.op("dve", lambda e, so=so, kun=kun, ksb=ksb, rn=rn, t1=t1, n=n, s=s: e.tensor_tensor(out=so[1][:, 0:n], in0=kun[:, 0:n], in1=rn[:, 0:n], op=ALU.mult),
                              reads=[kun, rn], writes=[so[1]])
                        for z in range(2):
                            zs = slice(z * 64, (z + 1) * 64)
                            p = self.psum()
                            kb.op("pe", lambda e, so=so, kun=kun, ksb=ksb, rn=rn, t1=t1, p=p, n=n, sl=sl, zs=zs: e.matmul(p[:, 0:n], lhsT=w2c[zs, sl], rhs=tw[zs, 0:n], start=True, stop=True),
                                  reads=[w2c, tw], writes=[p])
                            kb.op("act", lambda e, so=so, kun=kun, ksb=ksb, rn=rn, t1=t1, p=p, n=n, s=s, z=z: e.activation(out=so[6 + z][:, 0:n], in_=p[:, 0:n], func=AF.Sigmoid,
                                                                                    bias=C["w0T"][:, z * 8 + s:z * 8 + s + 1], scale=1.0),
                                  reads=[p, C["w0T"]], writes=[so[6 + z]])
                            p = self.psum()
                            kb.op("pe", lambda e, so=so, kun=kun, ksb=ksb, rn=rn, t1=t1, p=p, n=n, sl=sl, zs=zs: e.matmul(p[:, 0:n], lhsT=a2c[zs, sl], rhs=ta[zs, 0:n], start=True, stop=True),
                                  reads=[a2c, ta], writes=[p])
                            kb.op("act", lambda e, so=so, kun=kun, ksb=ksb, rn=rn, t1=t1, p=p, n=n, s=s, z=z: e.activation(out=av[z][:, 0:n], in_=p[:, 0:n], func=AF.Sigmoid,
                                                                                    bias=C["a0T"][:, z * 8 + s:z * 8 + s + 1], scale=1.0),
                                  reads=[p, C["a0T"]], writes=[av[z]])
                            kb.op("dve", lambda e, so=so, kun=kun, ksb=ksb, rn=rn, t1=t1, n=n, s=s, z=z: e.tensor_scalar(out=t1[:, 0:n], in0=av[z][:, 0:n], scalar1=C["kaT"][:, s:s + 1],
                                                                                  scalar2=C["omka"][:, s:s + 1], op0=ALU.mult, op1=ALU.add),
                                  reads=[av[z], C["kaT"], C["omka"]], writes=[t1])
                            kb.op("pool", lambda e, so=so, kun=kun, ksb=ksb, rn=rn, t1=t1, n=n, s=s, z=z: e.tensor_tensor(out=so[2 + z][:, 0:n], in0=t1[:, 0:n], in1=ksb[:, 0:n], op=ALU.mult),
                                  reads=[t1, ksb], writes=[so[2 + z]])
                            kb.op("pool", lambda e, so=so, kun=kun, ksb=ksb, rn=rn, t1=t1, n=n, s=s, z=z: e.tensor_tensor(out=so[4 + z][:, 0:n], in0=so[1][:, 0:n], in1=av[z][:, 0:n], op=ALU.mult),
                                  reads=[so[1], av[z]], writes=[so[4 + z]])
                        kb.op("dve", lambda e, so=so, kun=kun, ksb=ksb, rn=rn, t1=t1, n=n, s=s: e.tensor_tensor(out=t1[:, 0:n], in0=so[2][:, 0:n], in1=so[3][:, 0:n], op=ALU.add),
                              reads=[so[2], so[3]], writes=[t1])
                        kb.op("dve", lambda e, so=so, kun=kun, ksb=ksb, rn=rn, t1=t1, n=n, s=s: e.scalar_tensor_tensor(out=rkp[:, s, 0:n], in0=t1[:, 0:n], scalar=C["rkT"][:, s:s + 1],
                                                                                in1=so[0][:, 0:n], op0=ALU.mult, op1=ALU.mult),
                              reads=[t1, so[0], C["rkT"]], writes=[rkp])
                        for q in range(8):
                            kb.dma("sp" if q % 2 == 0 else "act", S["fm"][q, s * 128:(s + 1) * 128, t0:t0 + n],
                                   so[q][:, 0:n], reads=[so[q]])
                    tl = tiles_of(n)
                    for ti, (o, cnt) in enumerate(tl):
                        for hf in range(2):
                            hs_ = slice(hf * 512, (hf + 1) * 512)
                            p = self.psum()
                            for c in range(8):
                                kb.op("pe", lambda e, c=c, p=p, o=o, cnt=cnt, hs_=hs_: e.matmul(p[:cnt, :], lhsT=xv[:, c, o:o + cnt], rhs=wr[2][:, c, hs_],
                                                                                                start=(c == 0), stop=(c == 7)), reads=[xv, wr[2]], writes=[p])
                            kb.op("act", lambda e, p=p, cnt=cnt, ti=ti, hs_=hs_: e.copy(out=vst[:cnt, ti % 2, hs_], in_=p[:cnt, :]), reads=[p], writes=[vst])
                            p = self.psum()
                            kb.op("pe", lambda e, p=p, o=o, cnt=cnt, hs_=hs_: e.matmul(p[:cnt, :], lhsT=sg1[:, o:o + cnt], rhs=g2a[:, hs_], start=True, stop=False),
                                  reads=[sg1, g2a], writes=[p])
                            kb.op("pe", lambda e, p=p, o=o, cnt=cnt, hs_=hs_: e.matmul(p[:cnt, :], lhsT=sg2[:, o:o + cnt], rhs=g2b[:, hs_], start=False, stop=True),
                                  reads=[sg2, g2b], writes=[p])
                            kb.op("dve", lambda e, p=p, cnt=cnt, ti=ti, hs_=hs_: e.tensor_copy(out=gst[:cnt, ti % 2, hs_], in_=p[:cnt, :]), reads=[p], writes=[gst])
                        p = self.psum()
                        for s in range(8):
                            kb.op("pe", lambda e, p=p, s=s, o=o, cnt=cnt: e.matmul(p[:cnt, 2 * s:2 * s + 2], lhsT=rkp[:, s, o:o + cnt], rhs=C["hsel"][:, :],
                                                                                   start=True, stop=True), reads=[rkp, C["hsel"]], writes=[p])
                        kb.op("dve", lambda e, p=p, cnt=cnt, ti=ti: e.tensor_copy(out=bst[:cnt, ti, :], in_=p[:cnt, 0:NH]), reads=[p], writes=[bst])
                        kb.dma("sp", S["v"][t0 + o:t0 + o + cnt, :], vst[:cnt, ti % 2, :], reads=[vst])
                        kb.dma("act", S["g"][t0 + o:t0 + o + cnt, :], gst[:cnt, ti % 2, :], reads=[gst])
                        kb.dma("sp", S["bonus"][t0 + o:t0 + o + cnt, :], bst[:cnt, ti, :], reads=[bst])
            kb.barrier()

    def phase_scan(self):
        kb, nc, I, C = self.kb, self.nc, self.I, self.C
        with contextlib.ExitStack() as st:
            sb = lambda name, shape, dt: self.sb(st, name, shape, dt)

            def ld(name, shape, dt, src, cast=False):
                b = sb("k_" + name, shape, dt)
                kb.dma("pool" if cast else "sp", b[:], src, writes=[b])
                C[name] = b
            ld("mask0", [128, 512], F32, I["c_mask"][0])
            ld("mask1", [128, 512], F32, I["c_mask"][1])
            ld("maskT0", [64, 1024], F32, I["c_maskT"][0])
            ld("maskT1", [64, 1024], F32, I["c_maskT"][1])
            ld("eye16", [64, 1024], BF16, I["c_eye16"], cast=True)
            ld("cm", [128, 512], F32, I["c_cm"])

            def mkstream(tag):
                B = {}
                B["BK"] = sb(tag + "BK", [128, 8, 128], BF16)
                B["AR"] = sb(tag + "AR", [128, 8, 128], BF16)
                B["BKp"] = [sb(tag + "BKp%d" % i, [128, 8, 128], BF16) for i in range(2)]
                B["ARp"] = [sb(tag + "ARp%d" % i, [128, 8, 128], BF16) for i in range(2)]
                B["vz"] = sb(tag + "vz", [128, 1024], BF16)
                for t_ in B["BKp"] + B["ARp"] + [B["vz"]]:
                    kb.op("pool", lambda e, t_=t_: e.memset(t_[:], 0.0), writes=[t_])
                B["BKT"] = sb(tag + "BKT", [128, 8, 128], BF16)
                B["M"] = sb(tag + "M", [128, 16, 128], BF16)
                B["A"] = sb(tag + "A", [64, 16, 64], BF16)
                B["AT"] = sb(tag + "AT", [64, 16, 64], BF16)
                B["X"] = sb(tag + "X", [64, 16, 64], BF16)
                B["Z"] = sb(tag + "Z", [64, 1024], BF16)
                B["vu"] = sb(tag + "vu", [128, 1024], BF16)
                B["o"] = sb(tag + "o", [64, 1024], F32)
                B["ST"] = sb(tag + "ST", [128, 8, 64], F32)
                B["STb"] = sb(tag + "STb", [128, 8, 64], BF16)
                B["PC"] = sb(tag + "PC", [128, 8], F32)
                return B

            def evac(i, fn_act, fn_dve, reads, writes):
                if i % 2 == 0:
                    kb.op("act", fn_act, reads=reads, writes=writes)
                else:
                    kb.op("dve", fn_dve, reads=reads, writes=writes)

            import os
            STOP = int(os.environ.get("SCAN_STOP", "9"))
            SUB = int(os.environ.get("SCAN_SUB", "9"))
            HACK = int(os.environ.get("HACK", "0"))

            free_banks = list(self.psb)

            def galloc(n):
                while len(free_banks) < n:
                    yield
                return [free_banks.pop(0) for _ in range(n)]

            def release(bs):
                free_banks.extend(bs)

            def stream(B, si, z):
                S = self.S[si]
                T = self.seqs[si][2]
                nch = -(-T // CH)
                mask = C["mask%d" % z]
                maskT = C["maskT%d" % z]
                kb.op("pool", lambda e: e.memset(B["ST"][:], 0.0), writes=[B["ST"]])
                kb.op("pool", lambda e: e.memset(B["STb"][:], 0.0), writes=[B["STb"]])
                order = range(nch) if z == 0 else range(nch - 1, -1, -1)
                for c in order:
                    c0 = c * CH
                    nt = min(CH, T - c0)
                    while not free_sets:
                        yield
                    Tt = free_sets.pop(0)
                    srcs = (("r", 0), ("kk", 1), ("k", 2 + z), ("b", 4 + z), ("sg", 6 + z))
                    for qi, (nm, q) in enumerate(srcs):
                        if nt < CH:
                            kb.op("pool", lambda e, nm=nm, Tt=Tt: e.memset(Tt[nm][:], 0.0), writes=[Tt[nm]])
                        kb.dma("sp" if qi % 2 == 0 else "act", Tt[nm][:, :, 0:nt],
                               S["fm"][q].rearrange("(s p) t -> p s t", p=128)[:, :, c0:c0 + nt], writes=[Tt[nm]])
                    if nt < CH:
                        kb.op("pool", lambda e: e.memset(B["vu"][64:128, :], 0.0), writes=[B["vu"]])
                    kb.dma("sp", B["vu"][64:64 + nt, :], S["v"][c0:c0 + nt, :], writes=[B["vu"]])
                    if nt < CH:
                        kb.op("pool", lambda e: e.memset(B["vz"][64:128, :], 0.0), writes=[B["vz"]])
                    kb.dma("act", B["vz"][64:64 + nt, :], S["v"][c0:c0 + nt, :], writes=[B["vz"]])
                    yield
                    cum2 = Tt["cum"][:].rearrange("p s t -> p (s t)")
                    sg2 = Tt["sg"][:].rearrange("p s t -> p (s t)")
                    kb.op("dve", lambda e, cum2=cum2, sg2=sg2: e.tensor_tensor_scan(out=cum2, data0=C["cm"][:, :], data1=sg2, initial=0.0,
                                                                op0=ALU.mult, op1=ALU.add), reads=[Tt["sg"], C["cm"]], writes=[Tt["cum"]])
                    if z == 0:
                        E1 = Tt["cum"]
                        last = CH - 1
                    else:
                        E1 = Tt["e1"]
                        last = 0
                        kb.op("pool", lambda e, Tt=Tt: e.tensor_tensor(out=Tt["e1"][:], in0=Tt["sg"][:], in1=Tt["cum"][:], op=ALU.subtract),
                              reads=[Tt["sg"], Tt["cum"]], writes=[Tt["e1"]])
                        for s_ in range(8):
                            kb.op("dve", lambda e, s_=s_, Tt=Tt: e.tensor_scalar(out=Tt["e1"][:, s_, :], in0=Tt["e1"][:, s_, :],
                                                                         scalar1=Tt["cum"][:, s_, CH - 1:CH], scalar2=None, op0=ALU.add),
                                  reads=[Tt["e1"], Tt["cum"]], writes=[Tt["e1"]])
                    kb.op("act", lambda e, Tt=Tt, E1=E1: e.activation(out=Tt["eP"][:], in_=E1[:], func=AF.Exp, scale=-LWC), reads=[E1], writes=[Tt["eP"]])
                    kb.op("act", lambda e, Tt=Tt, E1=E1: e.activation(out=Tt["eN"][:], in_=E1[:], func=AF.Exp, scale=LWC), reads=[E1], writes=[Tt["eN"]])
                    kb.op("pool", lambda e, Tt=Tt, E1=E1: e.tensor_tensor(out=Tt["eA"][:], in0=E1[:], in1=Tt["sg"][:], op=ALU.subtract),
                          reads=[E1, Tt["sg"]], writes=[Tt["eA"]])
                    kb.op("act", lambda e, Tt=Tt: e.activation(out=Tt["eA"][:], in_=Tt["eA"][:], func=AF.Exp, scale=-LWC), reads=[Tt["eA"]], writes=[Tt["eA"]])
                    kb.op("dve", lambda e, Tt=Tt: e.scalar_tensor_tensor(out=B["AR"][:, :, 0:64], in0=Tt["kk"][:], scalar=-1.0, in1=Tt["eA"][:],
                                                                  op0=ALU.mult, op1=ALU.mult), reads=[Tt["kk"], Tt["eA"]], writes=[B["AR"]])
                    kb.op("pool", lambda e, Tt=Tt: e.tensor_tensor(out=B["AR"][:, :, 64:128], in0=Tt["r"][:], in1=Tt["eP"][:], op=ALU.mult),
                          reads=[Tt["r"], Tt["eP"]], writes=[B["AR"]])
                    kb.op("dve", lambda e, Tt=Tt: e.tensor_tensor(out=B["BK"][:, :, 0:64], in0=Tt["b"][:], in1=Tt["eN"][:], op=ALU.mult),
                          reads=[Tt["b"], Tt["eN"]], writes=[B["BK"]])
                    kb.op("pool", lambda e, Tt=Tt: e.tensor_tensor(out=B["BK"][:, :, 64:128], in0=Tt["k"][:], in1=Tt["eN"][:], op=ALU.mult),
                          reads=[Tt["k"], Tt["eN"]], writes=[B["BK"]])
                    kb.op("dve", lambda e, Tt=Tt, last=last: e.tensor_copy(out=B["PC"][:, :], in_=Tt["eP"][:, :, last]), reads=[Tt["eP"]], writes=[B["PC"]])
                    for hp_ in range(2):
                        pq = slice(hp_ * 64, hp_ * 64 + 64)
                        kb.op("pool", lambda e, hp_=hp_, pq=pq: e.tensor_copy(out=B["BKp"][hp_][pq, :, :], in_=B["BK"][pq, :, :]), reads=[B["BK"]], writes=[B["BKp"][hp_]])
                        kb.op("act", lambda e, hp_=hp_, pq=pq: e.copy(out=B["ARp"][hp_][pq, :, :], in_=B["AR"][pq, :, :]), reads=[B["AR"]], writes=[B["ARp"][hp_]])
                    free_sets.append(Tt)
                    yield
                    if STOP <= 1:
                        continue
                    pall = yield from galloc(6)
                    pms = pall[0:4]
                    for h in range(16):
                        s_, hp = h // 2, h % 2
                        pm = pms[h // 4]
                        kb.op("pe", lambda e, pm=pm, h=h, s_=s_, hp=hp: e.matmul(pm[:, (h % 4) * 128:(h % 4 + 1) * 128], lhsT=B["BKp"][hp][:, s_, :],
                                                                               rhs=B["AR"][:, s_, :], start=True, stop=True),
                              reads=[B["BKp"][hp], B["AR"]], writes=[pm])
                    pts = pall[4:6]
                    for h in range(16):
                        s_, hp = h // 2, h % 2
                        pm = pts[h // 8]
                        kb.op("pe", lambda e, pm=pm, h=h, s_=s_, hp=hp: e.matmul(pm[0:64, (h % 8) * 64:(h % 8 + 1) * 64], lhsT=B["ARp"][hp][:, s_, 0:64],
                                                                               rhs=B["BK"][:, s_, 0:64], start=True, stop=True),
                              reads=[B["BK"], B["ARp"][hp]], writes=[pm])
                    yield
                    for g in range(4):
                        kb.op("dve", lambda e, g=g, pms=pms: e.tensor_tensor(out=B["M"][:, g * 4:(g + 1) * 4, :].rearrange("p h t -> p (h t)"),
                                                                         in0=pms[g][:, :], in1=mask[:, :], op=ALU.mult), reads=[pms[g], mask], writes=[B["M"]])
                    for g in range(2):
                        kb.op("dve", lambda e, g=g, pts=pts: e.tensor_tensor(out=B["AT"][:, g * 8:(g + 1) * 8, :].rearrange("p h t -> p (h t)"),
                                                                           in0=pts[g][0:64, :], in1=maskT[:, g * 512:(g + 1) * 512], op=ALU.mult),
                              reads=[pts[g], maskT], writes=[B["AT"]])
                    release(pall)
                    ptt = yield from galloc(2)
                    for s_ in range(8):
                        kb.op("pe", lambda e, s_=s_, ptt=ptt: e.matmul(ptt[s_ // 4][:, (s_ % 4) * 128:(s_ % 4 + 1) * 128], lhsT=B["BK"][:, s_, :], rhs=C["ident"][:, :],
                                                                       start=True, stop=True), reads=[B["BK"], C["ident"]], writes=[ptt[s_ // 4]])
                    yield
                    kb.op("act", lambda e: e.copy(out=B["A"][:], in_=B["M"][0:64, :, 0:64]), reads=[B["M"]], writes=[B["A"]])
                    for g in range(2):
                        kb.op("act", lambda e, g=g, ptt=ptt: e.copy(out=B["BKT"][:, g * 4:(g + 1) * 4, :].rearrange("p s t -> p (s t)"), in_=ptt[g][:, :]),
                              reads=[ptt[g]], writes=[B["BKT"]])
                    release(ptt)
                    yield
                    kb.op("pool", lambda e: e.tensor_tensor(out=B["X"][:].rearrange("p h t -> p (h t)"), in0=B["A"][:].rearrange("p h t -> p (h t)"),
                                                            in1=C["eye16"][:, :], op=ALU.add), reads=[B["A"], C["eye16"]], writes=[B["X"]])
                    for lvl in range(5):
                        lastl = (lvl == 4)
                        pab = yield from galloc(2 if lastl else 4)
                        pa = pab[0:2]
                        for h in range(16):
                            kb.op("pe", lambda e, h=h, pa=pa: e.matmul(pa[h // 8][0:64, (h % 8) * 64:(h % 8 + 1) * 64], lhsT=B["A"][:, h, :], rhs=B["AT"][:, h, :],
                                                                       start=True, stop=True), reads=[B["A"], B["AT"]], writes=[pa[h // 8]])
                        if not lastl:
                            pb = pab[2:4]
                            for h in range(16):
                                kb.op("pe", lambda e, h=h, pb=pb: e.matmul(pb[h // 8][0:64, (h % 8) * 64:(h % 8 + 1) * 64], lhsT=B["AT"][:, h, :], rhs=B["A"][:, h, :],
                                                                           start=True, stop=True), reads=[B["A"], B["AT"]], writes=[pb[h // 8]])
                        yield
                        for g in range(2):
                            kb.op("act", lambda e, g=g, pa=pa: e.copy(out=B["AT"][:, g * 8:(g + 1) * 8, :].rearrange("p h t -> p (h t)"), in_=pa[g][0:64, :]),
                                  reads=[pa[g]], writes=[B["AT"]])
                        if not lastl:
                            for g in range(2):
                                kb.op("dve", lambda e, g=g, pb=pb: e.tensor_copy(out=B["A"][:, g * 8:(g + 1) * 8, :].rearrange("p h t -> p (h t)"), in_=pb[g][0:64, :]),
                                      reads=[pb[g]], writes=[B["A"]])
                        release(pab)
                        yield
                        px = yield from galloc(2)
                        for h in range(16):
                            kb.op("pe", lambda e, h=h, px=px: e.matmul(px[h // 8][0:64, (h % 8) * 64:(h % 8 + 1) * 64], lhsT=B["AT"][:, h, :], rhs=B["X"][:, h, :],
                                                                       start=True, stop=True), reads=[B["AT"], B["X"]], writes=[px[h // 8]])
                        yield
                        for g in range(2):
                            kb.op("dve", lambda e, g=g, px=px: e.tensor_tensor(out=B["X"][:, g * 8:(g + 1) * 8, :].rearrange("p h t -> p (h t)"),
                                                                               in0=px[g][0:64, :], in1=B["X"][:, g * 8:(g + 1) * 8, :].rearrange("p h t -> p (h t)"),
                                                                               op=ALU.add), reads=[px[g], B["X"]], writes=[B["X"]])
                        release(px)
                        yield
                    pz = yield from galloc(2)
                    for h in range(16):
                        s_, hp = h // 2, h % 2
                        oz = pz[h // 8][0:64, (h % 8) * 64:(h % 8 + 1) * 64]
                        kb.op("pe", lambda e, oz=oz, s_=s_, hp=hp: e.matmul(oz, lhsT=B["ARp"][hp][:, s_, 0:64], rhs=B["STb"][:, s_, :], start=True, stop=False),
                              reads=[B["ARp"][hp], B["STb"]], writes=[pz[h // 8]])
                        kb.op("pe", lambda e, oz=oz, h=h: e.matmul(oz, lhsT=B["M"][:, h, 0:64], rhs=B["vz"][:, h * 64:(h + 1) * 64], start=False, stop=True),
                              reads=[B["M"], B["vz"]], writes=[pz[h // 8]])
                    yield
                    for g in range(2):
                        evac(g, lambda e, g=g, pz=pz: e.copy(out=B["Z"][:, g * 512:(g + 1) * 512], in_=pz[g][0:64, :]),
                             lambda e, g=g, pz=pz: e.tensor_copy(out=B["Z"][:, g * 512:(g + 1) * 512], in_=pz[g][0:64, :]), [pz[g]], [B["Z"]])
                    release(pz)
                    yield
                    pu = yield from galloc(2)
                    for h in range(16):
                        kb.op("pe", lambda e, h=h, pu=pu: e.matmul(pu[h // 8][0:64, (h % 8) * 64:(h % 8 + 1) * 64], lhsT=B["X"][:, h, :], rhs=B["Z"][:, h * 64:(h + 1) * 64],
                                                            start=True, stop=True), reads=[B["X"], B["Z"]], writes=[pu[h // 8]])
                    yield
                    for g in range(2):
                        evac(g, lambda e, g=g, pu=pu: e.copy(out=B["vu"][0:64, g * 512:(g + 1) * 512], in_=pu[g][0:64, :]),
                             lambda e, g=g, pu=pu: e.tensor_copy(out=B["vu"][0:64, g * 512:(g + 1) * 512], in_=pu[g][0:64, :]), [pu[g]], [B["vu"]])
                    release(pu)
                    yield
                    pod = yield from galloc(4)
                    po = pod[0:2]
                    for h in range(16):
                        s_, hp = h // 2, h % 2
                        oo = po[h // 8][0:64, (h % 8) * 64:(h % 8 + 1) * 64]
                        kb.op("pe", lambda e, oo=oo, s_=s_, hp=hp: e.matmul(oo, lhsT=B["ARp"][hp][:, s_, 64:128], rhs=B["STb"][:, s_, :], start=True, stop=False),
                              reads=[B["ARp"][hp], B["STb"]], writes=[po[h // 8]])
                        kb.op("pe", lambda e, oo=oo, h=h: e.matmul(oo, lhsT=B["M"][:, h, 64:128], rhs=B["vu"][:, h * 64:(h + 1) * 64], start=False, stop=True),
                              reads=[B["M"], B["vu"]], writes=[po[h // 8]])
                    pd = pod[2:4]
                    for s_ in range(8):
                        kb.op("pe", lambda e, s_=s_, pd=pd: e.matmul(pd[s_ // 4][:, (s_ % 4) * 128:(s_ % 4 + 1) * 128], lhsT=B["BKT"][:, s_, :], rhs=B["vu"][:, s_ * 128:(s_ + 1) * 128],
                                                              start=True, stop=True), reads=[B["BKT"], B["vu"]], writes=[pd[s_ // 4]])
                    yield
                    for g in range(2):
                        kb.op("act", lambda e, g=g, po=po: e.copy(out=B["o"][:, g * 512:(g + 1) * 512], in_=po[g][0:64, :]), reads=[po[g]], writes=[B["o"]])
                    kb.dma("sp", S["o"][z, c0:c0 + nt, :], B["o"][0:nt, :], reads=[B["o"]])
                    for g in range(2):
                        pv = pd[g][:, :].rearrange("p (s t) -> p s t", t=128)
                        for hp in range(2):
                            ps_ = slice(hp * 64, hp * 64 + 64)
                            kb.op("dve", lambda e, g=g, pv=pv, hp=hp, ps_=ps_: e.tensor_tensor(out=B["ST"][ps_, g * 4:(g + 1) * 4, :], in0=pv[ps_, :, hp * 64:(hp + 1) * 64],
                                                                                             in1=B["ST"][ps_, g * 4:(g + 1) * 4, :], op=ALU.add),
                                  reads=[pd[g], B["ST"]], writes=[B["ST"]])
                    release(pod)
                    yield
                    for s_ in range(8):
                        kb.op("pool" if s_ % 2 else "dve", lambda e, s_=s_: e.tensor_scalar(out=B["ST"][:, s_, :], in0=B["ST"][:, s_, :], scalar1=B["PC"][:, s_:s_ + 1], scalar2=None, op0=ALU.mult),
                              reads=[B["ST"], B["PC"]], writes=[B["ST"]])
                    yield
                    kb.op("act", lambda e: e.copy(out=B["STb"][:], in_=B["ST"][:]), reads=[B["ST"]], writes=[B["STb"]])
                    yield

            NS = 4
            Bs = [mkstream("s%d_" % i) for i in range(NS)]
            free_sets = []
            for i in range(2):
                free_sets.append({nm: sb("t%d_%s" % (i, nm), [128, 8, 64], F32)
                                  for nm in ("r", "kk", "k", "b", "sg", "cum", "e1", "eP", "eN", "eA")})
            todo = [(si, z) for si in range(len(self.seqs)) for z in range(2)]
            active = [None] * NS
            while todo or any(a is not None for a in active):
                for i in range(NS):
                    if active[i] is None and todo:
                        si, z = todo.pop(0)
                        active[i] = stream(Bs[i], si, z)
                    if active[i] is not None:
                        try:
                            next(active[i])
                        except StopIteration:
                            active[i] = None
            kb.barrier()

    def phase3(self):
        self.phase3a()
        self.ffn_pass(0)

    def phase4(self):
        self.phase4a()
        self.ffn_pass(1)

    def phase3a(self):
        kb, nc, I, C = self.kb, self.nc, self.I, self.C
        with contextlib.ExitStack() as st:
            sb = lambda name, shape, dt: self.sb(st, "p3_" + name, shape, dt)
            gnw = self.rowvec(st, "gnw", I["gn_w"][0:1, :])
            gnb = self.rowvec(st, "gnb", I["gn_b"][0:1, :])
            wo = sb("wo", [128, 8, D], BF16)
            for c in range(8):
                kb.dma("pool", wo[:, c, :], I["w_o"][c * 128:(c + 1) * 128, :], writes=[wo])
            NB = 4
            sets = []
            for i in range(NB):
                sets.append(dict(a=sb("o0_%d" % i, [128, D], F32), b=sb("o1_%d" % i, [128, D], F32), v=sb("v_%d" % i, [128, D], BF16),
                                 g=sb("g_%d" % i, [128, D], BF16), bn=sb("b_%d" % i, [128, NH], F32), x=sb("x_%d" % i, [128, D], F32),
                                 s=sb("st_%d" % i, [128, 6, NH], F32), h=sb("h1_%d" % i, [128, D], F32),
                                 ogb=sb("ogb%d" % i, [128, D], BF16), ogT=sb("ogT%d" % i, [128, 8, 128], BF16)))
                kb.op("pool", lambda e, t_=sets[i]["ogb"]: e.memset(t_[:], 0.0), writes=[sets[i]["ogb"]])
            free_banks = list(self.psb)

            def galloc(nb_):
                while len(free_banks) < nb_:
                    yield
                return [free_banks.pop(0) for _ in range(nb_)]

            def tile_gen(Q, si, t0, n):
                S = self.S[si]
                a, b_, v_, g_, bn, x_, s_, h_, ogb, ogT = Q["a"], Q["b"], Q["v"], Q["g"], Q["bn"], Q["x"], Q["s"], Q["h"], Q["ogb"], Q["ogT"]
                kb.dma("sp", a[:n, :], S["o"][0, t0:t0 + n, :], writes=[a])
                kb.dma("act", b_[:n, :], S["o"][1, t0:t0 + n, :], writes=[b_])
                kb.dma("sp", v_[:n, :], S["v"][t0:t0 + n, :], writes=[v_])
                kb.dma("act", g_[:n, :], S["g"][t0:t0 + n, :], writes=[g_])
                kb.dma("sp", bn[:n, :], S["bonus"][t0:t0 + n, :], writes=[bn])
                for (src, ro, nr) in self.xrows(si, t0, n):
                    kb.dma("act", x_[ro:ro + nr, :], src, writes=[x_])
                yield
                kb.op("pool", lambda e: e.tensor_tensor(out=a[:n, :], in0=a[:n, :], in1=b_[:n, :], op=ALU.add), reads=[b_], writes=[a])
                kb.op("pool", lambda e: e.tensor_tensor(out=b_[:n, :], in0=a[:n, :], in1=a[:n, :], op=ALU.mult), reads=[a], writes=[b_])
                yield
                a3 = a[:n, :].rearrange("p (h j) -> p h j", j=HS)
                b3 = b_[:n, :].rearrange("p (h j) -> p h j", j=HS)
                kb.op("dve", lambda e: e.reduce_sum(out=s_[:n, 0, :], in_=a3, axis=AX.X), reads=[a], writes=[s_])
                kb.op("dve", lambda e: e.reduce_sum(out=s_[:n, 1, :], in_=b3, axis=AX.X), reads=[b_], writes=[s_])
                kb.op("dve", lambda e: e.tensor_scalar(out=s_[:n, 2, :], in0=s_[:n, 0, :], scalar1=1.0 / HS, scalar2=None, op0=ALU.mult), reads=[s_], writes=[s_])
                kb.op("dve", lambda e: e.tensor_tensor(out=s_[:n, 3, :], in0=s_[:n, 2, :], in1=s_[:n, 2, :], op=ALU.mult), reads=[s_], writes=[s_])
                kb.op("dve", lambda e: e.scalar_tensor_tensor(out=s_[:n, 4, :], in0=s_[:n, 1, :], scalar=1.0 / HS, in1=s_[:n, 3, :], op0=ALU.mult, op1=ALU.subtract), reads=[s_], writes=[s_])
                kb.op("dve", lambda e: e.tensor_scalar(out=s_[:n, 4, :], in0=s_[:n, 4, :], scalar1=64e-5, scalar2=None, op0=ALU.add), reads=[s_], writes=[s_])
                yield
                kb.op("act", lambda e: e.activation(out=s_[:n, 5, :], in_=s_[:n, 4, :], func=AF.Sqrt), reads=[s_], writes=[s_])
                yield
                kb.op("dve", lambda e: e.reciprocal(out=s_[:n, 5, :], in_=s_[:n, 5, :]), reads=[s_], writes=[s_])
                for h in range(NH):
                    hs_ = slice(h * HS, (h + 1) * HS)
                    kb.op("dve" if h % 2 == 0 else "pool", lambda e, h=h, hs_=hs_: e.tensor_scalar(
                        out=a[:n, hs_], in0=a[:n, hs_], scalar1=s_[:n, 2, h:h + 1], scalar2=s_[:n, 5, h:h + 1],
                        op0=ALU.subtract, op1=ALU.mult), reads=[a, s_], writes=[a])
                yield
                kb.op("pool", lambda e: e.tensor_tensor(out=a[:n, :], in0=a[:n, :], in1=gnw[:n, :], op=ALU.mult), reads=[gnw], writes=[a])
                kb.op("pool", lambda e: e.tensor_tensor(out=a[:n, :], in0=a[:n, :], in1=gnb[:n, :], op=ALU.add), reads=[gnb], writes=[a])
                yield
                for h in range(NH):
                    hs_ = slice(h * HS, (h + 1) * HS)
                    kb.op("dve", lambda e, h=h, hs_=hs_: e.scalar_tensor_tensor(
                        out=a[:n, hs_], in0=v_[:n, hs_], scalar=bn[:n, h:h + 1], in1=a[:n, hs_], op0=ALU.mult, op1=ALU.add),
                        reads=[v_, bn], writes=[a])
                yield
                kb.op("pool", lambda e: e.tensor_tensor(out=ogb[:n, :], in0=a[:n, :], in1=g_[:n, :], op=ALU.mult), reads=[a, g_], writes=[ogb])
                yield
                (pt,) = yield from galloc(1)
                ptb = pt.t[:].bitcast(BF16)
                for c in range(8):
                    kb.op("pe", lambda e, c=c: e.transpose(out=ptb[:, c * 128:c * 128 + n], in_=ogb[:n, c * 128:(c + 1) * 128],
                                                           identity=C["ident"][:n, :n]), reads=[ogb, C["ident"]], writes=[pt])
                yield
                src3 = ptb.rearrange("p (c t) -> p c t", t=128)
                kb.op("act", lambda e: e.copy(out=ogT[:, 0:8, 0:n], in_=src3[:, 0:8, 0:n]), reads=[pt], writes=[ogT])
                free_banks.append(pt)
                yield
                pp = yield from galloc(2)
                for hf in range(2):
                    hs_ = slice(hf * 512, (hf + 1) * 512)
                    for c in range(8):
                        kb.op("pe", lambda e, hf=hf, c=c, hs_=hs_: e.matmul(pp[hf][:n, :], lhsT=ogT[:, c, 0:n], rhs=wo[:, c, hs_], start=(c == 0), stop=(c == 7)),
                              reads=[ogT, wo], writes=[pp[hf]])
                yield
                for hf in range(2):
                    hs_ = slice(hf * 512, (hf + 1) * 512)
                    kb.op("dve", lambda e, hf=hf, hs_=hs_: e.tensor_tensor(out=h_[:n, hs_], in0=pp[hf][:n, :], in1=x_[:n, hs_], op=ALU.add),
                          reads=[pp[hf], x_], writes=[h_])
                free_banks.extend(pp)
                kb.dma("sp", S["h1"][t0:t0 + n, :], h_[:n, :], reads=[h_])
                yield

            todo = [(si, t0, n) for si, (kind, bi, T) in enumerate(self.seqs) for (t0, n) in tiles_of(T)]
            active = [None] * NB
            while todo or any(x is not None for x in active):
                for i in range(NB):
                    if active[i] is None and todo:
                        si, t0, n = todo.pop(0)
                        active[i] = tile_gen(sets[i], si, t0, n)
                    if active[i] is not None:
                        try:
                            next(active[i])
                        except StopIteration:
                            active[i] = None
            kb.barrier()

    def ffn_pass(self, l):
        kb, nc, I, C = self.kb, self.nc, self.I, self.C
        NF = DFF // 128
        with contextlib.ExitStack() as st:
            sb = lambda name, shape, dt: self.sb(st, "f%d_" % l + name, shape, dt)
            nf = self.rowvec(st, "nf%d" % l, I["norm_ffn"][l:l + 1, :])
            if l == 0:
                n2 = self.rowvec(st, "nm1", I["norm_mix"][1:2, :])
                dftc = sb("dftc", [128, 256], BF16)
                kb.dma("pool", dftc[:], I["c_dftc"], writes=[dftc])
            else:
                n2 = self.rowvec(st, "nfin", I["norm_final"][0:1, :])
            win = sb("win", [128, 8, 2 * DFF], BF16)
            for c in range(8):
                kb.dma("pool", win[:, c, :], I["ffn_w_in"][l, c * 128:(c + 1) * 128, :], writes=[win])
            wout = sb("wout", [128, NF, D], BF16)
            for f in range(NF):
                kb.dma("pool", wout[:, f, :], I["ffn_w_out"][l, f * 128:(f + 1) * 128, :], writes=[wout])
            cw = sb("cw", [128, 3 * NF], F32)
            cb = sb("cb", [128, NF], F32)
            for k in range(3):
                kb.dma("sp", cw[:, k * NF:(k + 1) * NF], I["conv_w"][l, k].rearrange("(f p) -> p f", p=128), writes=[cw])
            kb.dma("sp", cb[:, :], I["conv_b"][l].rearrange("(f p) -> p f", p=128), writes=[cb])
            ht = [sb("ht%d" % i, [128, D], F32) for i in range(4)]
            ssb = sb("ssb", [128, 4], F32)
            hnb = sb("hnb", [128, D], BF16)
            junk = hnb
            hnT = sb("hnT", [128, 8, 512], BF16)
            uaL = [sb("ua%d" % i, [128, 512], F32) for i in range(2)]
            tmpL = [sb("tmp%d" % i, [128, 512], F32) for i in range(2)]
            slL = [sb("sl%d" % i, [128, 512], F32) for i in range(2)]
            zT = sb("zT", [128, NF, 512], BF16)
            ot = sb("ot", [128, 2, D], BF16) if l == 0 else sb("ot", [128, 1, D], F32)
            kb.op("pool", lambda e: e.memset(zT[:], 0.0), writes=[zT])
            kb.op("pool", lambda e: e.memset(hnb[:], 0.0), writes=[hnb])
            for t_ in ht:
                kb.op("pool", lambda e, t_=t_: e.memset(t_[:], 0.0), writes=[t_])
            for si, (kind, bi, T) in enumerate(self.seqs):
                S = self.S[si]
                Hs = S["h1"] if l == 0 else S["h3"]
                yout = self.y_p if kind == "p" else self.y_s
                for (t0, nb) in blocks_of(T):
                    ws, we = t0 - 1, t0 + nb + 1
                    W = we - ws
                    n = nb
                    wt = tiles_of(W)
                    for ti, (o, cnt) in enumerate(wt):
                        a, b = ws + o, ws + o + cnt
                        a2_, b2_ = max(a, 0), min(b, T)
                        kb.dma("sp" if ti % 2 == 0 else "act", ht[ti][a2_ - a:b2_ - a, :], Hs[a2_:b2_, :], writes=[ht[ti]])
                        self.norm_transpose(ws, ht[ti], cnt, nf, hnT, o, junk, ssb, hnb)
                    if ws < 0:
                        kb.op("pool", lambda e: e.memset(hnT[:, :, 0:1], 0.0), writes=[hnT])
                    if we > T:
                        kb.op("pool", lambda e, W=W: e.memset(hnT[:, :, W - 1:W], 0.0), writes=[hnT])
                    pls = {}
                    for i in range(NF + 3):
                        if i < NF:
                            f = i
                            fa = slice(f * 128, (f + 1) * 128)
                            fl = slice(DFF + f * 128, DFF + (f + 1) * 128)
                            pa = self.psum()
                            for c in range(8):
                                kb.op("pe", lambda e, pa=pa, c=c, fa=fa, W=W: e.matmul(pa[:, 0:W], lhsT=win[:, c, fa], rhs=hnT[:, c, 0:W], start=(c == 0), stop=(c == 7)),
                                      reads=[win, hnT], writes=[pa])
                            pl = self.psum()
                            pls[f] = pl
                            for c in range(8):
                                kb.op("pe", lambda e, pl=pl, c=c, fl=fl, W=W: e.matmul(pl[:, 0:W], lhsT=win[:, c, fl], rhs=hnT[:, c, 0:W], start=(c == 0), stop=(c == 7)),
                                      reads=[win, hnT], writes=[pl])
                        if 2 <= i <= NF + 1:
                            f = i - 2
                            tmp, sl = tmpL[f % 2], slL[f % 2]
                            kb.op("act", lambda e, f=f, n=n, tmp=tmp, sl=sl: e.activation(out=sl[:, 0:n], in_=tmp[:, 0:n], func=AF.Silu, bias=cb[:, f:f + 1], scale=1.0),
                                  reads=[tmp, cb], writes=[sl])
                        if i < NF:
                            ua = uaL[i % 2]
                            kb.op("act", lambda e, pa=pa, W=W, ua=ua: e.copy(out=ua[:, 0:W], in_=pa[:, 0:W]), reads=[pa], writes=[ua])
                        if 3 <= i <= NF + 2:
                            f = i - 3
                            sl = slL[f % 2]
                            pl_ = pls.pop(f)
                            kb.op("dve", lambda e, f=f, n=n, pl_=pl_, sl=sl: e.tensor_tensor(out=zT[:, f, 1:n + 1], in0=pl_[:, 1:n + 1], in1=sl[:, 0:n], op=ALU.mult),
                                  reads=[pl_, sl], writes=[zT])
                        if 1 <= i <= NF:
                            f = i - 1
                            ua, tmp = uaL[f % 2], tmpL[f % 2]
                            kb.op("dve", lambda e, f=f, n=n, ua=ua, tmp=tmp: e.tensor_scalar(out=tmp[:, 0:n], in0=ua[:, 0:n], scalar1=cw[:, f:f + 1], scalar2=None, op0=ALU.mult),
                                  reads=[ua, cw], writes=[tmp])
                            kb.op("dve", lambda e, f=f, n=n, ua=ua, tmp=tmp: e.scalar_tensor_tensor(out=tmp[:, 0:n], in0=ua[:, 1:n + 1], scalar=cw[:, NF + f:NF + f + 1], in1=tmp[:, 0:n],
                                                                                    op0=ALU.mult, op1=ALU.add), reads=[ua, cw], writes=[tmp])
                            kb.op("dve", lambda e, f=f, n=n, ua=ua, tmp=tmp: e.scalar_tensor_tensor(out=tmp[:, 0:n], in0=ua[:, 2:n + 2], scalar=cw[:, 2 * NF + f:2 * NF + f + 1], in1=tmp[:, 0:n],
                                                                                    op0=ALU.mult, op1=ALU.add), reads=[ua, cw], writes=[tmp])
                    for ti, (o, cnt) in enumerate(wt):
                        h_ = ht[ti]
                        for hf in range(2):
                            hs_ = slice(hf * 512, (hf + 1) * 512)
                            py = self.psum()
                            for f in range(NF):
                                kb.op("pe", lambda e, py=py, f=f, o=o, cnt=cnt, hs_=hs_: e.matmul(py[:cnt, :], lhsT=zT[:, f, o:o + cnt], rhs=wout[:, f, hs_],
                                                                                                  start=(f == 0), stop=(f == NF - 1)), reads=[zT, wout], writes=[py])
                            kb.op("dve", lambda e, py=py, cnt=cnt, hs_=hs_, h_=h_: e.tensor_tensor(out=h_[:cnt, hs_], in0=py[:cnt, :], in1=h_[:cnt, hs_], op=ALU.add),
                                  reads=[py], writes=[h_])
                        lo = max(o, 1) - o
                        hi = min(o + cnt, W - 1) - o
                        tok0 = ws + o + lo
                        if l == 0:
                            if hi > lo:
                                kb.dma("sp", S["h2"][tok0:tok0 + hi - lo, :], h_[lo:hi, :], reads=[h_])
                            self.norm_transpose(ws, h_, cnt, n2, hnT, o, junk, ssb, hnb)
                            ob = ot
                            for gp in range(4):
                                pd = self.psum()
                                for gg in range(2):
                                    g = gp * 2 + gg
                                    kb.op("pe", lambda e, pd=pd, g=g, gg=gg, o=o, cnt=cnt: e.matmul(pd[:cnt, gg * 256:(gg + 1) * 256], lhsT=hnT[:, g, o:o + cnt], rhs=dftc[:, :],
                                                                                                    start=True, stop=True), reads=[hnT, dftc], writes=[pd])
                                pd4 = pd[:cnt, :].rearrange("p (g k c) -> p g k c", g=2, k=2)
                                for k in range(2):
                                    dst = ob[:cnt, k, gp * 256:(gp + 1) * 256].rearrange("p (g c) -> p g c", g=2)
                                    if k == 0:
                                        kb.op("act", lambda e, dst=dst, pd4=pd4, k=k: e.copy(out=dst, in_=pd4[:, :, k, :]), reads=[pd], writes=[ob])
                                    else:
                                        kb.op("dve", lambda e, dst=dst, pd4=pd4, k=k: e.tensor_copy(out=dst, in_=pd4[:, :, k, :]), reads=[pd], writes=[ob])
                            if hi > lo:
                                kb.dma("sp", S["yc"][0, tok0:tok0 + hi - lo, :], ob[lo:hi, 0, :], reads=[ob])
                                kb.dma("act", S["yc"][1, tok0:tok0 + hi - lo, :], ob[lo:hi, 1, :], reads=[ob])
                        else:
                            kb.op("dve", lambda e, h_=h_, cnt=cnt: e.scalar_tensor_tensor(out=junk[:cnt, :], in0=h_[:cnt, :], scalar=1.0, in1=h_[:cnt, :],
                                                                                         op0=ALU.mult, op1=ALU.mult, accum_out=ssb[:cnt, 0:1]), reads=[h_], writes=[junk, ssb])
                            kb.op("dve", lambda e, cnt=cnt: e.tensor_scalar(out=ssb[:cnt, 1:2], in0=ssb[:cnt, 0:1], scalar1=1.0 / D, scalar2=1e-6, op0=ALU.mult, op1=ALU.add),
                                  reads=[ssb], writes=[ssb])
                            kb.op("act", lambda e, cnt=cnt: e.activation(out=ssb[:cnt, 2:3], in_=ssb[:cnt, 1:2], func=AF.Sqrt), reads=[ssb], writes=[ssb])
                            kb.op("dve", lambda e, cnt=cnt: e.reciprocal(out=ssb[:cnt, 3:4], in_=ssb[:cnt, 2:3]), reads=[ssb], writes=[ssb])
                            kb.op("dve", lambda e, h_=h_, cnt=cnt: e.scalar_tensor_tensor(out=ot[:cnt, 0, :], in0=h_[:cnt, :], scalar=ssb[:cnt, 3:4], in1=n2[:cnt, :],
                                                                                         op0=ALU.mult, op1=ALU.mult), reads=[h_, ssb, n2], writes=[ot])
                            lo2 = max(lo, NMETA - (ws + o))
                            if hi > lo2:
                                tk = ws + o + lo2 - NMETA
                                kb.dma("sp", yout[bi, tk:tk + hi - lo2, :], ot[lo2:hi, 0, :], reads=[ot])
            kb.barrier()

    def phase4a(self):
        kb, nc, I, C = self.kb, self.nc, self.I, self.C
        NTmax = -(-max(T for _, _, T in self.seqs) // 128)
        with contextlib.ExitStack() as st:
            sb = lambda name, shape, dt: self.sb(st, "p4_" + name, shape, dt)
            wf = sb("wf", [128, 8, D], BF16)
            for c in range(8):
                kb.dma("pool", wf[:, c, :], I["w_f"][c * 128:(c + 1) * 128, :], writes=[wf])
            Y = [sb("Y%d" % k, [128, NTmax, D], BF16) for k in range(2)]
            M = [[sb("M%d_%d" % (k, j), [128, NTmax, 128], BF16) for k in range(2)] for j in range(2)]
            fb = sb("fb", [128, D], BF16)
            fT = sb("fT", [128, 8, 128], BF16)
            h2t = [sb("h2t%d" % i, [128, D], F32) for i in range(2)]
            h3t = [sb("h3t%d" % i, [128, D], F32) for i in range(2)]
            kb.op("pool", lambda e: e.memset(fb[:], 0.0), writes=[fb])
            it = 0
            for si, (kind, bi, T) in enumerate(self.seqs):
                S = self.S[si]
                dft = I["c_dft_p"] if kind == "p" else I["c_dft_s"]
                NT = -(-T // 128)
                nfull = T // 128
                rem = T - nfull * 128
                for k in range(2):
                    if nfull:
                        kb.dma("sp" if k == 0 else "act", Y[k][:, 0:nfull, :], S["yc"][k, 0:nfull * 128, :].rearrange("(c p) d -> p c d", p=128), writes=[Y[k]])
                    if rem:
                        kb.dma("sp" if k == 0 else "act", Y[k][0:rem, nfull, :], S["yc"][k, nfull * 128:T, :], writes=[Y[k]])
                for (k0, kn) in tiles_of(T):
                    j = it % 2
                    it += 1
                    for k in range(2):
                        if nfull:
                            kb.dma("sp" if k == 0 else "act", M[j][k][:, 0:nfull, 0:kn], dft[k, 0:nfull * 128, k0:k0 + kn].rearrange("(c p) q -> p c q", p=128), writes=[M[j][k]])
                        if rem:
                            kb.dma("sp" if k == 0 else "act", M[j][k][0:rem, nfull, 0:kn], dft[k, nfull * 128:T, k0:k0 + kn], writes=[M[j][k]])
                    kb.dma("sp", h2t[j][:kn, :], S["h2"][k0:k0 + kn, :], writes=[h2t[j]])
                    for hf in range(2):
                        hs_ = slice(hf * 512, (hf + 1) * 512)
                        pf = self.psum()
                        for ch in range(NT):
                            cc = 128 if ch < nfull else rem
                            for k in range(2):
                                kb.op("pe", lambda e, pf=pf, ch=ch, cc=cc, k=k, j=j, kn=kn, hs_=hs_: e.matmul(pf[:kn, :], lhsT=M[j][k][0:cc, ch, 0:kn], rhs=Y[k][0:cc, ch, hs_],
                                                                                                           start=(ch == 0 and k == 0), stop=(ch == NT - 1 and k == 1)),
                                      reads=[M[j][k], Y[k]], writes=[pf])
                        if hf == 0:
                            kb.op("act", lambda e, pf=pf, kn=kn, hs_=hs_: e.copy(out=fb[:kn, hs_], in_=pf[:kn, :]), reads=[pf], writes=[fb])
                        else:
                            kb.op("dve", lambda e, pf=pf, kn=kn, hs_=hs_: e.tensor_copy(out=fb[:kn, hs_], in_=pf[:kn, :]), reads=[pf], writes=[fb])
                    self.transpose_into(fb, kn, fT, 0)
                    for hf in range(2):
                        hs_ = slice(hf * 512, (hf + 1) * 512)
                        p = self.psum()
                        for c in range(8):
                            kb.op("pe", lambda e, p=p, c=c, kn=kn, hs_=hs_: e.matmul(p[:kn, :], lhsT=fT[:, c, 0:kn], rhs=wf[:, c, hs_], start=(c == 0), stop=(c == 7)),
                                  reads=[fT, wf], writes=[p])
                        kb.op("dve", lambda e, p=p, kn=kn, hs_=hs_, j=j: e.tensor_tensor(out=h3t[j][:kn, hs_], in0=p[:kn, :], in1=h2t[j][:kn, hs_], op=ALU.add),
                              reads=[p, h2t[j]], writes=[h3t[j]])
                    kb.dma("act", S["h3"][k0:k0 + kn, :], h3t[j][:kn, :], reads=[h3t[j]])
            kb.barrier()


def host_consts(Tp, Ts, dft=False):
    c = {}
    c["c_ident"] = np.eye(128, dtype=np.float32)
    blk = np.zeros((128, 128), np.float32)
    blk[:64, :64] = 1; blk[64:, 64:] = 1
    c["c_blk"] = blk
    hs = np.zeros((128, 2), np.float32); hs[:64, 0] = 1; hs[64:, 1] = 1
    c["c_hsel"] = hs
    s_ = np.arange(64)[:, None]; t_ = np.arange(64)[None, :]
    m = np.zeros((2, 128, 128), np.float32)
    for r0 in (0, 64):
        m[0, r0:r0 + 64, 0:64] = (s_ < t_); m[0, r0:r0 + 64, 64:128] = (s_ <= t_)
        m[1, r0:r0 + 64, 0:64] = (s_ > t_); m[1, r0:r0 + 64, 64:128] = (s_ >= t_)
    c["c_mask"] = np.tile(m, (1, 1, 4))
    mt = np.zeros((2, 64, 64), np.float32)
    mt[0] = (s_ > t_); mt[1] = (s_ < t_)
    c["c_maskT"] = np.tile(mt, (1, 1, 16))
    cm = np.ones((128, 512), np.float32); cm[:, ::64] = 0
    c["c_cm"] = cm
    c["c_eye2"] = np.eye(128, dtype=np.float32)
    c["c_eye16"] = np.tile(np.eye(64, dtype=np.float32), (1, 16))
    cc = np.arange(128)
    ang = 2 * np.pi * np.outer(cc, cc) / 128
    c["c_dftc"] = np.concatenate([np.cos(ang), np.sin(ang)], 1).astype(np.float32) / np.sqrt(128)
    for nm, T in ((("c_dft_p", Tp), ("c_dft_s", Ts)) if dft else ()):
        t = np.arange(T, dtype=np.int64)
        ph = (np.outer(t, t) % T).astype(np.float64) * (2 * np.pi / T)
        c[nm] = np.stack([np.cos(ph), -np.sin(ph)]).astype(np.float32) / np.float32(np.sqrt(T))
        c[nm] = c[nm].astype(ml_dtypes.bfloat16)
    return c


PHASES = ("p1", "scan", "p3", "p4")
NCORES = 8


def kernel(x_prompt, x_sample, meta_tokens, norm_mix, norm_ffn, norm_final,
           rwkv_mu, rwkv_w_rkv, rwkv_w0, rwkv_w1, rwkv_w2, rwkv_a0, rwkv_a1, rwkv_a2,
           rwkv_g1, rwkv_g2, rwkv_k_k, rwkv_k_a, rwkv_r_k, rwkv_gn_w, rwkv_gn_b, rwkv_w_o,
           fnet_w_o, ffn_w_in, ffn_conv_w, ffn_conv_b, ffn_w_out):
    f = lambda a: np.ascontiguousarray(np.asarray(a, dtype=np.float32))
    x_prompt, x_sample = f(x_prompt), f(x_sample)
    Bp, Sp, _ = x_prompt.shape
    Bs, Ss, _ = x_sample.shape
    npc, nsc = Bp // NCORES, Bs // NCORES
    Tp, Ts = Sp + NMETA, Ss + NMETA
    prog = Prog(npc, Tp, nsc, Ts, debug=False, phases=PHASES)
    nc = prog.build()
    shared = {
        "meta": f(meta_tokens), "norm_mix": f(norm_mix), "norm_ffn": f(norm_ffn),
        "norm_final": f(norm_final).reshape(1, D), "mu": f(rwkv_mu)[0], "w_rkv": f(rwkv_w_rkv)[0],
        "w0": f(rwkv_w0)[0], "w1": f(rwkv_w1)[0], "w2": f(rwkv_w2)[0],
        "a0": f(rwkv_a0)[0], "a1": f(rwkv_a1)[0], "a2": f(rwkv_a2)[0],
        "g1": f(rwkv_g1)[0], "g2": f(rwkv_g2)[0], "k_k": f(rwkv_k_k), "k_a": f(rwkv_k_a),
        "r_k": f(rwkv_r_k).reshape(1, D), "gn_w": f(rwkv_gn_w), "gn_b": f(rwkv_gn_b), "w_o": f(rwkv_w_o)[0],
        "w_f": f(fnet_w_o)[0], "ffn_w_in": f(ffn_w_in), "conv_w": f(ffn_conv_w), "conv_b": f(ffn_conv_b),
        "ffn_w_out": f(ffn_w_out),
    }
    shared.update(host_consts(Tp, Ts, dft=("p4" in PHASES)))
    in_maps = []
    for c in range(NCORES):
        m = dict(shared)
        m["x_p"] = x_prompt[c * npc:(c + 1) * npc]
        m["x_s"] = x_sample[c * nsc:(c + 1) * nsc]
        in_maps.append(m)
    res = run_bass_kernel_spmd(nc, in_maps, core_ids=list(range(NCORES)))
    y_p = np.concatenate([np.asarray(r["y_p"], dtype=np.float32) for r in res.results], axis=0)
    y_s = np.concatenate([np.asarray(r["y_s"], dtype=np.float32) for r in res.results], axis=0)
    return (y_p, y_s)
```
